# Optimizing a Trainium2 kernel written in Bass

```python
import math
import jax, jax.numpy as jnp
from jax import lax
import numpy as np

D_MODEL = 2048
BATCH = 8
SEQ = 2048
DEPTH = 2

CTX_LEN = 256
GRID_W = 64
MLP_HIDDEN = 4 * D_MODEL
NORM_EPS = 1e-6
NEG_INF = -1e30
ROPE_BASE = 10000.0
N_EVEN = (DEPTH + 1) // 2
N_ODD = DEPTH // 2

NA_HEADS = 8
NA_HEAD_DIM = 128
NA_WIDTH = NA_HEADS * NA_HEAD_DIM
NA_WIN_ROWS = 8
NA_WIN_COLS = 16
NA_QCOLS = 16
NA_KCOLS = NA_QCOLS + NA_WIN_COLS

S5_WIDTH = D_MODEL // 2
S5_GROUP = 16
S5_GROUPS = S5_WIDTH // S5_GROUP
S5_STATE = 64
S5_DT_MIN = 1e-3
S5_DT_MAX = 1e-1

AB_IN = 3 * NA_WIDTH + S5_WIDTH
AB_OUT = NA_WIDTH + S5_WIDTH

GLA_HEADS = 4
GLA_DK = D_MODEL // 2 // GLA_HEADS
GLA_DV = D_MODEL // GLA_HEADS
GLA_QK = GLA_HEADS * GLA_DK
GLA_VW = GLA_HEADS * GLA_DV
GLA_RANK = 16
GLA_TAU = 16.0
GLA_CHUNK = 64
GLA_IN = 2 * GLA_QK + 2 * GLA_VW + 2 * GLA_RANK

kernel_name = 'hybrid_natten_s5_gla_dit'


def rmsnorm(x, g):
    xf = x.astype(jnp.float32)
    return xf * lax.rsqrt(jnp.mean(xf * xf, axis=-1, keepdims=True) + NORM_EPS) * g.astype(jnp.float32)


def modulate(h, shift, scale):
    return h * (1.0 + scale) + shift


def sq_relu_mlp(h, w1, w2):
    return jnp.square(jax.nn.relu(h @ w1)) @ w2


def rope_1d(x, pos):
    d = x.shape[-1]
    freqs = ROPE_BASE ** (-jnp.arange(0, d, 2, dtype=jnp.float32) / d)
    ang = pos[:, None] * freqs[None, :]
    cos = jnp.cos(ang)[None, :, None, :]
    sin = jnp.sin(ang)[None, :, None, :]
    x1, x2 = x[..., : d // 2], x[..., d // 2:]
    return jnp.concatenate([x1 * cos - x2 * sin, x1 * sin + x2 * cos], axis=-1)


def axial_rope(x, row_pos, col_pos):
    half = x.shape[-1] // 2
    return jnp.concatenate([rope_1d(x[..., :half], row_pos), rope_1d(x[..., half:], col_pos)], axis=-1)


def ctx_attention(q, k, v):
    s = jnp.einsum('bqhd,bkhd->bhqk', q, k) * q.shape[-1] ** -0.5
    return jnp.einsum('bhqk,bkhd->bqhd', jax.nn.softmax(s, axis=-1), v)


def neighbourhood_attention(q, k, v, kc, vc, rel_bias):
    B, S, H, dh = q.shape
    L = kc.shape[1]
    rows = S // GRID_W
    wh = min(NA_WIN_ROWS, rows)
    n_cb = GRID_W // NA_QCOLS
    scale = dh ** -0.5
    qg = q.reshape(B, rows, GRID_W, H, dh)
    kg = k.reshape(B, rows, GRID_W, H, dh)
    vg = v.reshape(B, rows, GRID_W, H, dh)

    def block(idx):
        r = idx // n_cb
        c0 = (idx % n_cb) * NA_QCOLS
        rs = jnp.clip(r - wh // 2, 0, rows - wh)
        ks0 = jnp.clip(c0 - NA_WIN_COLS // 2, 0, GRID_W - NA_KCOLS)
        qb = lax.dynamic_slice(qg, (0, r, c0, 0, 0), (B, 1, NA_QCOLS, H, dh))[:, 0]
        kb = lax.dynamic_slice(kg, (0, rs, ks0, 0, 0), (B, wh, NA_KCOLS, H, dh)).reshape(B, wh * NA_KCOLS, H, dh)
        vb = lax.dynamic_slice(vg, (0, rs, ks0, 0, 0), (B, wh, NA_KCOLS, H, dh)).reshape(B, wh * NA_KCOLS, H, dh)
        qcols = c0 + jnp.arange(NA_QCOLS)
        kcols = ks0 + jnp.arange(NA_KCOLS)
        krows = rs + jnp.arange(wh)
        win_start = jnp.clip(qcols - NA_WIN_COLS // 2, 0, GRID_W - NA_WIN_COLS)
        col_ok = (kcols[None, :] >= win_start[:, None]) & (kcols[None, :] < win_start[:, None] + NA_WIN_COLS)
        ri = krows - r + NA_WIN_ROWS - 1
        ci = jnp.clip(kcols[None, :] - qcols[:, None] + NA_WIN_COLS - 1, 0, 2 * NA_WIN_COLS - 2)
        bias = rel_bias[:, ri[:, None, None], ci[None, :, :]]
        bias = bias.transpose(0, 2, 1, 3).reshape(H, NA_QCOLS, wh * NA_KCOLS).astype(jnp.float32)
        mask = jnp.broadcast_to(col_ok[:, None, :], (NA_QCOLS, wh, NA_KCOLS)).reshape(NA_QCOLS, wh * NA_KCOLS)
        s_win = jnp.einsum('bqhd,bkhd->bhqk', qb, kb) * scale + bias
        s_win = jnp.where(mask, s_win, NEG_INF)
        s_ctx = jnp.einsum('bqhd,blhd->bhql', qb, kc) * scale
        p = jax.nn.softmax(jnp.concatenate([s_ctx, s_win], axis=-1), axis=-1)
        return (jnp.einsum('bhql,blhd->bqhd', p[..., :L], vc)
                + jnp.einsum('bhqk,bkhd->bqhd', p[..., L:], vb))

    out = lax.map(block, jnp.arange(rows * n_cb))
    out = out.reshape(rows, n_cb, B, NA_QCOLS, H, dh).transpose(2, 0, 1, 3, 4, 5)
    return out.reshape(B, S, H, dh)


def s5_discretize(lam_re, lam_im, log_dt, b_re, b_im):
    f32 = jnp.float32
    lam_re, lam_im = lam_re.astype(f32), lam_im.astype(f32)
    b_re, b_im = b_re.astype(f32), b_im.astype(f32)
    dt = jnp.exp(log_dt.astype(f32))[:, None]
    mag = jnp.exp(lam_re * dt)
    a_re = mag * jnp.cos(lam_im * dt)
    a_im = mag * jnp.sin(lam_im * dt)
    den = lam_re * lam_re + lam_im * lam_im
    f_re = ((a_re - 1.0) * lam_re + a_im * lam_im) / den
    f_im = (a_im * lam_re - (a_re - 1.0) * lam_im) / den
    bb_re = f_re[..., None] * b_re - f_im[..., None] * b_im
    bb_im = f_re[..., None] * b_im + f_im[..., None] * b_re
    return a_re, a_im, bb_re, bb_im


def _complex_linear_combine(e1, e2):
    a1r, a1i, b1r, b1i = e1
    a2r, a2i, b2r, b2i = e2
    return (a1r * a2r - a1i * a2i,
            a1r * a2i + a1i * a2r,
            a2r * b1r - a2i * b1i + b2r,
            a2r * b1i + a2i * b1r + b2i)


def s5_scan(u, lam_re, lam_im, log_dt, b_re, b_im, c_re, c_im, h0_re, h0_im, reverse):
    a_re, a_im, bb_re, bb_im = s5_discretize(lam_re, lam_im, log_dt, b_re, b_im)
    bu_re = jnp.einsum('btgc,gpc->btgp', u, bb_re)
    bu_im = jnp.einsum('btgc,gpc->btgp', u, bb_im)
    first = -1 if reverse else 0
    last = 0 if reverse else -1
    bu_re = bu_re.at[:, first].add(a_re * h0_re - a_im * h0_im)
    bu_im = bu_im.at[:, first].add(a_re * h0_im + a_im * h0_re)
    T = u.shape[1]
    ar = jnp.broadcast_to(a_re, (1, T) + a_re.shape)
    ai = jnp.broadcast_to(a_im, (1, T) + a_im.shape)
    _, _, h_re, h_im = lax.associative_scan(_complex_linear_combine, (ar, ai, bu_re, bu_im),
                                            reverse=reverse, axis=1)
    y = (jnp.einsum('btgp,gcp->btgc', h_re, c_re.astype(jnp.float32))
         - jnp.einsum('btgp,gcp->btgc', h_im, c_im.astype(jnp.float32)))
    return y, h_re[:, last], h_im[:, last]


def s5_bidirectional(ul, uc, lam_re, lam_im, log_dt, b_re, b_im, c_re, c_im, d_skip, glu_w, glu_b, ctx_out):
    B = ul.shape[0]
    grp = lambda u: u.reshape(u.shape[0], u.shape[1], S5_GROUPS, S5_GROUP).astype(jnp.float32)
    ul_g, uc_g = grp(ul), grp(uc)
    zero = jnp.zeros((B, S5_GROUPS, S5_STATE), jnp.float32)
    yl = 0.0
    yc = 0.0
    for dr, rev in enumerate((False, True)):
        prm = (lam_re[dr], lam_im[dr], log_dt[dr], b_re[dr], b_im[dr], c_re[dr], c_im[dr])
        ycd, hr, hi = s5_scan(uc_g, *prm, zero, zero, rev)
        yld, _, _ = s5_scan(ul_g, *prm, hr, hi, rev)
        yl = yl + yld
        yc = yc + ycd

    def finish(y, u):
        y = y.reshape(u.shape) + d_skip * u
        gl = jax.nn.gelu(y, approximate=False)
        return gl * jax.nn.sigmoid(gl @ glu_w + glu_b)

    return finish(yl, ul), (finish(yc, uc) if ctx_out else None)


def ab_mixer(hl, hc, w_in, w_out, rel_bias, lam_re, lam_im, log_dt, b_re, b_im, c_re, c_im,
             d_skip, glu_w, glu_b, ctx_out):
    B, S, _ = hl.shape
    L = hc.shape[1]
    cuts = [NA_WIDTH, 2 * NA_WIDTH, 3 * NA_WIDTH]
    ql, kl, vl, ul = jnp.split(hl @ w_in, cuts, axis=-1)
    qc, kc, vc, uc = jnp.split(hc @ w_in, cuts, axis=-1)
    heads = lambda z: z.reshape(z.shape[0], z.shape[1], NA_HEADS, NA_HEAD_DIM)
    kc_h, vc_h = heads(kc), heads(vc)
    na_l = neighbourhood_attention(heads(ql), heads(kl), heads(vl), kc_h, vc_h, rel_bias).reshape(B, S, NA_WIDTH)
    s5_l, s5_c = s5_bidirectional(ul, uc, lam_re, lam_im, log_dt, b_re, b_im, c_re, c_im,
                                  d_skip, glu_w, glu_b, ctx_out)
    yl = jnp.concatenate([na_l, s5_l], axis=-1) @ w_out
    if not ctx_out:
        return yl, None
    na_c = ctx_attention(heads(qc), kc_h, vc_h).reshape(B, L, NA_WIDTH)
    yc = jnp.concatenate([na_c, s5_c], axis=-1) @ w_out
    return yl, yc


def gla_chunked(q, k, v, log_a, s0):
    B, T, H, dk = q.shape
    dv = v.shape[-1]
    n = T // GLA_CHUNK
    q = q.reshape(B, n, GLA_CHUNK, H, dk)
    k = k.reshape(B, n, GLA_CHUNK, H, dk)
    v = v.reshape(B, n, GLA_CHUNK, H, dv)
    bc = jnp.cumsum(log_a.reshape(B, n, GLA_CHUNK, H, dk), axis=2)
    b_last = bc[:, :, -1:]
    q_in = q * jnp.exp(bc)
    k_in = k * jnp.exp(-bc)
    k_end = k * jnp.exp(b_last - bc)
    causal = jnp.tril(jnp.ones((GLA_CHUNK, GLA_CHUNK), dtype=bool))
    att = jnp.where(causal, jnp.einsum('bnihd,bnjhd->bnhij', q_in, k_in), 0.0)
    o_intra = jnp.einsum('bnhij,bnjhv->bnihv', att, v)
    decay = jnp.exp(b_last[:, :, 0])

    def step(state, xs):
        qi, ke, vv, dec = xs
        o = jnp.einsum('blhd,bhdv->blhv', qi, state)
        state = dec[..., None] * state + jnp.einsum('blhd,blhv->bhdv', ke, vv)
        return state, o

    xs = tuple(jnp.moveaxis(t, 1, 0) for t in (q_in, k_end, v, decay))
    s_fin, o_inter = lax.scan(step, s0, xs)
    o = o_intra + jnp.moveaxis(o_inter, 0, 1)
    return o.reshape(B, T, H, dv), s_fin


def gla_chunked_reverse(q, k, v, log_a, s0):
    o, s_fin = gla_chunked(jnp.flip(q, 1), jnp.flip(k, 1), jnp.flip(v, 1), jnp.flip(log_a, 1), s0)
    return jnp.flip(o, 1), s_fin


def gla_mixer(hl, hc, row_pos, col_pos, w_in, w_a2, b_a, norm_g, w_out, ctx_out):
    cuts = [GLA_QK, 2 * GLA_QK, 2 * GLA_QK + GLA_VW, 2 * GLA_QK + 2 * GLA_VW]

    def project(h):
        B, T, _ = h.shape
        q, k, v, g, a = jnp.split(h @ w_in, cuts, axis=-1)
        q = q.reshape(B, T, GLA_HEADS, GLA_DK).astype(jnp.float32) * GLA_DK ** -0.5
        k = k.reshape(B, T, GLA_HEADS, GLA_DK).astype(jnp.float32)
        v = v.reshape(B, T, GLA_HEADS, GLA_DV).astype(jnp.float32)
        g = g.reshape(B, T, GLA_HEADS, GLA_DV)
        log_a = [jax.nn.log_sigmoid((a[..., d * GLA_RANK:(d + 1) * GLA_RANK] @ w_a2[d] + b_a[d])
                                    .astype(jnp.float32)).reshape(B, T, GLA_HEADS, GLA_DK) / GLA_TAU
                 for d in range(2)]
        return q, k, v, g, log_a

    ql, kl, vl, gl, la_l = project(hl)
    qc, kc, vc, gc, la_c = project(hc)
    ql = axial_rope(ql, row_pos, col_pos)
    kl = axial_rope(kl, row_pos, col_pos)
    B = hl.shape[0]
    s0 = jnp.zeros((B, GLA_HEADS, GLA_DK, GLA_DV), jnp.float32)
    oc_f, sc_f = gla_chunked(qc, kc, vc, la_c[0], s0)
    oc_b, sc_b = gla_chunked_reverse(qc, kc, vc, la_c[1], s0)
    ol_f, _ = gla_chunked(ql, kl, vl, la_l[0], sc_f)
    ol_b, _ = gla_chunked_reverse(ql, kl, vl, la_l[1], sc_b)

    def finish(o, g):
        Bo, T = o.shape[0], o.shape[1]
        o = rmsnorm(o, norm_g) * jax.nn.silu(g)
        return o.reshape(Bo, T, GLA_VW) @ w_out

    return finish(ol_f + ol_b, gl), (finish(oc_f + oc_b, gc) if ctx_out else None)


def setup_inputs(seed: int = 0) -> dict:
    key = jax.random.key(seed)
    keys = iter(jax.random.split(key, 32))
    f32 = jnp.float32

    def nrm(shape, scale):
        return jax.random.normal(next(keys), shape, f32) * scale

    D = D_MODEL
    G, P, CG = S5_GROUPS, S5_STATE, S5_GROUP
    x = nrm((BATCH, SEQ, D), 1.0)
    c = nrm((BATCH, D), 1.0)
    ctx = nrm((BATCH, CTX_LEN, D), 1.0)
    c_ctx = nrm((D,), 1.0)
    ada_w = nrm((DEPTH, D, 6 * D), D ** -0.5)
    ada_b = nrm((DEPTH, 6 * D), 0.02)
    norm1_g = 1.0 + nrm((DEPTH, D), 0.05)
    norm2_g = 1.0 + nrm((DEPTH, D), 0.05)
    mlp_w1 = nrm((DEPTH, D, MLP_HIDDEN), D ** -0.5)
    mlp_w2 = nrm((DEPTH, MLP_HIDDEN, D), MLP_HIDDEN ** -0.5)
    final_g = 1.0 + nrm((D,), 0.05)
    ab_w_in = nrm((N_EVEN, D, AB_IN), D ** -0.5)
    ab_w_out = nrm((N_EVEN, AB_OUT, D), AB_OUT ** -0.5)
    na_rel_bias = nrm((N_EVEN, NA_HEADS, 2 * NA_WIN_ROWS - 1, 2 * NA_WIN_COLS - 1), 0.1)
    s5_lambda_re = -0.5 + nrm((N_EVEN, 2, G, P), 0.01)
    s5_lambda_im = jnp.pi * jnp.arange(P, dtype=f32) + nrm((N_EVEN, 2, G, P), 0.01)
    s5_log_dt = jax.random.uniform(next(keys), (N_EVEN, 2, G), f32,
                                   math.log(S5_DT_MIN), math.log(S5_DT_MAX))
    s5_b_re = nrm((N_EVEN, 2, G, P, CG), (2 * CG) ** -0.5)
    s5_b_im = nrm((N_EVEN, 2, G, P, CG), (2 * CG) ** -0.5)
    s5_c_re = nrm((N_EVEN, 2, G, CG, P), P ** -0.5)
    s5_c_im = nrm((N_EVEN, 2, G, CG, P), P ** -0.5)
    s5_d = nrm((N_EVEN, S5_WIDTH), 1.0)
    s5_glu_w = nrm((N_EVEN, S5_WIDTH, S5_WIDTH), S5_WIDTH ** -0.5)
    s5_glu_b = nrm((N_EVEN, S5_WIDTH), 0.02)
    gla_w_in = nrm((N_ODD, D, GLA_IN), D ** -0.5)
    gla_w_a2 = nrm((N_ODD, 2, GLA_RANK, GLA_QK), GLA_RANK ** -0.5)
    gla_b_a = nrm((N_ODD, 2, GLA_QK), 0.1)
    gla_norm_g = 1.0 + nrm((N_ODD, GLA_DV), 0.05)
    gla_w_out = nrm((N_ODD, GLA_VW, D), GLA_VW ** -0.5)
    return {'x': x, 'c': c, 'ctx': ctx, 'c_ctx': c_ctx,
            'ada_w': ada_w, 'ada_b': ada_b, 'norm1_g': norm1_g, 'norm2_g': norm2_g,
            'mlp_w1': mlp_w1, 'mlp_w2': mlp_w2, 'final_g': final_g,
            'ab_w_in': ab_w_in, 'ab_w_out': ab_w_out, 'na_rel_bias': na_rel_bias,
            's5_lambda_re': s5_lambda_re, 's5_lambda_im': s5_lambda_im, 's5_log_dt': s5_log_dt,
            's5_b_re': s5_b_re, 's5_b_im': s5_b_im, 's5_c_re': s5_c_re, 's5_c_im': s5_c_im,
            's5_d': s5_d, 's5_glu_w': s5_glu_w, 's5_glu_b': s5_glu_b,
            'gla_w_in': gla_w_in, 'gla_w_a2': gla_w_a2, 'gla_b_a': gla_b_a,
            'gla_norm_g': gla_norm_g, 'gla_w_out': gla_w_out}


def reference(x, c, ctx, c_ctx, ada_w, ada_b, norm1_g, norm2_g, mlp_w1, mlp_w2, final_g,
              ab_w_in, ab_w_out, na_rel_bias, s5_lambda_re, s5_lambda_im, s5_log_dt,
              s5_b_re, s5_b_im, s5_c_re, s5_c_im, s5_d, s5_glu_w, s5_glu_b,
              gla_w_in, gla_w_a2, gla_b_a, gla_norm_g, gla_w_out):
    f32 = jnp.float32
    S = x.shape[1]
    t = jnp.arange(S)
    row_pos = (t // GRID_W).astype(f32)
    col_pos = (t % GRID_W).astype(f32)
    xl = x.astype(f32)
    xc = ctx.astype(f32)
    for i in range(DEPTH):
        last = i == DEPTH - 1
        mod_l = (jax.nn.silu(c.astype(f32)) @ ada_w[i] + ada_b[i])[:, None, :]
        mod_c = (jax.nn.silu(c_ctx.astype(f32)) @ ada_w[i] + ada_b[i])[None, None, :]
        sh1l, sc1l, g1l, sh2l, sc2l, g2l = jnp.split(mod_l, 6, axis=-1)
        sh1c, sc1c, g1c, sh2c, sc2c, g2c = jnp.split(mod_c, 6, axis=-1)
        hl = modulate(rmsnorm(xl, norm1_g[i]), sh1l, sc1l)
        hc = modulate(rmsnorm(xc, norm1_g[i]), sh1c, sc1c)
        j = i // 2
        if i % 2 == 0:
            yl, yc = ab_mixer(hl, hc, ab_w_in[j], ab_w_out[j], na_rel_bias[j],
                              s5_lambda_re[j], s5_lambda_im[j], s5_log_dt[j],
                              s5_b_re[j], s5_b_im[j], s5_c_re[j], s5_c_im[j],
                              s5_d[j], s5_glu_w[j], s5_glu_b[j], not last)
        else:
            yl, yc = gla_mixer(hl, hc, row_pos, col_pos, gla_w_in[j], gla_w_a2[j], gla_b_a[j],
                               gla_norm_g[j], gla_w_out[j], not last)
        xl = xl + g1l * yl
        xl = xl + g2l * sq_relu_mlp(modulate(rmsnorm(xl, norm2_g[i]), sh2l, sc2l), mlp_w1[i], mlp_w2[i])
        if not last:
            xc = xc + g1c * yc
            xc = xc + g2c * sq_relu_mlp(modulate(rmsnorm(xc, norm2_g[i]), sh2c, sc2c), mlp_w1[i], mlp_w2[i])
    return rmsnorm(xl, final_g).astype(x.dtype)
```

```python
import numpy as np
import concourse.bass as bass
import concourse.mybir as mybir
from concourse.bass_utils import run_bass_kernel_spmd

F32 = mybir.dt.float32
BF16 = mybir.dt.bfloat16
AF = mybir.ActivationFunctionType
ALU = mybir.AluOpType

D = 2048
TC = 256
TL = 2048
T = TC + TL
NCH = D // 128
EPS = 1e-6


class Buf:
    def __init__(self, prog, name, handle, dma_written=False):
        self.name = name
        self.h = handle
        self.w = {}
        self.r = {}
        self.dsem = None
        self.dcount = 0
        self.prog = prog

    def __getitem__(self, idx):
        return self.h[idx]


class Prog:
    ENGS = ("pe", "act", "dve", "pool", "sp")

    def __init__(self, nc):
        self.nc = nc
        self.lists = {e: [] for e in self.ENGS}
        self.sem = {e: nc.alloc_semaphore(name="sem_" + e) for e in self.ENGS}
        self.cnt = {e: 0 for e in self.ENGS}
        self.waited = {e: {} for e in self.ENGS}
        self.semobj = {}
        for e in self.ENGS:
            self.semobj[id(self.sem[e])] = self.sem[e]
        self.nbuf = 0
        self.ptr = 16512
        self.top = 229344
        self.live = []
        self.pend_w = {}
        self.pend_r = {}
        self.free_dsems = []
        self.dfinal = {}

    def sb(self, name, shape, dt):
        n = 1
        for d_ in shape[1:]:
            n *= d_
        size = n * (2 if dt == BF16 else 4)
        size = (size + 63) // 64 * 64
        off = self.ptr
        self.ptr += size
        assert self.ptr <= self.top, "SBUF overflow allocating %s (%d)" % (name, size)
        self.nbuf += 1
        h = self.nc.alloc_sbuf_tensor_at("sb%d_%s" % (self.nbuf, name), list(shape), dt, offset=off)
        b = Buf(self, name, h)
        b.w = dict(self.pend_w)
        b.r = dict(self.pend_r)
        self.live.append(b)
        return b

    def mark(self):
        return (self.ptr, len(self.live))

    def release(self, mark):
        ptr, nl = mark
        for b in self.live[nl:]:
            for k_, v in list(b.w.items()) + list(b.r.items()):
                if self.pend_w.get(k_, 0) < v:
                    self.pend_w[k_] = v
                    self.pend_r[k_] = v
            if b.dsem is not None:
                self.free_dsems.append((b.dsem, b.dcount, b.dq))
                b.dsem = None
        del self.live[nl:]
        self.ptr = ptr

    def ps(self, name, shape, dt=F32):
        return Buf(self, name, self.nc.alloc_psum_tensor(name, list(shape), dt))

    def dram(self, name, shape, dt, kind="Internal"):
        t = self.nc.dram_tensor(name, list(shape), dt, kind=kind)
        b = Buf(self, name, t.ap())
        return b

    def _deps(self, reads, writes):
        d = {}
        for b in reads:
            for k, v in b.w.items():
                if d.get(k, 0) < v:
                    d[k] = v
        for b in writes:
            for k, v in b.w.items():
                if d.get(k, 0) < v:
                    d[k] = v
            for k, v in b.r.items():
                if d.get(k, 0) < v:
                    d[k] = v
        return d

    def _waits(self, eng, deps):
        out = []
        wd = self.waited[eng]
        for k, v in deps.items():
            if wd.get(k, 0) < v:
                wd[k] = v
                out.append((self.semobj[k], v))
        return out

    skip = False

    def op(self, eng, fn, reads=(), writes=(), inc=True):
        if self.skip:
            return
        deps = self._deps(reads, writes)
        sem = self.sem[eng]
        if deps.get(id(sem), 0) > self.cnt[eng]:
            del deps[id(sem)]
        waits = self._waits(eng, deps)
        if inc:
            self.cnt[eng] += 1
            val = self.cnt[eng]
            k = id(sem)
            for b in writes:
                b.w[k] = val
            for b in reads:
                b.r[k] = val
        else:
            val = self.cnt[eng] + 1
            k = id(sem)
            for b in writes:
                b.w[k] = val
            for b in reads:
                b.r[k] = val

        def run(e, waits=waits, fn=fn, inc=inc, sem=sem):
            for s, v in waits:
                e.wait_ge(s, v)
            ins = fn(e)
            if inc:
                ins.then_inc(sem, 1)

        self.lists[eng].append(run)

    def dma(self, q, out_ap, in_ap, reads=(), writes=()):
        if self.skip:
            return
        dst = writes[0]
        if dst.dsem is None:
            fl = [i for i, t in enumerate(self.free_dsems) if t[2] == q]
            if fl:
                dst.dsem, dst.dcount, _ = self.free_dsems.pop(fl[0])
                dst.dq = q
            else:
                dst.dq = q
                dst.dsem = self.nc.alloc_semaphore(name="d%d_%s" % (len(self.semobj), dst.name))
                self.semobj[id(dst.dsem)] = dst.dsem
        deps = self._deps(reads, writes)
        k = id(dst.dsem)
        if dst.dcount > 0 and deps.get(k, 0) < dst.dcount:
            deps[k] = dst.dcount
        waits = self._waits(q, deps)
        dst.dcount += 16
        val = dst.dcount
        self.dfinal[k] = val
        for b in writes:
            b.w[k] = val
        for b in reads:
            b.r[k] = val
        dsem = dst.dsem

        def run(e, waits=waits, dsem=dsem, out_ap=out_ap, in_ap=in_ap):
            for s, v in waits:
                e.wait_ge(s, v)
            e.dma_start(out=out_ap, in_=in_ap).then_inc(dsem, 16)

        self.lists[q].append(run)

    def finish(self, final_bufs):
        deps = self._deps(final_bufs, ())
        for k_, v in self.dfinal.items():
            if deps.get(k_, 0) < v:
                deps[k_] = v
        waits = self._waits("sp", deps)

        def run(e, waits=waits):
            for s, v in waits:
                e.wait_ge(s, v)

        self.lists["sp"].append(run)
        L = self.lists
        with self.nc.Block() as block:

            @block.tensor
            def _(e):
                for f in L["pe"]:
                    f(e)

            @block.scalar
            def _(e):
                for f in L["act"]:
                    f(e)

            @block.vector
            def _(e):
                for f in L["dve"]:
                    f(e)

            @block.gpsimd
            def _(e):
                for f in L["pool"]:
                    f(e)

            @block.sync
            def _(e):
                for f in L["sp"]:
                    f(e)


class K:
    def __init__(self, stage=99, debug=False):
        self.stage = stage
        nc = bass.Bass("TRN2", target_bir_lowering=False)
        self.nc = nc
        P = Prog(nc)
        self.P = P
        self.inputs = {}
        self.outs = {}
        self.psb = [P.ps("ps%d" % i, [128, 512]) for i in range(7)]
        self.psbf = P.ps("psbf", [128, 1024], BF16)
        self.ps_rr = 0

    def inp(self, name, shape, dt=F32):
        b = self.P.dram(name, shape, dt, kind="ExternalInput")
        self.inputs[name] = b
        return b

    def out(self, name, shape, dt=F32):
        b = self.P.dram(name, shape, dt, kind="ExternalOutput")
        self.outs[name] = b
        return b

    def psum(self):
        b = self.psb[self.ps_rr % 7]
        self.ps_rr += 1
        return b


def build(stage=99):
    k = K(stage)
    P = k.P
    nc = k.nc
    x_in = k.inp("x", [TL, D])
    ctx_in = k.inp("ctx", [TC, D])
    cc_in = k.inp("cc", [128, NCH, 2])
    ada_w = k.inp("ada_w", [2, D, 6 * D])
    ada_b = k.inp("ada_b", [128, 2, 96])
    n1g = k.inp("n1g", [128, 2, NCH])
    n2g = k.inp("n2g", [128, 2, NCH])
    fing = k.inp("fing", [128, NCH])
    ident_in = k.inp("ident", [128, 128])
    abw_in = k.inp("ab_w_in", [D, 4096])
    natab_in = k.inp("na_tab", [128, 37, 8, 64])
    s5lam_in = k.inp("s5_lam", [128, 3, 64])
    s5B_in = k.inp("s5_B", [128, 2, 2, 32, 128])
    s5C_in = k.inp("s5_C", [128, 2, 2, 32, 128])
    s5d_in = k.inp("s5_dg", [128, 2, 8])
    glu_w = k.inp("s5_glu_w", [1024, 1024])
    abw_out = k.inp("ab_w_out", [D, D])
    mlp_w1 = k.inp("mlp_w1", [2, D, 4 * D])
    mlp_w2 = k.inp("mlp_w2", [2, 4 * D, D])
    gla_wcat = k.inp("gla_wcat", [D, 8704])
    rope_in = k.inp("rope_tab", [128, 2, 2, TL])
    wa2_in = k.inp("gla_wa2", [64, 1024])
    tri_in = k.inp("tri", [128, 2, 128])
    gng_in = k.inp("gla_ng", [128, 512])
    glaw_out = k.inp("gla_w_out", [D, D])

    ident = P.sb("ident", [128, 128], F32)
    P.dma("sp", ident[:], ident_in[:], reads=[ident_in], writes=[ident])
    ones_bf = P.sb("ones_bf", [128, 128], BF16)
    P.op("dve", lambda e: e.memset(ones_bf[:], 1.0), writes=[ones_bf])
    identb = P.sb("identb", [128, 128], BF16)
    P.op("dve", lambda e: e.tensor_copy(out=identb[:], in_=ident[:]), reads=[ident], writes=[identb])

    P.skip = stage >= 100
    mod = P.sb("mod", [128, 2, 96, 2], F32)
    cc = P.sb("cc", [128, NCH, 2], F32)
    P.dma("sp", cc[:], cc_in[:], reads=[cc_in], writes=[cc])
    scb = P.sb("scb", [128, NCH, 2], BF16)
    P.op("act", lambda e: e.activation(out=scb[:], in_=cc[:], func=AF.Silu), reads=[cc], writes=[scb])
    adab = P.sb("adab", [128, 2, 96], F32)
    P.dma("sp", adab[:], ada_b[:], reads=[ada_b], writes=[adab])
    NSL = 1024
    mk0 = P.mark()
    wsl = [P.sb("adaw%d" % i, [128, NCH, NSL], BF16) for i in range(2)]
    si = 0
    for layer in range(2):
        for s in range(6 * D // NSL):
            wb = wsl[si % 2]
            si += 1
            P.dma("pool", wb[:], ada_w[layer, :, s * NSL:(s + 1) * NSL].rearrange("(kc p) n -> p kc n", p=128),
                  reads=[ada_w], writes=[wb])
            pt = k.psum()
            nsub = NSL // 128
            for j in range(nsub):
                for kc in range(NCH):
                    last = (kc == NCH - 1) and (j == nsub - 1)
                    P.op("pe", lambda e, j=j, kc=kc, wb=wb, pt=pt: e.matmul(
                        pt[:, 2 * j:2 * j + 2], lhsT=wb[:, kc, j * 128:(j + 1) * 128], rhs=scb[:, kc, :],
                        start=(kc == 0), stop=(kc == NCH - 1)),
                        reads=[wb, scb], writes=[pt], inc=last)
            c0 = s * nsub
            P.op("dve", lambda e, pt=pt, layer=layer, c0=c0, nsub=nsub: e.tensor_tensor(
                out=mod[:, layer, c0:c0 + nsub, :],
                in0=pt[:, 0:2 * nsub].rearrange("p (c w) -> p c w", w=2),
                in1=adab[:, layer, c0:c0 + nsub].unsqueeze(2).to_broadcast([128, nsub, 2]),
                op=ALU.add), reads=[pt, adab], writes=[mod])

    P.release(mk0)
    if stage == 0:
        o = k.out("o_mod", [128, 2 * 96 * 2])
        P.dma("sp", o[:], mod[:].rearrange("p a b c -> p (a b c)"), reads=[mod], writes=[o])
        P.finish([o])
        return k

    TT = [(0, 256, 1)] + [(256 + 256 * i, 256, 0) for i in range(8)]

    n1 = P.sb("n1", [128, 2, NCH], F32)
    n2 = P.sb("n2", [128, 2, NCH], F32)
    fg = P.sb("fg", [128, NCH], F32)
    P.dma("sp", n1[:], n1g[:], reads=[n1g], writes=[n1])
    P.dma("sp", n2[:], n2g[:], reads=[n2g], writes=[n2])
    P.dma("sp", fg[:], fing[:], reads=[fing], writes=[fg])
    acoef = P.sb("acoef", [128, 2, 2, 2, NCH], F32)
    for layer in range(2):
        for nm in range(2):
            g = n1 if nm == 0 else n2
            for wh in range(2):
                scl = mod[:, layer, (3 * nm + 1) * NCH:(3 * nm + 2) * NCH, wh]
                P.op("dve", lambda e, layer=layer, nm=nm, wh=wh, g=g, scl=scl: e.scalar_tensor_tensor(
                    out=acoef[:, layer, nm, wh, :], in0=scl, scalar=1.0, in1=g[:, layer, :],
                    op0=ALU.add, op1=ALU.mult), reads=[mod, g], writes=[acoef])

    def shift_ap(layer, nm, wh, c):
        return mod[:, layer, 3 * nm * NCH + c, wh:wh + 1]

    def gate_ap(layer, nm, wh, c):
        return mod[:, layer, (3 * nm + 2) * NCH + c, wh:wh + 1]

    XT = P.dram("XT", [128, NCH, T], F32)
    QT = P.dram("QT", [128, 8, T], BF16)
    KT = P.dram("KT", [128, 8, T], BF16)
    UT = P.dram("UT", [128, 8, T], BF16)
    VTOK = P.dram("VTOK", [T, 1024], BF16)
    CAT = P.dram("CAT", [128, NCH, T], BF16)
    epsb = P.sb("epsb", [128, 1], F32)
    P.op("dve", lambda e: e.memset(epsb[:], EPS), writes=[epsb])

    class NS:
        pass

    def alloc_norm_scratch():
        NS.xt_tiles = [P.sb("xt%d" % i, [128, NCH, 256], F32) for i in range(2)]
        NS.sq = P.sb("sq", [128, NCH, 256], BF16)
        NS.rstd = P.sb("rstd", [128, 256], F32)
        NS.tmpn = [P.sb("tmpn%d" % i, [128, 256], F32) for i in range(2)]

    def rstd_tile(xt, n):
        sqb, rs = NS.sq, NS.rstd
        P.op("act", lambda e: e.activation(out=sqb[:, :, :n], in_=xt[:, :, :n], func=AF.Square),
             reads=[xt], writes=[sqb])
        pt = k.psum()
        for c in range(NCH):
            P.op("pe", lambda e, c=c: e.matmul(pt[:, :n], lhsT=ones_bf[:], rhs=sqb[:, c, :n],
                                               start=(c == 0), stop=(c == NCH - 1)),
                 reads=[ones_bf, sqb], writes=[pt], inc=(c == NCH - 1))
        P.op("act", lambda e: e.activation(out=rs[:, :n], in_=pt[:, :n], func=AF.Sqrt, bias=epsb[:, 0:1],
                                           scale=1.0 / D), reads=[pt, epsb], writes=[rs])
        P.op("dve", lambda e: e.reciprocal(out=rs[:, :n], in_=rs[:, :n]), reads=[rs], writes=[rs])
        return rs

    def norm_tile(xt, n, layer, nm, wh, dst, dst_t0):
        rs = rstd_tile(xt, n)
        tmpn = NS.tmpn
        for c in range(NCH):
            tb = tmpn[c % 2]
            P.op("dve", lambda e, c=c, tb=tb: e.scalar_tensor_tensor(
                out=tb[:, :n], in0=xt[:, c, :n], scalar=acoef[:, layer, nm, wh, c:c + 1], in1=rs[:, :n],
                op0=ALU.mult, op1=ALU.mult), reads=[xt, acoef, rs], writes=[tb])
            P.op("act", lambda e, c=c, tb=tb: e.activation(
                out=dst[:, c, dst_t0:dst_t0 + n], in_=tb[:, :n], func=AF.Identity,
                bias=shift_ap(layer, nm, wh, c), scale=1.0), reads=[tb, mod], writes=[dst])

    evac_rr = [0]

    def evac(out_ap, in_ap, reads, writes):
        evac_rr[0] += 1
        if evac_rr[0] % 2 == 0:
            P.op("dve", lambda e: e.tensor_copy(out=out_ap, in_=in_ap), reads=reads, writes=writes)
        else:
            P.op("act", lambda e: e.copy(out=out_ap, in_=in_ap), reads=reads, writes=writes)

    TOKT = [(0, 256), (256, 512), (768, 512), (1280, 512), (1792, 512)]

    def linear(src, W, ncols, epi_fm=None, epi_tm=None, col_mode=None, toks=TOKT, nkc=NCH, wrows=None):
        mk = P.mark()
        SL = 512
        wsl = [P.sb("wsl%d" % i, [128, nkc, SL], BF16) for i in range(2)]
        for sidx in range(ncols // SL):
            wb = wsl[sidx % 2]
            P.dma("pool", wb[:], W[:, sidx * SL:(sidx + 1) * SL].rearrange("(kc p) n -> p kc n", p=128),
                  reads=[], writes=[wb])
            mode = col_mode(sidx) if col_mode else "fm"
            if mode == "fm":
                for j in range(SL // 128):
                    for (t0, n) in toks:
                        pt = k.psum()
                        for kc in range(nkc):
                            P.op("pe", lambda e, kc=kc, j=j, t0=t0, n=n, pt=pt, wb=wb: e.matmul(
                                pt[:, :n], lhsT=wb[:, kc, j * 128:(j + 1) * 128], rhs=src[:, kc, t0:t0 + n],
                                start=(kc == 0), stop=(kc == nkc - 1)),
                                reads=[wb, src], writes=[pt], inc=(kc == nkc - 1))
                        epi_fm(pt, sidx * SL + j * 128, t0, n)
            else:
                for (t0, n) in toks:
                    for sub in range(n // 128):
                        tok0 = t0 + sub * 128
                        pt = k.psum()
                        for kc in range(nkc):
                            P.op("pe", lambda e, kc=kc, tok0=tok0, pt=pt, wb=wb: e.matmul(
                                pt[:, :], lhsT=src[:, kc, tok0:tok0 + 128], rhs=wb[:, kc, :],
                                start=(kc == 0), stop=(kc == nkc - 1)),
                                reads=[wb, src], writes=[pt], inc=(kc == nkc - 1))
                        epi_tm(pt, sidx * SL, tok0)
        P.release(mk)

    mk_h = P.mark()
    hT = P.sb("hT", [128, NCH, T], BF16)
    mk1 = P.mark()
    alloc_norm_scratch()
    xt_tiles = NS.xt_tiles
    xtok = [P.sb("xtok%d" % i, [128, D], F32) for i in range(2)]
    li = 0
    for ti, (t0, n, wh) in enumerate(TT):
        xt = xt_tiles[ti % 2]
        for sub in range(n // 128):
            xk = xtok[li % 2]
            li += 1
            tok0 = t0 + sub * 128
            src = ctx_in[tok0:tok0 + 128, :] if wh == 1 else x_in[tok0 - TC:tok0 - TC + 128, :]
            P.dma("sp", xk[:], src, reads=[ctx_in if wh == 1 else x_in], writes=[xk])
            for q in range(4):
                pt = k.psum()
                for j in range(4):
                    c = q * 4 + j
                    P.op("pe", lambda e, c=c, j=j, pt=pt, xk=xk: e.transpose(
                        out=pt[:, j * 128:(j + 1) * 128], in_=xk[:, c * 128:(c + 1) * 128], identity=ident[:]),
                        reads=[xk, ident], writes=[pt], inc=(j == 3))
                evac(xt[:, q * 4:q * 4 + 4, sub * 128:(sub + 1) * 128],
                     pt[:, :].rearrange("p (j t) -> p j t", j=4), [pt], [xt])
        P.dma("sp", XT[:, :, t0:t0 + n], xt[:, :, :n], reads=[xt], writes=[XT])
        norm_tile(xt, n, 0, 0, wh, hT, t0)
    P.release(mk1)
    if stage == 1:
        o = k.out("o_h", [128, NCH * T], BF16)
        P.dma("sp", o[:], hT[:].rearrange("p a b -> p (a b)"), reads=[hT], writes=[o])
        o2 = k.out("o_xt", [128, NCH * T], F32)
        P.dma("sp", o2[:], XT[:].rearrange("p a b -> p (a b)"), reads=[XT], writes=[o2])
        P.finish([o, o2])
        return k

    mk2 = P.mark()
    stg_fm = [P.sb("stgfm%d" % i, [128, T], BF16) for i in range(2)]
    stg_tm = [P.sb("stgtm%d" % i, [128, 512], BF16) for i in range(3)]
    cnt = {"fm": 0, "tm": 0}

    def epi_fm0(pt, col0, t0, n):
        sg = stg_fm[(cnt["fm"] // len(TOKT)) % 2]
        cnt["fm"] += 1
        evac(sg[:, t0:t0 + n], pt[:, :n], [pt], [sg])
        if t0 + n == T:
            dstT, ch = (QT, col0 // 128) if col0 < 1024 else ((KT, (col0 - 1024) // 128) if col0 < 2048 else (UT, (col0 - 3072) // 128))
            P.dma("sp", dstT[:, ch, :], sg[:], reads=[sg], writes=[dstT])

    def epi_tm0(pt, col0, tok0):
        sg = stg_tm[cnt["tm"] % 3]
        cnt["tm"] += 1
        evac(sg[:], pt[:, :], [pt], [sg])
        P.dma("sp", VTOK[tok0:tok0 + 128, col0 - 2048:col0 - 2048 + 512], sg[:], reads=[sg], writes=[VTOK])

    linear(hT, abw_in, 4096, epi_fm=epi_fm0, epi_tm=epi_tm0,
           col_mode=lambda sidx: "tm" if 4 <= sidx < 6 else "fm")
    P.release(mk2)
    P.release(mk_h)
    if stage == 2:
        outs_ = []
        for nm_, tb_, shp in (("o_q", QT, [128, 8 * T]), ("o_k", KT, [128, 8 * T]), ("o_u", UT, [128, 8 * T])):
            o = k.out(nm_, shp, BF16)
            P.dma("sp", o[:], tb_[:].rearrange("p a b -> p (a b)"), reads=[tb_], writes=[o])
            outs_.append(o)
        o = k.out("o_v", [T, 1024], BF16)
        P.dma("sp", o[:], VTOK[:], reads=[VTOK], writes=[o])
        outs_.append(o)
        P.finish(outs_)
        return k

    mk3 = P.mark()
    qT = P.sb("qT", [128, 8, T], BF16)
    kT = P.sb("kT", [128, 8, T], BF16)
    vtok = P.sb("vtok", [128, 18, 1024], BF16)
    ET = P.sb("ET", [128, 37, 8, 64], BF16)
    P.dma("sp", qT[:], QT[:], reads=[QT], writes=[qT])
    P.dma("sp", kT[:], KT[:], reads=[KT], writes=[kT])
    P.dma("sp", vtok[:], VTOK[:].rearrange("(tt p) n -> p tt n", p=128), reads=[VTOK], writes=[vtok])
    mk3b = P.mark()
    tabf = [P.sb("tabf%d" % i, [128, 8, 8, 64], F32) for i in range(2)]
    for i, t0_ in enumerate(range(0, 37, 8)):
        nt_ = min(8, 37 - t0_)
        tb = tabf[i % 2]
        P.dma("sp", tb[:, :nt_], natab_in[:, t0_:t0_ + nt_], reads=[natab_in], writes=[tb])
        P.op("act", lambda e, tb=tb, t0_=t0_, nt_=nt_: e.activation(out=ET[:, t0_:t0_ + nt_], in_=tb[:, :nt_], func=AF.Exp),
             reads=[tb], writes=[ET])
    P.release(mk3b)
    PTb = [P.sb("PT%d" % i, [128, 7, 64], BF16) for i in range(16)]
    recb = [P.sb("rec%d" % i, [128, 512], F32) for i in range(2)]
    narow = [P.sb("narow%d" % i, [128, 8, 64], BF16) for i in range(2)]
    SCALE = 128.0 ** -0.5
    rows = [("c", i) for i in range(4)] + [("l", r) for r in range(32)]
    pti = 0
    for ri_, (kind, r) in enumerate(rows):
        if kind == "c":
            q0 = 64 * r
            wtiles = []
            tab0 = None
        else:
            q0 = TC + 64 * r
            rs = min(max(r - 4, 0), 24)
            o = r - rs
            if rs % 2 == 0:
                wtiles = [(TC + 64 * rs + 128 * i) for i in range(4)]
                tab0 = 4 * o
            else:
                wtiles = [(TC + 64 * (rs - 1) + 128 * i) for i in range(5)]
                tab0 = 32
        ktiles = wtiles + [0, 128]
        nw = len(wtiles)
        nk = len(ktiles)
        bankA = k.psum()
        bankB = k.psum()
        pts = []
        for h in range(8):
            pt = k.psum()
            for i, kt0 in enumerate(ktiles):
                P.op("pe", lambda e, pt=pt, i=i, kt0=kt0, h=h, q0=q0: e.matmul(
                    pt[:, i * 64:(i + 1) * 64], lhsT=kT[:, h, kt0:kt0 + 128], rhs=qT[:, h, q0:q0 + 64],
                    start=True, stop=True), reads=[kT, qT], writes=[pt], inc=(i == nk - 1))
            PT = PTb[pti % 16]
            pti += 1
            P.op("act", lambda e, pt=pt, PT=PT, nk=nk: e.activation(
                out=PT[:, :nk, :], in_=pt[:, :nk * 64].rearrange("p (a b) -> p a b", b=64), func=AF.Exp, scale=SCALE),
                reads=[pt], writes=[PT])
            if nw:
                P.op("dve", lambda e, PT=PT, nw=nw, tab0=tab0, h=h: e.tensor_tensor(
                    out=PT[:, :nw, :], in0=PT[:, :nw, :], in1=ET[:, tab0:tab0 + nw, h, :], op=ALU.mult),
                    reads=[PT, ET], writes=[PT])
            pts.append(PT)
        for h in range(8):
            PT = pts[h]
            for i, kt0 in enumerate(ktiles):
                P.op("pe", lambda e, PT=PT, i=i, kt0=kt0, h=h, bankA=bankA, nk=nk: e.matmul(
                    bankA[:, h * 64:(h + 1) * 64], lhsT=vtok[:, kt0 // 128, h * 128:(h + 1) * 128], rhs=PT[:, i, :],
                    start=(i == 0), stop=(i == nk - 1)), reads=[vtok, PT], writes=[bankA], inc=(i == nk - 1 and h == 7))
        for h in range(8):
            PT = pts[h]
            for i, kt0 in enumerate(ktiles):
                P.op("pe", lambda e, PT=PT, i=i, h=h, bankB=bankB, nk=nk: e.matmul(
                    bankB[:, h * 64:(h + 1) * 64], lhsT=ones_bf[:], rhs=PT[:, i, :],
                    start=(i == 0), stop=(i == nk - 1)), reads=[ones_bf, PT], writes=[bankB], inc=(i == nk - 1 and h == 7))
        rec = recb[ri_ % 2]
        P.op("dve", lambda e, rec=rec, bankB=bankB: e.reciprocal(out=rec[:], in_=bankB[:, :]), reads=[bankB], writes=[rec])
        nr = narow[ri_ % 2]
        P.op("dve", lambda e, rec=rec, bankA=bankA, nr=nr: e.tensor_tensor(
            out=nr[:], in0=bankA[:, :].rearrange("p (h q) -> p h q", q=64),
            in1=rec[:].rearrange("p (h q) -> p h q", q=64), op=ALU.mult), reads=[bankA, rec], writes=[nr])
        P.dma("sp", CAT[:, 0:8, q0:q0 + 64], nr[:], reads=[nr], writes=[CAT])
    P.release(mk3)
    if stage == 3:
        o = k.out("o_na", [128, 8 * T], BF16)
        P.dma("sp", o[:], CAT[:, 0:8, :], reads=[CAT], writes=[o])
        P.finish([o])
        return k

    YF = P.dram("YF", [128, 8, T], F32)
    YB = P.dram("YB", [128, 8, T], F32)
    mk4 = P.mark()
    Bm = P.sb("Bm", [128, 2, 2, 32, 128], BF16)
    P.dma("pool", Bm[:], s5B_in[:], reads=[s5B_in], writes=[Bm])
    Cb = P.sb("Cb", [128, 2, 2, 32, 128], BF16)
    A1 = P.sb("A1", [128, 2, 2, 32], F32)
    A2 = P.sb("A2", [128, 2, 2, 32], F32)
    mk4a = P.mark()
    lam = P.sb("lam", [128, 3, 64], F32)
    P.dma("sp", lam[:], s5lam_in[:], reads=[s5lam_in], writes=[lam])
    sc_ = {}

    def T64(nm):
        sc_[nm] = P.sb("s5_" + nm, [128, 64], F32)
        return sc_[nm]

    dt_ = T64("dt"); mag = T64("mag"); th = T64("th"); t1 = T64("t1"); t2 = T64("t2")
    sn = T64("sn"); cs = T64("cs"); are = T64("are"); aim = T64("aim"); den = T64("den")
    fre = T64("fre"); fim = T64("fim"); nfim = T64("nfim"); am1 = T64("am1"); nfre = T64("nfre")
    TWO_PI = float(2 * np.pi)
    MAGIC = 12582912.0

    def dv(fn, reads, writes):
        P.op("dve", fn, reads=reads, writes=writes)

    P.op("act", lambda e: e.activation(out=dt_[:], in_=lam[:, 2, :], func=AF.Exp), reads=[lam], writes=[dt_])
    dv(lambda e: e.tensor_tensor(out=t1[:], in0=lam[:, 0, :], in1=dt_[:], op=ALU.mult), [lam, dt_], [t1])
    P.op("act", lambda e: e.activation(out=mag[:], in_=t1[:], func=AF.Exp), reads=[t1], writes=[mag])
    dv(lambda e: e.tensor_tensor(out=th[:], in0=lam[:, 1, :], in1=dt_[:], op=ALU.mult), [lam, dt_], [th])

    def sin_of(dst, shift):
        dv(lambda e: e.tensor_scalar(out=t1[:], in0=th[:], scalar1=shift, scalar2=1.0 / TWO_PI, op0=ALU.add, op1=ALU.mult), [th], [t1])
        dv(lambda e: e.tensor_scalar(out=t2[:], in0=t1[:], scalar1=MAGIC, scalar2=None, op0=ALU.add), [t1], [t2])
        dv(lambda e: e.tensor_scalar(out=t2[:], in0=t2[:], scalar1=-MAGIC, scalar2=None, op0=ALU.add), [t2], [t2])
        dv(lambda e: e.tensor_tensor(out=t1[:], in0=t1[:], in1=t2[:], op=ALU.subtract), [t1, t2], [t1])
        dv(lambda e: e.tensor_scalar(out=t1[:], in0=t1[:], scalar1=TWO_PI, scalar2=3.1415925, op0=ALU.mult, op1=ALU.min), [t1], [t1])
        dv(lambda e: e.tensor_scalar(out=t1[:], in0=t1[:], scalar1=-3.1415925, scalar2=None, op0=ALU.max), [t1], [t1])
        P.op("act", lambda e: e.activation(out=dst[:], in_=t1[:], func=AF.Sin), reads=[t1], writes=[dst])

    sin_of(sn, 0.0)
    sin_of(cs, float(np.pi / 2))
    dv(lambda e: e.tensor_tensor(out=are[:], in0=mag[:], in1=cs[:], op=ALU.mult), [mag, cs], [are])
    dv(lambda e: e.tensor_tensor(out=aim[:], in0=mag[:], in1=sn[:], op=ALU.mult), [mag, sn], [aim])
    dv(lambda e: e.tensor_tensor(out=den[:], in0=lam[:, 0, :], in1=lam[:, 0, :], op=ALU.mult), [lam], [den])
    dv(lambda e: e.tensor_tensor(out=t1[:], in0=lam[:, 1, :], in1=lam[:, 1, :], op=ALU.mult), [lam], [t1])
    dv(lambda e: e.tensor_tensor(out=den[:], in0=den[:], in1=t1[:], op=ALU.add), [den, t1], [den])
    dv(lambda e: e.reciprocal(out=den[:], in_=den[:]), [den], [den])
    dv(lambda e: e.tensor_scalar(out=am1[:], in0=are[:], scalar1=-1.0, scalar2=None, op0=ALU.add), [are], [am1])
    dv(lambda e: e.tensor_tensor(out=t1[:], in0=am1[:], in1=lam[:, 0, :], op=ALU.mult), [am1, lam], [t1])
    dv(lambda e: e.tensor_tensor(out=t2[:], in0=aim[:], in1=lam[:, 1, :], op=ALU.mult), [aim, lam], [t2])
    dv(lambda e: e.tensor_tensor(out=t1[:], in0=t1[:], in1=t2[:], op=ALU.add), [t1, t2], [t1])
    dv(lambda e: e.tensor_tensor(out=fre[:], in0=t1[:], in1=den[:], op=ALU.mult), [t1, den], [fre])
    dv(lambda e: e.tensor_tensor(out=t1[:], in0=aim[:], in1=lam[:, 0, :], op=ALU.mult), [aim, lam], [t1])
    dv(lambda e: e.tensor_tensor(out=t2[:], in0=am1[:], in1=lam[:, 1, :], op=ALU.mult), [am1, lam], [t2])
    dv(lambda e: e.tensor_tensor(out=t1[:], in0=t1[:], in1=t2[:], op=ALU.subtract), [t1, t2], [t1])
    dv(lambda e: e.tensor_tensor(out=fim[:], in0=t1[:], in1=den[:], op=ALU.mult), [t1, den], [fim])
    dv(lambda e: e.tensor_scalar(out=nfim[:], in0=fim[:], scalar1=-1.0, scalar2=None, op0=ALU.mult), [fim], [nfim])
    dv(lambda e: e.tensor_scalar(out=nfre[:], in0=fre[:], scalar1=-1.0, scalar2=None, op0=ALU.mult), [fre], [nfre])
    for d_ in range(2):
        sl = slice(d_ * 32, d_ * 32 + 32)
        dv(lambda e, d_=d_, sl=sl: e.tensor_copy(out=A1[:, d_, 0, :], in_=are[:, sl]), [are], [A1])
        dv(lambda e, d_=d_, sl=sl: e.tensor_copy(out=A1[:, d_, 1, :], in_=are[:, sl]), [are], [A1])
        dv(lambda e, d_=d_, sl=sl: e.tensor_scalar(out=A2[:, d_, 0, :], in0=aim[:, sl], scalar1=-1.0, scalar2=None, op0=ALU.mult), [aim], [A2])
        dv(lambda e, d_=d_, sl=sl: e.tensor_copy(out=A2[:, d_, 1, :], in_=aim[:, sl]), [aim], [A2])
    Cf = P.sb("Cf", [128, 2, 32, 128], F32)
    ctmp = P.sb("ctmp", [128, 128], F32)
    for d_ in range(2):
        P.dma("sp", Cf[:], s5C_in[:, d_], reads=[s5C_in], writes=[Cf])
        for gp in range(32):
            col = d_ * 32 + gp
            dv(lambda e, gp=gp, col=col: e.tensor_scalar(out=ctmp[:], in0=Cf[:, 0, gp, :], scalar1=fre[:, col:col + 1], scalar2=None, op0=ALU.mult), [Cf, fre], [ctmp])
            dv(lambda e, gp=gp, col=col, d_=d_: e.scalar_tensor_tensor(out=Cb[:, d_, 0, gp, :], in0=Cf[:, 1, gp, :], scalar=nfim[:, col:col + 1], in1=ctmp[:], op0=ALU.mult, op1=ALU.add), [Cf, nfim, ctmp], [Cb])
            dv(lambda e, gp=gp, col=col: e.tensor_scalar(out=ctmp[:], in0=Cf[:, 0, gp, :], scalar1=nfim[:, col:col + 1], scalar2=None, op0=ALU.mult), [Cf, nfim], [ctmp])
            dv(lambda e, gp=gp, col=col, d_=d_: e.scalar_tensor_tensor(out=Cb[:, d_, 1, gp, :], in0=Cf[:, 1, gp, :], scalar=nfre[:, col:col + 1], in1=ctmp[:], op0=ALU.mult, op1=ALU.add), [Cf, nfre, ctmp], [Cb])
    P.release(mk4a)
    W = 32
    NW = T // W
    H = [[P.sb("H%d%d" % (d_, i), [128, 3, 32, W], F32) for i in range(2)] for d_ in range(2)]
    BU = [[P.sb("BU%d%d" % (d_, i), [128, 2, 32, W], F32) for i in range(2)] for d_ in range(2)]
    Hb = [[P.sb("Hb%d%d" % (d_, i), [128, 2, 32, W], BF16) for i in range(2)] for d_ in range(2)]
    uw = [[P.sb("uw%d%d" % (d_, i), [128, 8, W], BF16) for i in range(2)] for d_ in range(2)]
    ys = [[P.sb("ys%d%d" % (d_, i), [128, 8, W], F32) for i in range(2)] for d_ in range(2)]
    tm1 = [P.sb("tm1_%d" % d_, [128, 2, 32], F32) for d_ in range(2)]
    tm2 = [P.sb("tm2_%d" % d_, [128, 2, 32], F32) for d_ in range(2)]
    ENG = ["dve", "pool"]

    def win_tok0(d_, w):
        if d_ == 0:
            return w * W
        pos = T - (w + 1) * W
        return TC + pos if pos < TL else pos - TL

    for w in range(NW):
        par = w % 2
        for d_ in range(2):
            tok0 = win_tok0(d_, w)
            uwb = uw[d_][par]
            P.dma("sp", uwb[:], UT[:, :, tok0:tok0 + W], reads=[UT], writes=[uwb])
            bu = BU[d_][par]
            for reim in range(2):
                for half in range(2):
                    pt = k.psum()
                    for g16 in range(16):
                        gp = half * 16 + g16
                        P.op("pe", lambda e, pt=pt, g16=g16, gp=gp, d_=d_, reim=reim, uwb=uwb: e.matmul(
                            pt[:, g16 * W:(g16 + 1) * W], lhsT=Bm[:, d_, reim, gp, :], rhs=uwb[:, gp // 4, :],
                            start=True, stop=True), reads=[Bm, uwb], writes=[pt], inc=(g16 == 15))
                    P.op("act", lambda e, pt=pt, bu=bu, reim=reim, half=half: e.copy(
                        out=bu[:, reim, half * 16:(half + 1) * 16, :],
                        in_=pt[:, :16 * W].rearrange("p (g w) -> p g w", w=W)), reads=[pt], writes=[bu])
            eng = ENG[d_]
            Hc = H[d_][par]
            Hp = H[d_][1 - par]
            a1, a2 = tm1[d_], tm2[d_]
            for j in range(W):
                c = j if d_ == 0 else W - 1 - j
                if j == 0:
                    Hprev, cp = Hp, (W - 1 if d_ == 0 else 0)
                else:
                    Hprev, cp = Hc, (c - 1 if d_ == 0 else c + 1)
                if w == 0 and j == 0:
                    P.op(eng, lambda e, Hc=Hc, bu=bu, c=c: e.tensor_copy(out=Hc[:, 0:2, :, c], in_=bu[:, :, :, c]),
                         reads=[bu], writes=[Hc])
                else:
                    P.op(eng, lambda e, Hprev=Hprev, cp=cp, a1=a1, d_=d_: e.tensor_tensor(
                        out=a1[:], in0=A1[:, d_], in1=Hprev[:, 0:2, :, cp], op=ALU.mult), reads=[A1, Hprev], writes=[a1])
                    P.op(eng, lambda e, Hprev=Hprev, cp=cp, a2=a2, d_=d_: e.tensor_tensor(
                        out=a2[:], in0=A2[:, d_], in1=Hprev[:, 1:3, :, cp], op=ALU.mult), reads=[A2, Hprev], writes=[a2])
                    P.op(eng, lambda e, a1=a1, a2=a2: e.tensor_tensor(out=a1[:], in0=a1[:], in1=a2[:], op=ALU.add),
                         reads=[a1, a2], writes=[a1])
                    P.op(eng, lambda e, Hc=Hc, bu=bu, c=c, a1=a1: e.tensor_tensor(
                        out=Hc[:, 0:2, :, c], in0=a1[:], in1=bu[:, :, :, c], op=ALU.add), reads=[a1, bu], writes=[Hc])
                P.op(eng, lambda e, Hc=Hc, c=c: e.tensor_copy(out=Hc[:, 2, :, c], in_=Hc[:, 0, :, c]),
                     reads=[Hc], writes=[Hc])
            hb = Hb[d_][par]
            P.op("act", lambda e, hb=hb, Hc=Hc: e.copy(out=hb[:], in_=Hc[:, 0:2, :, :]), reads=[Hc], writes=[hb])
            pt = k.psum()
            for ch in range(8):
                for i4 in range(4):
                    gp = ch * 4 + i4
                    for reim in range(2):
                        first = (i4 == 0 and reim == 0)
                        lastm = (i4 == 3 and reim == 1)
                        P.op("pe", lambda e, pt=pt, ch=ch, gp=gp, reim=reim, d_=d_, hb=hb, first=first, lastm=lastm: e.matmul(
                            pt[:, ch * W:(ch + 1) * W], lhsT=Cb[:, d_, reim, gp, :], rhs=hb[:, reim, gp, :],
                            start=first, stop=lastm), reads=[Cb, hb], writes=[pt], inc=(lastm and ch == 7))
            ysb = ys[d_][par]
            P.op("act", lambda e, pt=pt, ysb=ysb: e.copy(out=ysb[:], in_=pt[:, :8 * W].rearrange("p (c w) -> p c w", w=W)),
                 reads=[pt], writes=[ysb])
            Yd = YF if d_ == 0 else YB
            P.dma("sp", Yd[:, :, tok0:tok0 + W], ysb[:], reads=[ysb], writes=[Yd])
    P.release(mk4)
    if stage == 4:
        o1 = k.out("o_yf", [128, 8 * T], F32)
        P.dma("sp", o1[:], YF[:].rearrange("p a b -> p (a b)"), reads=[YF], writes=[o1])
        o2 = k.out("o_yb", [128, 8 * T], F32)
        P.dma("sp", o2[:], YB[:].rearrange("p a b -> p (a b)"), reads=[YB], writes=[o2])
        P.finish([o1, o2])
        return k

    mk5 = P.mark()
    dg = P.sb("dg", [128, 2, 8], F32)
    P.dma("sp", dg[:], s5d_in[:], reads=[s5d_in], writes=[dg])
    glT = P.sb("glT", [128, 8, T], BF16)
    mk5a = P.mark()
    yft = [P.sb("yft%d" % i, [128, T], F32) for i in range(2)]
    ybt = [P.sb("ybt%d" % i, [128, T], F32) for i in range(2)]
    ut = [P.sb("ut%d" % i, [128, T], BF16) for i in range(2)]
    for ch in range(8):
        a_, b_, u_ = yft[ch % 2], ybt[ch % 2], ut[ch % 2]
        P.dma("sp", a_[:], YF[:, ch, :], reads=[YF], writes=[a_])
        P.dma("sp", b_[:], YB[:, ch, :], reads=[YB], writes=[b_])
        P.dma("sp", u_[:], UT[:, ch, :], reads=[UT], writes=[u_])
        P.op("dve", lambda e, a_=a_, b_=b_: e.tensor_tensor(out=a_[:], in0=a_[:], in1=b_[:], op=ALU.add),
             reads=[a_, b_], writes=[a_])
        P.op("dve", lambda e, a_=a_, u_=u_, ch=ch: e.scalar_tensor_tensor(
            out=a_[:], in0=u_[:], scalar=dg[:, 0, ch:ch + 1], in1=a_[:], op0=ALU.mult, op1=ALU.add),
            reads=[a_, u_, dg], writes=[a_])
        P.op("act", lambda e, a_=a_, ch=ch: e.activation(out=glT[:, ch, :], in_=a_[:], func=AF.Gelu),
             reads=[a_], writes=[glT])
    P.release(mk5a)
    stg5 = [P.sb("stg5_%d" % i, [128, T], BF16) for i in range(2)]
    sgt = [P.sb("sgt%d" % i, [128, 512], BF16) for i in range(2)]
    c5 = {"n": 0}

    def epi_glu(pt, col0, t0, n):
        ch = col0 // 128
        sg = sgt[c5["n"] % 2]
        st = stg5[(c5["n"] // len(TOKT)) % 2]
        c5["n"] += 1
        P.op("act", lambda e: e.activation(out=sg[:, :n], in_=pt[:, :n], func=AF.Sigmoid, bias=dg[:, 1, ch:ch + 1], scale=1.0),
             reads=[pt, dg], writes=[sg])
        P.op("dve", lambda e: e.tensor_tensor(out=st[:, t0:t0 + n], in0=glT[:, ch, t0:t0 + n], in1=sg[:, :n], op=ALU.mult),
             reads=[glT, sg], writes=[st])
        if t0 + n == T:
            P.dma("sp", CAT[:, 8 + ch, :], st[:], reads=[st], writes=[CAT])

    linear(glT, glu_w, 1024, epi_fm=epi_glu, nkc=8)
    P.release(mk5)
    if stage == 5:
        o = k.out("o_cat", [128, NCH * T], BF16)
        P.dma("sp", o[:], CAT[:].rearrange("p a b -> p (a b)"), reads=[CAT], writes=[o])
        P.finish([o])
        return k

    def out_proj_residual(Wout, layer, toks):
        mk = P.mark()
        catT = P.sb("catT", [128, NCH, T], BF16)
        P.dma("sp", catT[:], CAT[:], reads=[CAT], writes=[catT])
        xrow = [P.sb("xrow%d" % i, [128, T], F32) for i in range(2)]
        cnt_ = {"n": 0}
        tlo = toks[0][0]
        thi = toks[-1][0] + toks[-1][1]
        whmap = {t0: wh for (t0, n, wh) in toks}

        def epi(pt, col0, t0, n):
            ch = col0 // 128
            xr = xrow[(cnt_["n"] // len(toks)) % 2]
            if cnt_["n"] % len(toks) == 0:
                P.dma("sp", xr[:, tlo:thi], XT[:, ch, tlo:thi], reads=[XT], writes=[xr])
            cnt_["n"] += 1
            wh = whmap[t0]
            P.op("dve", lambda e: e.scalar_tensor_tensor(
                out=xr[:, t0:t0 + n], in0=pt[:, :n], scalar=gate_ap(layer, 0, wh, ch), in1=xr[:, t0:t0 + n],
                op0=ALU.mult, op1=ALU.add), reads=[pt, mod, xr], writes=[xr])
            if t0 + n == thi:
                P.dma("sp", XT[:, ch, tlo:thi], xr[:, tlo:thi], reads=[xr], writes=[XT])

        linear(catT, Wout, D, epi_fm=epi, toks=[(t0, n) for (t0, n, wh) in toks])
        P.release(mk)

    out_proj_residual(abw_out, 0, [(0, 256, 1), (256, 512, 0), (768, 512, 0), (1280, 512, 0), (1792, 512, 0)])
    if stage == 6:
        o = k.out("o_xt", [128, NCH * T], F32)
        P.dma("sp", o[:], XT[:].rearrange("p a b -> p (a b)"), reads=[XT], writes=[o])
        P.finish([o])
        return k

    def mlp(layer, blocks):
        def do_block(blk):
            b0 = blk[0][0]
            nt = sum(n for (_, n, _) in blk)
            mk = P.mark()
            h2 = P.sb("h2", [128, NCH, nt], BF16)
            hid = P.sb("hid", [128, 64, nt], BF16)
            mkn = P.mark()
            alloc_norm_scratch()
            ti = 0
            for (t0, n, wh) in blk:
                for s0 in range(0, n, 256):
                    xt = NS.xt_tiles[ti % 2]
                    ti += 1
                    P.dma("sp", xt[:, :, :256], XT[:, :, t0 + s0:t0 + s0 + 256], reads=[XT], writes=[xt])
                    norm_tile(xt, 256, layer, 1, wh, h2, t0 + s0 - b0)
            P.release(mkn)
            w1s = [P.sb("w1s%d" % i, [128, NCH, 512], BF16) for i in range(2)]
            rl = [P.sb("rl%d" % i, [128, 512], BF16) for i in range(2)]
            ri = 0
            for sidx in range(4 * D // 512):
                wb = w1s[sidx % 2]
                P.dma("pool", wb[:], mlp_w1[layer, :, sidx * 512:(sidx + 1) * 512].rearrange("(kc p) n -> p kc n", p=128),
                      reads=[], writes=[wb])
                for j in range(4):
                    hc_ = sidx * 4 + j
                    for (t0, n, wh) in blk:
                        pt = k.psum()
                        for kc in range(NCH):
                            P.op("pe", lambda e, kc=kc, j=j, t0=t0, n=n, pt=pt, wb=wb: e.matmul(
                                pt[:, :n], lhsT=wb[:, kc, j * 128:(j + 1) * 128], rhs=h2[:, kc, t0 - b0:t0 - b0 + n],
                                start=(kc == 0), stop=(kc == NCH - 1)), reads=[wb, h2], writes=[pt], inc=(kc == NCH - 1))
                        r_ = rl[ri % 2]
                        P.op("act", lambda e, pt=pt, r_=r_, n=n: e.activation(out=r_[:, :n], in_=pt[:, :n], func=AF.Relu),
                             reads=[pt], writes=[r_])
                        P.op("dve" if ri % 2 == 0 else "pool", lambda e, r_=r_, n=n, hc_=hc_, t0=t0: e.tensor_tensor(
                            out=hid[:, hc_, t0 - b0:t0 - b0 + n], in0=r_[:, :n], in1=r_[:, :n], op=ALU.mult),
                            reads=[r_], writes=[hid])
                        ri += 1
            w2s = [P.sb("w2s%d" % i, [128, 64, 128], BF16) for i in range(2)]
            xr2 = [P.sb("xr2_%d" % i, [128, nt], F32) for i in range(2)]
            for ch in range(NCH):
                wb = w2s[ch % 2]
                P.dma("pool", wb[:], mlp_w2[layer, :, ch * 128:(ch + 1) * 128].rearrange("(kc p) n -> p kc n", p=128),
                      reads=[], writes=[wb])
                xr = xr2[ch % 2]
                P.dma("sp", xr[:], XT[:, ch, b0:b0 + nt], reads=[XT], writes=[xr])
                for (t0, n, wh) in blk:
                    pt = k.psum()
                    for kc in range(64):
                        P.op("pe", lambda e, kc=kc, t0=t0, n=n, pt=pt, wb=wb: e.matmul(
                            pt[:, :n], lhsT=wb[:, kc, :], rhs=hid[:, kc, t0 - b0:t0 - b0 + n],
                            start=(kc == 0), stop=(kc == 63)), reads=[wb, hid], writes=[pt], inc=(kc == 63))
                    P.op("dve", lambda e, pt=pt, xr=xr, t0=t0, n=n, wh=wh, ch=ch: e.scalar_tensor_tensor(
                        out=xr[:, t0 - b0:t0 - b0 + n], in0=pt[:, :n], scalar=gate_ap(layer, 1, wh, ch),
                        in1=xr[:, t0 - b0:t0 - b0 + n], op0=ALU.mult, op1=ALU.add), reads=[pt, mod, xr], writes=[xr])
                P.dma("sp", XT[:, ch, b0:b0 + nt], xr[:], reads=[xr], writes=[XT])
            P.release(mk)

        for blk_ in blocks:
            do_block(blk_)

    mlp(0, [[(0, 256, 1), (256, 512, 0)], [(768, 512, 0), (1280, 256, 0)], [(1536, 512, 0), (2048, 256, 0)]])
    if stage == 7:
        o = k.out("o_xt", [128, NCH * T], F32)
        P.dma("sp", o[:], XT[:].rearrange("p a b -> p (a b)"), reads=[XT], writes=[o])
        P.finish([o])
        return k

    VT2 = P.dram("VT2", [T, 2048], BF16)
    GT2 = P.dram("GT2", [T, 2048], BF16)
    ATd = P.dram("ATd", [64, T], BF16)
    OF = P.dram("OF", [TL, 2048], F32)
    OB = P.dram("OB", [TL, 2048], F32)
    mk8 = P.mark()
    hT = P.sb("hT1", [128, NCH, T], BF16)
    mkn = P.mark()
    alloc_norm_scratch()
    for ti, (t0, n, wh) in enumerate(TT):
        xt = NS.xt_tiles[ti % 2]
        P.dma("sp", xt[:, :, :n], XT[:, :, t0:t0 + n], reads=[XT], writes=[xt])
        norm_tile(xt, n, 1, 0, wh, hT, t0)
    P.release(mkn)
    rope = P.sb("rope", [128, 2, 2, TL], F32)
    P.dma("sp", rope[:], rope_in[:], reads=[rope_in], writes=[rope])
    pre = P.sb("pre", [128, T], F32)
    rtmp = [P.sb("rtmp%d" % i, [128, 512], F32) for i in range(2)]
    stg8 = [P.sb("stg8_%d" % i, [128, T], BF16) for i in range(2)]
    stg8t = [P.sb("stg8t%d" % i, [128, 512], BF16) for i in range(3)]
    c8 = {"fm": 0, "tm": 0}

    def epi_fm8(pt, col0, t0, n):
        c = col0 // 128
        if c >= 32:
            if col0 == 8192:
                sg = stg8[0]
                evac(sg[:64, t0:t0 + n], pt[:64, :n], [pt], [sg])
                if t0 + n == T:
                    P.dma("sp", ATd[:, :], sg[:64, :], reads=[sg], writes=[ATd])
            return
        isk = c >= 16
        cc_ = (c % 16) // 2
        primed = c % 2 == 1
        qscale = 1.0 if isk else 1.0 / 16.0
        if not primed:
            P.op("act", lambda e: e.mul(out=pre[:, t0:t0 + n], in_=pt[:, :n], mul=qscale), reads=[pt], writes=[pre])
            return
        sg = stg8[cc_ % 2]
        if t0 < TC:
            P.op("act", lambda e: e.copy(out=sg[:, t0:t0 + n], in_=pre[:, t0:t0 + n]), reads=[pre], writes=[sg])
        else:
            rc = cc_ % 2
            l0 = t0 - TC
            rt = rtmp[c8["fm"] % 2]
            c8["fm"] += 1
            P.op("dve", lambda e: e.tensor_tensor(out=rt[:, :n], in0=pt[:, :n], in1=rope[:, 1, rc, l0:l0 + n], op=ALU.mult),
                 reads=[pt, rope], writes=[rt])
            P.op("pool", lambda e: e.tensor_tensor(out=pre[:, t0:t0 + n], in0=pre[:, t0:t0 + n], in1=rope[:, 0, rc, l0:l0 + n], op=ALU.mult),
                 reads=[pre, rope], writes=[pre])
            P.op("dve", lambda e: e.scalar_tensor_tensor(out=sg[:, t0:t0 + n], in0=rt[:, :n], scalar=qscale, in1=pre[:, t0:t0 + n],
                                                         op0=ALU.mult, op1=ALU.add), reads=[rt, pre], writes=[sg])
        if t0 + n == T:
            dst = KT if isk else QT
            P.dma("sp", dst[:, cc_, :], sg[:], reads=[sg], writes=[dst])

    def epi_tm8(pt, col0, tok0):
        sg = stg8t[c8["tm"] % 3]
        c8["tm"] += 1
        evac(sg[:], pt[:, :], [pt], [sg])
        if col0 < 4096 + 2048:
            P.dma("sp", VT2[tok0:tok0 + 128, col0 - 4096:col0 - 4096 + 512], sg[:], reads=[sg], writes=[VT2])
        else:
            P.dma("sp", GT2[tok0:tok0 + 128, col0 - 6144:col0 - 6144 + 512], sg[:], reads=[sg], writes=[GT2])

    linear(hT, gla_wcat, 8704, epi_fm=epi_fm8, epi_tm=epi_tm8,
           col_mode=lambda sidx: "tm" if 8 <= sidx < 16 else "fm")
    P.release(mk8)
    if stage == 8:
        outs_ = []
        for nm_, tb_, shp in (("o_q", QT, [128, 8 * T]), ("o_k", KT, [128, 8 * T])):
            o = k.out(nm_, shp, BF16)
            P.dma("sp", o[:], tb_[:].rearrange("p a b -> p (a b)"), reads=[tb_], writes=[o])
            outs_.append(o)
        for nm_, tb_, shp in (("o_v", VT2, [T, 2048]), ("o_g", GT2, [T, 2048]), ("o_a", ATd, [64, T])):
            o = k.out(nm_, shp, BF16)
            P.dma("sp", o[:], tb_[:], reads=[tb_], writes=[o])
            outs_.append(o)
        P.finish(outs_)
        return k

    P.skip = False
    if stage >= 100:
        P.op("dve", lambda e: e.memset(epsb[:], EPS), writes=[epsb])
        QT = k.inp("QT_in", [128, 8, T], BF16)
        KT = k.inp("KT_in", [128, 8, T], BF16)
        VT2 = k.inp("VT2_in", [T, 2048], BF16)
        ATd = k.inp("ATd_in", [64, T], BF16)
    mk9 = P.mark()
    qT = P.sb("gqT", [128, 8, T], BF16)
    kT = P.sb("gkT", [128, 8, T], BF16)
    aT = P.sb("gaT", [64, T], BF16)
    P.dma("sp", qT[:], QT[:], reads=[QT], writes=[qT])
    P.dma("sp", kT[:], KT[:], reads=[KT], writes=[kT])
    P.op("dve", lambda e: e.memset(aT[:], 1.0), writes=[aT])
    P.dma("sp", aT[0:16, :], ATd[0:16, :], reads=[ATd], writes=[aT])
    P.dma("sp", aT[32:48, :], ATd[32:48, :], reads=[ATd], writes=[aT])
    wa2 = P.sb("wa2", [64, 1024], BF16)
    P.dma("pool", wa2[:], wa2_in[:], reads=[wa2_in], writes=[wa2])
    tri = P.sb("tri", [128, 2, 128], F32)
    P.dma("sp", tri[:], tri_in[:], reads=[tri_in], writes=[tri])
    S32 = [P.sb("S32_%d" % i, [128, 8, 512], F32) for i in range(2)]
    Sbf = [P.sb("Sbf_%d" % i, [128, 8, 512], BF16) for i in range(2)]
    for i in range(2):
        P.op("pool", lambda e, i=i: e.memset(S32[i][:], 0.0), writes=[S32[i]])
        P.op("pool", lambda e, i=i: e.memset(Sbf[i][:], 0.0), writes=[Sbf[i]])
    vtc = [P.sb("vtc%d" % i, [128, 2048], BF16) for i in range(3)]
    lap = [P.sb("lap%d" % i, [128, 1024], F32) for i in range(2)]
    e1 = [P.sb("e1_%d" % i, [128, 512], F32) for i in range(2)]
    eq4 = [P.sb("eq4_%d" % i, [128, 512], F32) for i in range(2)]
    ek4 = [P.sb("ek4_%d" % i, [128, 512], F32) for i in range(2)]
    qin = [P.sb("qin%d" % i, [128, 8, 128], BF16) for i in range(2)]
    kin = [P.sb("kin%d" % i, [128, 8, 128], BF16) for i in range(2)]
    kintok = [P.sb("kintok%d" % i, [128, 1024], BF16) for i in range(2)]
    ATb = [P.sb("ATb%d" % i, [128, 4, 128], BF16) for i in range(2)]
    ostg = [P.sb("ostg%d" % i, [128, 2048], F32) for i in range(2)]
    decb = [P.sb("dec%d" % i, [128, 8], F32) for i in range(2)]
    psbf = k.psbf
    seqs = [list(range(18)), [1, 0] + list(range(17, 1, -1))]
    un = 0

    def gla_unit(d_, ci, un):
        tok0 = 128 * ci
        is_lat = ci >= 2
        base = 32 * d_
        last = 127 if d_ == 0 else 0
        v_ = vtc[un % 3]
        P.dma("sp", v_[:], VT2[tok0:tok0 + 128, :], reads=[VT2], writes=[v_])
        la_ = lap[un % 2]
        for half in range(2):
            pz = k.psum()
            P.op("pe", lambda e, pz=pz, half=half: e.matmul(pz[:, :], lhsT=aT[base:base + 17, tok0:tok0 + 128],
                                          rhs=wa2[base:base + 17, half * 512:(half + 1) * 512], start=True, stop=True),
                 reads=[aT, wa2], writes=[pz])
            e_ = e1[half]
            P.op("act", lambda e, pz=pz, e_=e_: e.activation(out=e_[:], in_=pz[:, :], func=AF.Exp, scale=-1.0), reads=[pz], writes=[e_])
            P.op("act", lambda e, e_=e_, half=half: e.activation(out=la_[:, half * 512:(half + 1) * 512], in_=e_[:], func=AF.Ln, bias=1.0, scale=1.0),
                 reads=[e_], writes=[la_])
        qi_, ki_ = qin[un % 2], kin[un % 2]
        dc_ = decb[un % 2]
        for bnk in range(2):
            pc = k.psum()
            for c4 in range(4):
                c = bnk * 4 + c4
                P.op("pe", lambda e, c=c, c4=c4, pc=pc: e.matmul(pc[:, c4 * 128:(c4 + 1) * 128], lhsT=la_[:, c * 128:(c + 1) * 128],
                                                          rhs=tri[:, d_, :], start=True, stop=True),
                     reads=[la_, tri], writes=[pc], inc=(c4 == 3))
            eq_, ek_ = eq4[bnk], ek4[bnk]
            P.op("act", lambda e, pc=pc, eq_=eq_: e.activation(out=eq_[:], in_=pc[:, :], func=AF.Exp, scale=-1.0 / 16.0), reads=[pc], writes=[eq_])
            P.op("act", lambda e, pc=pc, ek_=ek_: e.activation(out=ek_[:], in_=pc[:, :], func=AF.Exp, scale=1.0 / 16.0), reads=[pc], writes=[ek_])
            if is_lat:
                P.op("dve", lambda e, bnk=bnk, eq_=eq_: e.tensor_tensor(out=qi_[:, bnk * 4:bnk * 4 + 4, :], in0=qT[:, bnk * 4:bnk * 4 + 4, tok0:tok0 + 128],
                                                      in1=eq_[:].rearrange("p (c t) -> p c t", t=128), op=ALU.mult),
                     reads=[qT, eq_], writes=[qi_])
            P.op("dve", lambda e, bnk=bnk, ek_=ek_: e.tensor_tensor(out=ki_[:, bnk * 4:bnk * 4 + 4, :], in0=kT[:, bnk * 4:bnk * 4 + 4, tok0:tok0 + 128],
                                                  in1=ek_[:].rearrange("p (c t) -> p c t", t=128), op=ALU.mult),
                 reads=[kT, ek_], writes=[ki_])
            P.op("dve", lambda e, bnk=bnk, eq_=eq_: e.tensor_copy(out=dc_[:, bnk * 4:bnk * 4 + 4],
                                                in_=eq_[:].rearrange("p (c t) -> p c t", t=128)[:, :, last]),
                 reads=[eq_], writes=[dc_])
        kt_ = kintok[un % 2]
        for c in range(8):
            P.op("pe", lambda e, c=c: e.transpose(out=psbf[:, c * 128:(c + 1) * 128], in_=ki_[:, c, :], identity=identb[:]),
                 reads=[ki_, identb], writes=[psbf], inc=(c == 7))
        P.op("act", lambda e: e.copy(out=kt_[:], in_=psbf[:, :]), reads=[psbf], writes=[kt_])
        S32_, Sbf_ = S32[d_], Sbf[d_]
        if is_lat:
            pa = k.psum()
            for h in range(4):
                for dc in range(2):
                    P.op("pe", lambda e, h=h, dc=dc: e.matmul(pa[:, h * 128:(h + 1) * 128], lhsT=ki_[:, 2 * h + dc, :],
                                                              rhs=qi_[:, 2 * h + dc, :], start=(dc == 0), stop=(dc == 1)),
                         reads=[ki_, qi_], writes=[pa], inc=(h == 3 and dc == 1))
            at_ = ATb[un % 2]
            P.op("dve", lambda e: e.tensor_tensor(out=at_[:], in0=pa[:, :].rearrange("p (h t) -> p h t", t=128),
                                                  in1=tri[:, d_:d_ + 1, :].to_broadcast([128, 4, 128]), op=ALU.mult),
                 reads=[pa, tri], writes=[at_])
            os_ = ostg[un % 2]
            for h in range(4):
                po = k.psum()
                P.op("pe", lambda e, h=h, po=po: e.matmul(po[:, :], lhsT=at_[:, h, :], rhs=v_[:, h * 512:(h + 1) * 512],
                                                          start=True, stop=False), reads=[at_, v_], writes=[po], inc=False)
                for dc in range(2):
                    P.op("pe", lambda e, h=h, dc=dc, po=po: e.matmul(po[:, :], lhsT=qi_[:, 2 * h + dc, :], rhs=Sbf_[:, 2 * h + dc, :],
                                                                     start=False, stop=(dc == 1)),
                         reads=[qi_, Sbf_], writes=[po], inc=(dc == 1))
                evac(os_[:, h * 512:(h + 1) * 512], po[:, :], [po], [os_])
            Od = OF if d_ == 0 else OB
            P.dma("sp", Od[tok0 - TC:tok0 - TC + 128, :], os_[:], reads=[os_], writes=[Od])
        for c in range(8):
            h = c // 2
            pS = k.psum()
            P.op("pe", lambda e, c=c, h=h, pS=pS: e.matmul(pS[:, :], lhsT=kt_[:, c * 128:(c + 1) * 128], rhs=v_[:, h * 512:(h + 1) * 512],
                                                          start=True, stop=True), reads=[kt_, v_], writes=[pS])
            P.op("pool", lambda e, c=c: e.tensor_scalar(out=S32_[:, c, :], in0=S32_[:, c, :], scalar1=dc_[:, c:c + 1], scalar2=None,
                                                        op0=ALU.mult), reads=[S32_, dc_], writes=[S32_])
            P.op("dve", lambda e, c=c, pS=pS: e.scalar_tensor_tensor(out=S32_[:, c, :], in0=pS[:, :], scalar=dc_[:, c:c + 1],
                                                                    in1=S32_[:, c, :], op0=ALU.mult, op1=ALU.add),
                 reads=[pS, dc_, S32_], writes=[S32_])
            P.op("pool", lambda e, c=c: e.tensor_copy(out=Sbf_[:, c, :], in_=S32_[:, c, :]), reads=[S32_], writes=[Sbf_])

    for step in range(18):
        for d_ in range(2):
            gla_unit(d_, seqs[d_][step], un)
            un += 1
    P.release(mk9)
    if stage % 100 == 9:
        o1 = k.out("o_of", [TL, 2048], F32)
        P.dma("sp", o1[:], OF[:], reads=[OF], writes=[o1])
        o2 = k.out("o_ob", [TL, 2048], F32)
        P.dma("sp", o2[:], OB[:], reads=[OB], writes=[o2])
        P.finish([o1, o2])
        return k

    if stage >= 100:
        OF = k.inp("OF_in", [TL, 2048], F32)
        OB = k.inp("OB_in", [TL, 2048], F32)
        GT2 = k.inp("GT2_in", [T, 2048], BF16)
    mk10 = P.mark()
    ngb = P.sb("ngb", [128, 512], F32)
    P.dma("sp", ngb[:], gng_in[:], reads=[gng_in], writes=[ngb])
    oft = [P.sb("oft%d" % i, [128, 2048], F32) for i in range(2)]
    obt = [P.sb("obt%d" % i, [128, 2048], F32) for i in range(2)]
    gtt = [P.sb("gtt%d" % i, [128, 2048], BF16) for i in range(2)]
    sgl = [P.sb("sgl%d" % i, [128, 2048], BF16) for i in range(2)]
    sqj = P.sb("sqj", [128, 512], BF16)
    ssq = [P.sb("ssq%d" % i, [128, 4], F32) for i in range(2)]
    ytk = [P.sb("ytk%d" % i, [128, 2048], BF16) for i in range(2)]
    cst = [P.sb("cst%d" % i, [128, NCH, 128], BF16) for i in range(2)]
    psbf = k.psbf

    def fin_chunk(ci):
        r0 = 128 * ci
        a_, b_, g_, s_, q_, y_, c_ = oft[ci % 2], obt[ci % 2], gtt[ci % 2], sgl[ci % 2], ssq[ci % 2], ytk[ci % 2], cst[ci % 2]
        P.dma("sp", a_[:], OF[r0:r0 + 128, :], reads=[OF], writes=[a_])
        P.dma("sp", b_[:], OB[r0:r0 + 128, :], reads=[OB], writes=[b_])
        P.dma("sp", g_[:], GT2[TC + r0:TC + r0 + 128, :], reads=[GT2], writes=[g_])
        P.op("pool", lambda e: e.tensor_tensor(out=a_[:], in0=a_[:], in1=b_[:], op=ALU.add), reads=[a_, b_], writes=[a_])
        P.op("act", lambda e: e.activation(out=s_[:], in_=g_[:], func=AF.Silu), reads=[g_], writes=[s_])
        for h in range(4):
            P.op("act", lambda e, h=h: e.activation(out=sqj[:], in_=a_[:, h * 512:(h + 1) * 512], func=AF.Square,
                                                    accum_out=q_[:, h:h + 1]), reads=[a_], writes=[sqj, q_])
        P.op("act", lambda e: e.activation(out=q_[:], in_=q_[:], func=AF.Sqrt, bias=epsb[:, 0:1], scale=1.0 / 512.0),
             reads=[q_, epsb], writes=[q_])
        P.op("dve", lambda e: e.reciprocal(out=q_[:], in_=q_[:]), reads=[q_], writes=[q_])
        for h in range(4):
            P.op("dve", lambda e, h=h: e.scalar_tensor_tensor(out=a_[:, h * 512:(h + 1) * 512], in0=a_[:, h * 512:(h + 1) * 512],
                                                              scalar=q_[:, h:h + 1], in1=ngb[:], op0=ALU.mult, op1=ALU.mult),
                 reads=[a_, q_, ngb], writes=[a_])
        P.op("dve", lambda e: e.tensor_tensor(out=y_[:], in0=a_[:], in1=s_[:], op=ALU.mult), reads=[a_, s_], writes=[y_])
        for half in range(2):
            for c8_ in range(8):
                c = half * 8 + c8_
                P.op("pe", lambda e, c=c, c8_=c8_: e.transpose(out=psbf[:, c8_ * 128:(c8_ + 1) * 128], in_=y_[:, c * 128:(c + 1) * 128],
                                                              identity=identb[:]), reads=[y_, identb], writes=[psbf], inc=(c8_ == 7))
            P.op("act", lambda e, half=half: e.copy(out=c_[:, half * 8:half * 8 + 8, :],
                                                    in_=psbf[:, :].rearrange("p (c t) -> p c t", t=128)), reads=[psbf], writes=[c_])
        P.dma("sp", CAT[:, :, TC + r0:TC + r0 + 128], c_[:], reads=[c_], writes=[CAT])

    for ci in range(16):
        fin_chunk(ci)
    P.release(mk10)
    if stage % 100 == 10:
        o = k.out("o_cat1", [128, NCH * TL], BF16)
        P.dma("sp", o[:].rearrange("p (a b) -> p a b", a=NCH), CAT[:, :, TC:], reads=[CAT], writes=[o])
        P.finish([o])
        return k

    LAT5 = [(256, 512, 0), (768, 512, 0), (1280, 512, 0), (1792, 512, 0)]
    out_proj_residual(glaw_out, 1, LAT5)
    mlp(1, [[(256, 512, 0), (768, 256, 0)], [(1024, 512, 0), (1536, 256, 0)], [(1792, 512, 0)]])

    mkf = P.mark()
    alloc_norm_scratch()
    xt_tiles = NS.xt_tiles
    out = k.out("out", [TL, D])
    ynf = P.sb("ynf", [128, NCH, 256], F32)
    ytok = [P.sb("ytok%d" % i, [128, D], F32) for i in range(2)]
    yi = 0
    for ti, (t0, n, wh) in enumerate(TT):
        if wh == 1:
            continue
        xt = xt_tiles[ti % 2]
        P.dma("sp", xt[:, :, :n], XT[:, :, t0:t0 + n], reads=[XT], writes=[xt])
        rs = rstd_tile(xt, n)
        for c in range(NCH):
            P.op("dve", lambda e, c=c, xt=xt, rs=rs: e.scalar_tensor_tensor(
                out=ynf[:, c, :n], in0=xt[:, c, :n], scalar=fg[:, c:c + 1], in1=rs[:, :n],
                op0=ALU.mult, op1=ALU.mult), reads=[xt, fg, rs], writes=[ynf])
        for sub in range(n // 128):
            yk = ytok[yi % 2]
            yi += 1
            for q in range(4):
                pt = k.psum()
                for j in range(4):
                    c = q * 4 + j
                    P.op("pe", lambda e, c=c, j=j, pt=pt, sub=sub: e.transpose(
                        out=pt[:, j * 128:(j + 1) * 128], in_=ynf[:, c, sub * 128:(sub + 1) * 128],
                        identity=ident[:]), reads=[ynf, ident], writes=[pt], inc=(j == 3))
                evac(yk[:, q * 512:(q + 1) * 512], pt[:, :], [pt], [yk])
            tok0 = t0 - TC + sub * 128
            P.dma("sp", out[tok0:tok0 + 128, :], yk[:], reads=[yk], writes=[out])
    P.finish([out])
    return k


_CACHE = {}


def na_table(rel_bias):
    keyl = np.arange(128)
    q = np.arange(64)
    kcol = keyl % 64
    win_start = np.clip(q - 8, 0, 48)
    col_ok = (kcol[:, None] >= win_start[None, :]) & (kcol[:, None] < win_start[None, :] + 16)
    ci = np.clip(kcol[:, None] - q[None, :] + 15, 0, 30)
    tab = np.full((128, 37, 8, 64), -1e4, np.float32)
    variants = [(o, 0, 4, 4 * o) for o in range(8)] + [(4, -1, 5, 32)]
    for (o, shift, nt, t0) in variants:
        for kt in range(nt):
            krow_off = 2 * kt + keyl // 64 + shift
            row_ok = (krow_off >= 0) & (krow_off < 8)
            ri = np.clip(krow_off - o + 7, 0, 14)
            ok = col_ok & row_ok[:, None]
            for h in range(8):
                g = rel_bias[h][ri[:, None], ci]
                tab[:, t0 + kt, h, :] = np.where(ok, g, np.float32(-1e4))
    return tab


def rope_table():
    t = np.arange(TL)
    pos = np.stack([(t // 64).astype(np.float32), (t % 64).astype(np.float32)])
    freqs = (10000.0 ** (-np.arange(0, 128, 2, dtype=np.float32) / 128)).astype(np.float32)
    fd = freqs[np.arange(128) % 64]
    ang = (pos[None, :, :] * fd[:, None, None]).astype(np.float32)
    sign = np.where(np.arange(128) < 64, -1.0, 1.0).astype(np.float32)
    return np.ascontiguousarray(np.stack([np.cos(ang), np.sin(ang) * sign[:, None, None]], axis=1).astype(np.float32))


def host_inputs(b, inputs):
    f = np.float32
    def pc(v, n):
        return np.ascontiguousarray(np.asarray(v, f).reshape(n, 128).T)
    m = {}
    m["x"] = np.ascontiguousarray(inputs["x"][b], f)
    m["ctx"] = np.ascontiguousarray(inputs["ctx"][b], f)
    m["cc"] = np.ascontiguousarray(np.stack([pc(inputs["c"][b], NCH), pc(inputs["c_ctx"], NCH)], axis=-1))
    m["ada_w"] = np.ascontiguousarray(inputs["ada_w"], f)
    m["ada_b"] = np.ascontiguousarray(np.stack([pc(inputs["ada_b"][l], 96) for l in range(2)], axis=1))
    m["n1g"] = np.ascontiguousarray(np.stack([pc(inputs["norm1_g"][l], NCH) for l in range(2)], axis=1))
    m["n2g"] = np.ascontiguousarray(np.stack([pc(inputs["norm2_g"][l], NCH) for l in range(2)], axis=1))
    m["fing"] = pc(inputs["final_g"], NCH)
    m["ident"] = np.eye(128, dtype=f)
    m["ab_w_in"] = np.ascontiguousarray(inputs["ab_w_in"][0], f)
    m["na_tab"] = na_table(np.asarray(inputs["na_rel_bias"][0], f))
    def st_layout(a):
        a = np.asarray(a, f).reshape(2, 32, 2, 64)
        return np.ascontiguousarray(a.transpose(2, 3, 0, 1).reshape(128, 64))
    ldt = np.broadcast_to(np.asarray(inputs["s5_log_dt"][0], f)[:, :, None], (2, 64, 64))
    m["s5_lam"] = np.ascontiguousarray(np.stack([st_layout(inputs["s5_lambda_re"][0]), st_layout(inputs["s5_lambda_im"][0]),
                                                 st_layout(ldt)], axis=1))
    Bm = np.zeros((128, 2, 2, 32, 128), f)
    Cm = np.zeros((128, 2, 2, 32, 128), f)
    for reim, (bn, cn) in enumerate((("s5_b_re", "s5_c_re"), ("s5_b_im", "s5_c_im"))):
        Bsrc = np.asarray(inputs[bn][0], f)
        Csrc = np.asarray(inputs[cn][0], f)
        for d_ in range(2):
            for g in range(64):
                gp, two = g // 2, g % 2
                r0 = (g % 8) * 16
                Bm[r0:r0 + 16, d_, reim, gp, two * 64:(two + 1) * 64] = Bsrc[d_, g].T
                Cm[two * 64:(two + 1) * 64, d_, reim, gp, r0:r0 + 16] = Csrc[d_, g].T
    m["s5_B"] = Bm
    m["s5_C"] = Cm
    m["s5_dg"] = np.ascontiguousarray(np.stack([pc(inputs["s5_d"][0], 8), pc(inputs["s5_glu_b"][0], 8)], axis=1))
    m["s5_glu_w"] = np.ascontiguousarray(inputs["s5_glu_w"][0], f)
    m["ab_w_out"] = np.ascontiguousarray(inputs["ab_w_out"][0], f)
    m["mlp_w1"] = np.ascontiguousarray(inputs["mlp_w1"], f)
    m["mlp_w2"] = np.ascontiguousarray(inputs["mlp_w2"], f)
    gw = np.asarray(inputs["gla_w_in"][0], f)
    perm = np.concatenate([np.arange(64, 128), np.arange(0, 64)])
    cols = []
    for base in (0, 1024):
        for c_ in range(8):
            blk = base + c_ * 128
            cols.append(np.arange(blk, blk + 128))
            cols.append(blk + perm)
    cols = np.concatenate(cols)
    apad = np.zeros((D, 512), f)
    apad[:, 0:16] = gw[:, 6144:6160]
    apad[:, 32:48] = gw[:, 6160:6176]
    m["gla_wcat"] = np.ascontiguousarray(np.concatenate([gw[:, cols], gw[:, 2048:6144], apad], axis=1))
    m["rope_tab"] = rope_table()
    wa2 = np.zeros((64, 1024), f)
    wa2[0:16] = inputs["gla_w_a2"][0][0]
    wa2[16] = inputs["gla_b_a"][0][0]
    wa2[32:48] = inputs["gla_w_a2"][0][1]
    wa2[48] = inputs["gla_b_a"][0][1]
    m["gla_wa2"] = wa2
    jj = np.arange(128)
    m["tri"] = np.ascontiguousarray(np.stack([(jj[:, None] <= jj[None, :]).astype(f), (jj[:, None] >= jj[None, :]).astype(f)], axis=1))
    m["gla_ng"] = np.ascontiguousarray(np.broadcast_to(np.asarray(inputs["gla_norm_g"][0], f)[None, :], (128, 512)))
    m["gla_w_out"] = np.ascontiguousarray(inputs["gla_w_out"][0], f)
    return m


def run(inputs, stage=99, cores=8):
    k = build(stage)
    maps = []
    for b in range(cores):
        m = host_inputs(b, inputs)
        maps.append({n: m[n] for n in k.inputs})
    res = run_bass_kernel_spmd(k.nc, maps, core_ids=list(range(cores)))
    return res.results


def kernel(**inputs):
    res = run(inputs)
    out = np.stack([r["out"] for r in res], axis=0)
    return out.astype(np.float32)
```

```python
import numpy as np
import concourse.bass as bass
import concourse.mybir as mybir
from concourse.bass_utils import run_bass_kernel_spmd

F32 = mybir.dt.float32
BF16 = mybir.dt.bfloat16
AF = mybir.ActivationFunctionType
ALU = mybir.AluOpType

D = 2048
TC = 256
TL = 2048
T = TC + TL
NCH = D // 128
EPS = 1e-6


class Buf:
    def __init__(self, prog, name, handle, dma_written=False):
        self.name = name
        self.h = handle
        self.w = {}
        self.r = {}
        self.dsem = None
        self.dcount = 0
        self.prog = prog

    def __getitem__(self, idx):
        return self.h[idx]


class Prog:
    ENGS = ("pe", "act", "dve", "pool", "sp")

    def __init__(self, nc):
        self.nc = nc
        self.lists = {e: [] for e in self.ENGS}
        self.sem = {e: nc.alloc_semaphore(name="sem_" + e) for e in self.ENGS}
        self.cnt = {e: 0 for e in self.ENGS}
        self.waited = {e: {} for e in self.ENGS}
        self.semobj = {}
        for e in self.ENGS:
            self.semobj[id(self.sem[e])] = self.sem[e]
        self.nbuf = 0
        self.ptr = 16512
        self.top = 229344
        self.live = []
        self.pend_w = {}
        self.pend_r = {}
        self.free_dsems = []
        self.dfinal = {}

    def sb(self, name, shape, dt):
        n = 1
        for d_ in shape[1:]:
            n *= d_
        size = n * (2 if dt == BF16 else 4)
        size = (size + 63) // 64 * 64
        off = self.ptr
        self.ptr += size
        assert self.ptr <= self.top, "SBUF overflow allocating %s (%d)" % (name, size)
        self.nbuf += 1
        h = self.nc.alloc_sbuf_tensor_at("sb%d_%s" % (self.nbuf, name), list(shape), dt, offset=off)
        b = Buf(self, name, h)
        b.w = dict(self.pend_w)
        b.r = dict(self.pend_r)
        self.live.append(b)
        return b

    def mark(self):
        return (self.ptr, len(self.live))

    def release(self, mark):
        ptr, nl = mark
        for b in self.live[nl:]:
            for k_, v in list(b.w.items()) + list(b.r.items()):
                if self.pend_w.get(k_, 0) < v:
                    self.pend_w[k_] = v
                    self.pend_r[k_] = v
            if b.dsem is not None:
                self.free_dsems.append((b.dsem, b.dcount, b.dq))
                b.dsem = None
        del self.live[nl:]
        self.ptr = ptr

    def ps(self, name, shape, dt=F32):
        return Buf(self, name, self.nc.alloc_psum_tensor(name, list(shape), dt))

    def dram(self, name, shape, dt, kind="Internal"):
        t = self.nc.dram_tensor(name, list(shape), dt, kind=kind)
        b = Buf(self, name, t.ap())
        return b

    def _deps(self, reads, writes):
        d = {}
        for b in reads:
            for k, v in b.w.items():
                if d.get(k, 0) < v:
                    d[k] = v
        for b in writes:
            for k, v in b.w.items():
                if d.get(k, 0) < v:
                    d[k] = v
            for k, v in b.r.items():
                if d.get(k, 0) < v:
                    d[k] = v
        return d

    def _waits(self, eng, deps):
        out = []
        wd = self.waited[eng]
        for k, v in deps.items():
            if wd.get(k, 0) < v:
                wd[k] = v
                out.append((self.semobj[k], v))
        return out

    skip = False

    def op(self, eng, fn, reads=(), writes=(), inc=True, nosame=False):
        if self.skip:
            return
        deps = self._deps(reads, writes)
        sem = self.sem[eng]
        if nosame and id(sem) in deps:
            del deps[id(sem)]
        if deps.get(id(sem), 0) > self.cnt[eng]:
            del deps[id(sem)]
        waits = self._waits(eng, deps)
        if inc:
            self.cnt[eng] += 1
            val = self.cnt[eng]
            k = id(sem)
            for b in writes:
                b.w[k] = val
            for b in reads:
                b.r[k] = val
        else:
            val = self.cnt[eng] + 1
            k = id(sem)
            for b in writes:
                b.w[k] = val
            for b in reads:
                b.r[k] = val

        def run(e, waits=waits, fn=fn, inc=inc, sem=sem):
            for s, v in waits:
                e.wait_ge(s, v)
            ins = fn(e)
            if inc:
                ins.then_inc(sem, 1)

        self.lists[eng].append(run)

    def dma(self, q, out_ap, in_ap, reads=(), writes=()):
        if self.skip:
            return
        dst = writes[0]
        if dst.dsem is None:
            fl = [i for i, t in enumerate(self.free_dsems) if t[2] == q]
            if fl:
                dst.dsem, dst.dcount, _ = self.free_dsems.pop(fl[0])
                dst.dq = q
            else:
                dst.dq = q
                dst.dsem = self.nc.alloc_semaphore(name="d%d_%s" % (len(self.semobj), dst.name))
                self.semobj[id(dst.dsem)] = dst.dsem
        deps = self._deps(reads, writes)
        k = id(dst.dsem)
        if dst.dcount > 0 and deps.get(k, 0) < dst.dcount:
            deps[k] = dst.dcount
        waits = self._waits(q, deps)
        dst.dcount += 16
        val = dst.dcount
        self.dfinal[k] = val
        for b in writes:
            b.w[k] = val
        for b in reads:
            b.r[k] = val
        dsem = dst.dsem

        def run(e, waits=waits, dsem=dsem, out_ap=out_ap, in_ap=in_ap):
            for s, v in waits:
                e.wait_ge(s, v)
            e.dma_start(out=out_ap, in_=in_ap).then_inc(dsem, 16)

        self.lists[q].append(run)

    def finish(self, final_bufs):
        deps = self._deps(final_bufs, ())
        for k_, v in self.dfinal.items():
            if deps.get(k_, 0) < v:
                deps[k_] = v
        waits = self._waits("sp", deps)

        def run(e, waits=waits):
            for s, v in waits:
                e.wait_ge(s, v)

        self.lists["sp"].append(run)
        L = self.lists
        with self.nc.Block() as block:

            @block.tensor
            def _(e):
                for f in L["pe"]:
                    f(e)

            @block.scalar
            def _(e):
                for f in L["act"]:
                    f(e)

            @block.vector
            def _(e):
                for f in L["dve"]:
                    f(e)

            @block.gpsimd
            def _(e):
                for f in L["pool"]:
                    f(e)

            @block.sync
            def _(e):
                for f in L["sp"]:
                    f(e)


class K:
    def __init__(self, stage=99, debug=False):
        self.stage = stage
        nc = bass.Bass("TRN2", target_bir_lowering=False)
        self.nc = nc
        P = Prog(nc)
        self.P = P
        self.inputs = {}
        self.outs = {}
        self.psb = [P.ps("ps%d" % i, [128, 512]) for i in range(7)]
        self.psbf = P.ps("psbf", [128, 1024], BF16)
        self.ps_rr = 0

    def inp(self, name, shape, dt=F32):
        b = self.P.dram(name, shape, dt, kind="ExternalInput")
        self.inputs[name] = b
        return b

    def out(self, name, shape, dt=F32):
        b = self.P.dram(name, shape, dt, kind="ExternalOutput")
        self.outs[name] = b
        return b

    def psum(self):
        b = self.psb[self.ps_rr % 7]
        self.ps_rr += 1
        return b


def build(stage=99):
    k = K(stage)
    P = k.P
    nc = k.nc
    x_in = k.inp("x", [TL, D])
    ctx_in = k.inp("ctx", [TC, D])
    cc_in = k.inp("cc", [128, NCH, 2])
    ada_w = k.inp("ada_w", [2, D, 6 * D])
    ada_b = k.inp("ada_b", [128, 2, 96])
    n1g = k.inp("n1g", [128, 2, NCH])
    n2g = k.inp("n2g", [128, 2, NCH])
    fing = k.inp("fing", [128, NCH])
    ident_in = k.inp("ident", [128, 128])
    abw_in = k.inp("ab_w_in", [D, 4096])
    natab_in = k.inp("na_tab", [128, 37, 8, 64])
    s5lam_in = k.inp("s5_lam", [128, 3, 64])
    s5B_in = k.inp("s5_B", [128, 2, 2, 32, 128])
    s5C_in = k.inp("s5_C", [128, 2, 2, 32, 128])
    s5d_in = k.inp("s5_dg", [128, 2, 8])
    glu_w = k.inp("s5_glu_w", [1024, 1024])
    abw_out = k.inp("ab_w_out", [D, D])
    mlp_w1 = k.inp("mlp_w1", [2, D, 4 * D])
    mlp_w2 = k.inp("mlp_w2", [2, 4 * D, D])
    gla_wcat = k.inp("gla_wcat", [D, 8704])
    rope_in = k.inp("rope_tab", [128, 2, 2, TL])
    wa2_in = k.inp("gla_wa2", [64, 1024])
    tri_in = k.inp("tri", [128, 2, 128])
    gng_in = k.inp("gla_ng", [128, 512])
    glaw_out = k.inp("gla_w_out", [D, D])

    ident = P.sb("ident", [128, 128], F32)
    P.dma("sp", ident[:], ident_in[:], reads=[ident_in], writes=[ident])
    ones_bf = P.sb("ones_bf", [128, 128], BF16)
    P.op("dve", lambda e: e.memset(ones_bf[:], 1.0), writes=[ones_bf])
    identb = P.sb("identb", [128, 128], BF16)
    P.op("dve", lambda e: e.tensor_copy(out=identb[:], in_=ident[:]), reads=[ident], writes=[identb])

    wl_rr = [0]

    def wload(wb, src, ncols, nkc, stage_bufs, via_hw):
        if not via_hw:
            P.dma("pool", wb[:, :nkc, :ncols], src.rearrange("(kc p) n -> p kc n", p=128), reads=[], writes=[wb])
            return
        pw = stage_bufs[0].h.shape[2]
        for c0 in range(0, ncols, pw):
            st = stage_bufs[wl_rr[0] % len(stage_bufs)]
            eng = "pool" if wl_rr[0] % 2 == 0 else "act"
            wl_rr[0] += 1
            P.dma("sp", st[:, :nkc, :], src[:, c0:c0 + pw].rearrange("(kc p) n -> p kc n", p=128), reads=[], writes=[st])
            if eng == "pool":
                P.op("pool", lambda e, st=st, c0=c0: e.tensor_copy(out=wb[:, :nkc, c0:c0 + pw], in_=st[:, :nkc, :]),
                     reads=[st], writes=[wb])
            else:
                P.op("act", lambda e, st=st, c0=c0: e.copy(out=wb[:, :nkc, c0:c0 + pw], in_=st[:, :nkc, :]),
                     reads=[st], writes=[wb])


    P.skip = stage >= 100
    mod = P.sb("mod", [128, 2, 96, 2], F32)
    cc = P.sb("cc", [128, NCH, 2], F32)
    P.dma("sp", cc[:], cc_in[:], reads=[cc_in], writes=[cc])
    scb = P.sb("scb", [128, NCH, 2], BF16)
    P.op("act", lambda e: e.activation(out=scb[:], in_=cc[:], func=AF.Silu), reads=[cc], writes=[scb])
    adab = P.sb("adab", [128, 2, 96], F32)
    P.dma("sp", adab[:], ada_b[:], reads=[ada_b], writes=[adab])
    NSL = 1024
    mk0 = P.mark()
    wsl = [P.sb("adaw%d" % i, [128, NCH, NSL], BF16) for i in range(2)]
    ada_st = [P.sb("adast%d" % i, [128, NCH, 512], F32) for i in range(2)]
    si = 0
    for layer in range(2):
        for s in range(6 * D // NSL):
            wb = wsl[si % 2]
            si += 1
            wload(wb, ada_w[layer, :, s * NSL:(s + 1) * NSL], NSL, NCH, ada_st, via_hw=(si % 2 == 0))
            pt = k.psum()
            nsub = NSL // 128
            for j in range(nsub):
                for kc in range(NCH):
                    last = (kc == NCH - 1) and (j == nsub - 1)
                    P.op("pe", lambda e, j=j, kc=kc, wb=wb, pt=pt: e.matmul(
                        pt[:, 2 * j:2 * j + 2], lhsT=wb[:, kc, j * 128:(j + 1) * 128], rhs=scb[:, kc, :],
                        start=(kc == 0), stop=(kc == NCH - 1)),
                        reads=[wb, scb], writes=[pt], inc=last)
            c0 = s * nsub
            P.op("dve", lambda e, pt=pt, layer=layer, c0=c0, nsub=nsub: e.tensor_tensor(
                out=mod[:, layer, c0:c0 + nsub, :],
                in0=pt[:, 0:2 * nsub].rearrange("p (c w) -> p c w", w=2),
                in1=adab[:, layer, c0:c0 + nsub].unsqueeze(2).to_broadcast([128, nsub, 2]),
                op=ALU.add), reads=[pt, adab], writes=[mod])

    P.release(mk0)
    if stage == 0:
        o = k.out("o_mod", [128, 2 * 96 * 2])
        P.dma("sp", o[:], mod[:].rearrange("p a b c -> p (a b c)"), reads=[mod], writes=[o])
        P.finish([o])
        return k

    TT = [(0, 256, 1)] + [(256 + 256 * i, 256, 0) for i in range(8)]

    n1 = P.sb("n1", [128, 2, NCH], F32)
    n2 = P.sb("n2", [128, 2, NCH], F32)
    fg = P.sb("fg", [128, NCH], F32)
    P.dma("sp", n1[:], n1g[:], reads=[n1g], writes=[n1])
    P.dma("sp", n2[:], n2g[:], reads=[n2g], writes=[n2])
    P.dma("sp", fg[:], fing[:], reads=[fing], writes=[fg])
    acoef = P.sb("acoef", [128, 2, 2, 2, NCH], F32)
    for layer in range(2):
        for nm in range(2):
            g = n1 if nm == 0 else n2
            for wh in range(2):
                scl = mod[:, layer, (3 * nm + 1) * NCH:(3 * nm + 2) * NCH, wh]
                P.op("dve", lambda e, layer=layer, nm=nm, wh=wh, g=g, scl=scl: e.scalar_tensor_tensor(
                    out=acoef[:, layer, nm, wh, :], in0=scl, scalar=1.0, in1=g[:, layer, :],
                    op0=ALU.add, op1=ALU.mult), reads=[mod, g], writes=[acoef])

    def shift_ap(layer, nm, wh, c):
        return mod[:, layer, 3 * nm * NCH + c, wh:wh + 1]

    def gate_ap(layer, nm, wh, c):
        return mod[:, layer, (3 * nm + 2) * NCH + c, wh:wh + 1]

    XT = P.dram("XT", [128, NCH, T], F32)
    QT = P.dram("QT", [128, 8, T], BF16)
    KT = P.dram("KT", [128, 8, T], BF16)
    UT = P.dram("UT", [128, 8, T], BF16)
    VTOK = P.dram("VTOK", [T, 1024], BF16)
    CAT = P.dram("CAT", [128, NCH, T], BF16)
    epsb = P.sb("epsb", [128, 1], F32)
    P.op("dve", lambda e: e.memset(epsb[:], EPS), writes=[epsb])

    class NS:
        pass

    def alloc_norm_scratch():
        NS.xt_tiles = [P.sb("xt%d" % i, [128, NCH, 256], F32) for i in range(2)]
        NS.sq = P.sb("sq", [128, NCH, 256], BF16)
        NS.rstd = P.sb("rstd", [128, 256], F32)
        NS.tmpn = [P.sb("tmpn%d" % i, [128, 256], F32) for i in range(2)]

    def rstd_tile(xt, n):
        sqb, rs = NS.sq, NS.rstd
        P.op("act", lambda e: e.activation(out=sqb[:, :, :n], in_=xt[:, :, :n], func=AF.Square),
             reads=[xt], writes=[sqb])
        pt = k.psum()
        for c in range(NCH):
            P.op("pe", lambda e, c=c: e.matmul(pt[:, :n], lhsT=ones_bf[:], rhs=sqb[:, c, :n],
                                               start=(c == 0), stop=(c == NCH - 1)),
                 reads=[ones_bf, sqb], writes=[pt], inc=(c == NCH - 1))
        P.op("act", lambda e: e.activation(out=rs[:, :n], in_=pt[:, :n], func=AF.Sqrt, bias=epsb[:, 0:1],
                                           scale=1.0 / D), reads=[pt, epsb], writes=[rs])
        P.op("dve", lambda e: e.reciprocal(out=rs[:, :n], in_=rs[:, :n]), reads=[rs], writes=[rs])
        return rs

    def norm_tile(xt, n, layer, nm, wh, dst, dst_t0):
        rs = rstd_tile(xt, n)
        tmpn = NS.tmpn
        for c in range(NCH):
            tb = tmpn[c % 2]
            P.op("dve", lambda e, c=c, tb=tb: e.scalar_tensor_tensor(
                out=tb[:, :n], in0=xt[:, c, :n], scalar=acoef[:, layer, nm, wh, c:c + 1], in1=rs[:, :n],
                op0=ALU.mult, op1=ALU.mult), reads=[xt, acoef, rs], writes=[tb])
            P.op("act", lambda e, c=c, tb=tb: e.activation(
                out=dst[:, c, dst_t0:dst_t0 + n], in_=tb[:, :n], func=AF.Identity,
                bias=shift_ap(layer, nm, wh, c), scale=1.0), reads=[tb, mod], writes=[dst])

    evac_rr = [0]

    def evac(out_ap, in_ap, reads, writes):
        evac_rr[0] += 1
        if evac_rr[0] % 2 == 0:
            P.op("dve", lambda e: e.tensor_copy(out=out_ap, in_=in_ap), reads=reads, writes=writes)
        else:
            P.op("act", lambda e: e.copy(out=out_ap, in_=in_ap), reads=reads, writes=writes)

    TOKT = [(0, 256), (256, 512), (768, 512), (1280, 512), (1792, 512)]
    def linear(src, W, ncols, epi_fm=None, epi_tm=None, col_mode=None, toks=TOKT, nkc=NCH, wrows=None):
        mk = P.mark()
        SL = 512
        wsl = [P.sb("wsl%d" % i, [128, nkc, SL], BF16) for i in range(2)]
        wst = [P.sb("wst%d" % i, [128, nkc, 256], F32) for i in range(2)]
        for sidx in range(ncols // SL):
            wb = wsl[sidx % 2]
            wload(wb, W[:, sidx * SL:(sidx + 1) * SL], SL, nkc, wst, via_hw=(sidx % 2 == 1))
            mode = col_mode(sidx) if col_mode else "fm"
            if mode == "fm":
                for j in range(SL // 128):
                    for (t0, n) in toks:
                        pt = k.psum()
                        for kc in range(nkc):
                            P.op("pe", lambda e, kc=kc, j=j, t0=t0, n=n, pt=pt, wb=wb: e.matmul(
                                pt[:, :n], lhsT=wb[:, kc, j * 128:(j + 1) * 128], rhs=src[:, kc, t0:t0 + n],
                                start=(kc == 0), stop=(kc == nkc - 1)),
                                reads=[wb, src], writes=[pt], inc=(kc == nkc - 1))
                        epi_fm(pt, sidx * SL + j * 128, t0, n)
            else:
                for (t0, n) in toks:
                    for sub in range(n // 128):
                        tok0 = t0 + sub * 128
                        pt = k.psum()
                        for kc in range(nkc):
                            P.op("pe", lambda e, kc=kc, tok0=tok0, pt=pt, wb=wb: e.matmul(
                                pt[:, :], lhsT=src[:, kc, tok0:tok0 + 128], rhs=wb[:, kc, :],
                                start=(kc == 0), stop=(kc == nkc - 1)),
                                reads=[wb, src], writes=[pt], inc=(kc == nkc - 1))
                        epi_tm(pt, sidx * SL, tok0)
        P.release(mk)

    mk_h = P.mark()
    hT = P.sb("hT", [128, NCH, T], BF16)
    mk1 = P.mark()
    alloc_norm_scratch()
    xt_tiles = NS.xt_tiles
    xtok = [P.sb("xtok%d" % i, [128, D], F32) for i in range(2)]
    li = 0
    for ti, (t0, n, wh) in enumerate(TT):
        xt = xt_tiles[ti % 2]
        for sub in range(n // 128):
            xk = xtok[li % 2]
            li += 1
            tok0 = t0 + sub * 128
            src = ctx_in[tok0:tok0 + 128, :] if wh == 1 else x_in[tok0 - TC:tok0 - TC + 128, :]
            P.dma("sp", xk[:], src, reads=[ctx_in if wh == 1 else x_in], writes=[xk])
            for q in range(4):
                pt = k.psum()
                for j in range(4):
                    c = q * 4 + j
                    P.op("pe", lambda e, c=c, j=j, pt=pt, xk=xk: e.transpose(
                        out=pt[:, j * 128:(j + 1) * 128], in_=xk[:, c * 128:(c + 1) * 128], identity=ident[:]),
                        reads=[xk, ident], writes=[pt], inc=(j == 3))
                evac(xt[:, q * 4:q * 4 + 4, sub * 128:(sub + 1) * 128],
                     pt[:, :].rearrange("p (j t) -> p j t", j=4), [pt], [xt])
        P.dma("sp", XT[:, :, t0:t0 + n], xt[:, :, :n], reads=[xt], writes=[XT])
        norm_tile(xt, n, 0, 0, wh, hT, t0)
    P.release(mk1)
    if stage == 1:
        o = k.out("o_h", [128, NCH * T], BF16)
        P.dma("sp", o[:], hT[:].rearrange("p a b -> p (a b)"), reads=[hT], writes=[o])
        o2 = k.out("o_xt", [128, NCH * T], F32)
        P.dma("sp", o2[:], XT[:].rearrange("p a b -> p (a b)"), reads=[XT], writes=[o2])
        P.finish([o, o2])
        return k

    mk2 = P.mark()
    stg_fm = [P.sb("stgfm%d" % i, [128, T], BF16) for i in range(2)]
    stg_tm = [P.sb("stgtm%d" % i, [128, 512], BF16) for i in range(3)]
    cnt = {"fm": 0, "tm": 0}

    def epi_fm0(pt, col0, t0, n):
        sg = stg_fm[(cnt["fm"] // len(TOKT)) % 2]
        cnt["fm"] += 1
        evac(sg[:, t0:t0 + n], pt[:, :n], [pt], [sg])
        if t0 + n == T:
            dstT, ch = (QT, col0 // 128) if col0 < 1024 else ((KT, (col0 - 1024) // 128) if col0 < 2048 else (UT, (col0 - 3072) // 128))
            P.dma("sp", dstT[:, ch, :], sg[:], reads=[sg], writes=[dstT])

    def epi_tm0(pt, col0, tok0):
        sg = stg_tm[cnt["tm"] % 3]
        cnt["tm"] += 1
        evac(sg[:], pt[:, :], [pt], [sg])
        P.dma("sp", VTOK[tok0:tok0 + 128, col0 - 2048:col0 - 2048 + 512], sg[:], reads=[sg], writes=[VTOK])

    linear(hT, abw_in, 4096, epi_fm=epi_fm0, epi_tm=epi_tm0,
           col_mode=lambda sidx: "tm" if 4 <= sidx < 6 else "fm")
    P.release(mk2)
    P.release(mk_h)
    if stage == 2:
        outs_ = []
        for nm_, tb_, shp in (("o_q", QT, [128, 8 * T]), ("o_k", KT, [128, 8 * T]), ("o_u", UT, [128, 8 * T])):
            o = k.out(nm_, shp, BF16)
            P.dma("sp", o[:], tb_[:].rearrange("p a b -> p (a b)"), reads=[tb_], writes=[o])
            outs_.append(o)
        o = k.out("o_v", [T, 1024], BF16)
        P.dma("sp", o[:], VTOK[:], reads=[VTOK], writes=[o])
        outs_.append(o)
        P.finish(outs_)
        return k

    mk3 = P.mark()
    qT = P.sb("qT", [128, 8, T], BF16)
    kT = P.sb("kT", [128, 8, T], BF16)
    vtok = P.sb("vtok", [128, 18, 1024], BF16)
    ET = P.sb("ET", [128, 37, 8, 64], BF16)
    P.dma("sp", qT[:], QT[:], reads=[QT], writes=[qT])
    P.dma("sp", kT[:], KT[:], reads=[KT], writes=[kT])
    P.dma("sp", vtok[:], VTOK[:].rearrange("(tt p) n -> p tt n", p=128), reads=[VTOK], writes=[vtok])
    mk3b = P.mark()
    tabf = [P.sb("tabf%d" % i, [128, 8, 8, 64], F32) for i in range(2)]
    for i, t0_ in enumerate(range(0, 37, 8)):
        nt_ = min(8, 37 - t0_)
        tb = tabf[i % 2]
        P.dma("sp", tb[:, :nt_], natab_in[:, t0_:t0_ + nt_], reads=[natab_in], writes=[tb])
        P.op("act", lambda e, tb=tb, t0_=t0_, nt_=nt_: e.activation(out=ET[:, t0_:t0_ + nt_], in_=tb[:, :nt_], func=AF.Exp),
             reads=[tb], writes=[ET])
    P.release(mk3b)
    PTb = [P.sb("PT%d" % i, [128, 7, 64], BF16) for i in range(16)]
    recb = [P.sb("rec%d" % i, [128, 512], F32) for i in range(2)]
    narow = [P.sb("narow%d" % i, [128, 8, 64], BF16) for i in range(2)]
    SCALE = 128.0 ** -0.5
    rows = [("c", i) for i in range(4)] + [("l", r) for r in range(32)]
    pti = 0
    for ri_, (kind, r) in enumerate(rows):
        if kind == "c":
            q0 = 64 * r
            wtiles = []
            tab0 = None
        else:
            q0 = TC + 64 * r
            rs = min(max(r - 4, 0), 24)
            o = r - rs
            if rs % 2 == 0:
                wtiles = [(TC + 64 * rs + 128 * i) for i in range(4)]
                tab0 = 4 * o
            else:
                wtiles = [(TC + 64 * (rs - 1) + 128 * i) for i in range(5)]
                tab0 = 32
        ktiles = wtiles + [0, 128]
        nw = len(wtiles)
        nk = len(ktiles)
        bankA = k.psum()
        bankB = k.psum()
        pts = []
        for h in range(8):
            pt = k.psum()
            for i, kt0 in enumerate(ktiles):
                P.op("pe", lambda e, pt=pt, i=i, kt0=kt0, h=h, q0=q0: e.matmul(
                    pt[:, i * 64:(i + 1) * 64], lhsT=kT[:, h, kt0:kt0 + 128], rhs=qT[:, h, q0:q0 + 64],
                    start=True, stop=True), reads=[kT, qT], writes=[pt], inc=(i == nk - 1))
            PT = PTb[pti % 16]
            pti += 1
            P.op("act", lambda e, pt=pt, PT=PT, nk=nk: e.activation(
                out=PT[:, :nk, :], in_=pt[:, :nk * 64].rearrange("p (a b) -> p a b", b=64), func=AF.Exp, scale=SCALE),
                reads=[pt], writes=[PT])
            if nw:
                P.op("dve", lambda e, PT=PT, nw=nw, tab0=tab0, h=h: e.tensor_tensor(
                    out=PT[:, :nw, :], in0=PT[:, :nw, :], in1=ET[:, tab0:tab0 + nw, h, :], op=ALU.mult),
                    reads=[PT, ET], writes=[PT])
            pts.append(PT)
        for h in range(8):
            PT = pts[h]
            for i, kt0 in enumerate(ktiles):
                P.op("pe", lambda e, PT=PT, i=i, kt0=kt0, h=h, bankA=bankA, nk=nk: e.matmul(
                    bankA[:, h * 64:(h + 1) * 64], lhsT=vtok[:, kt0 // 128, h * 128:(h + 1) * 128], rhs=PT[:, i, :],
                    start=(i == 0), stop=(i == nk - 1)), reads=[vtok, PT], writes=[bankA], inc=(i == nk - 1 and h == 7))
        for h in range(8):
            PT = pts[h]
            for i, kt0 in enumerate(ktiles):
                P.op("pe", lambda e, PT=PT, i=i, h=h, bankB=bankB, nk=nk: e.matmul(
                    bankB[:, h * 64:(h + 1) * 64], lhsT=ones_bf[:], rhs=PT[:, i, :],
                    start=(i == 0), stop=(i == nk - 1)), reads=[ones_bf, PT], writes=[bankB], inc=(i == nk - 1 and h == 7))
        rec = recb[ri_ % 2]
        P.op("dve", lambda e, rec=rec, bankB=bankB: e.reciprocal(out=rec[:], in_=bankB[:, :]), reads=[bankB], writes=[rec])
        nr = narow[ri_ % 2]
        P.op("dve", lambda e, rec=rec, bankA=bankA, nr=nr: e.tensor_tensor(
            out=nr[:], in0=bankA[:, :].rearrange("p (h q) -> p h q", q=64),
            in1=rec[:].rearrange("p (h q) -> p h q", q=64), op=ALU.mult), reads=[bankA, rec], writes=[nr])
        P.dma("sp", CAT[:, 0:8, q0:q0 + 64], nr[:], reads=[nr], writes=[CAT])
    P.release(mk3)
    if stage == 3:
        o = k.out("o_na", [128, 8 * T], BF16)
        P.dma("sp", o[:], CAT[:, 0:8, :], reads=[CAT], writes=[o])
        P.finish([o])
        return k

    YF = P.dram("YF", [128, 8, T], F32)
    YB = P.dram("YB", [128, 8, T], F32)
    mk4 = P.mark()
    Bm = P.sb("Bm", [128, 2, 2, 32, 128], BF16)
    P.dma("pool", Bm[:], s5B_in[:], reads=[s5B_in], writes=[Bm])
    Cb = P.sb("Cb", [128, 2, 2, 32, 128], BF16)
    A1 = P.sb("A1", [128, 2, 2, 32], F32)
    A2 = P.sb("A2", [128, 2, 2, 32], F32)
    mk4a = P.mark()
    lam = P.sb("lam", [128, 3, 64], F32)
    P.dma("sp", lam[:], s5lam_in[:], reads=[s5lam_in], writes=[lam])
    sc_ = {}

    def T64(nm):
        sc_[nm] = P.sb("s5_" + nm, [128, 64], F32)
        return sc_[nm]

    dt_ = T64("dt"); mag = T64("mag"); th = T64("th"); t1 = T64("t1"); t2 = T64("t2")
    sn = T64("sn"); cs = T64("cs"); are = T64("are"); aim = T64("aim"); den = T64("den")
    fre = T64("fre"); fim = T64("fim"); nfim = T64("nfim"); am1 = T64("am1"); nfre = T64("nfre")
    TWO_PI = float(2 * np.pi)
    MAGIC = 12582912.0

    def dv(fn, reads, writes):
        P.op("dve", fn, reads=reads, writes=writes)

    P.op("act", lambda e: e.activation(out=dt_[:], in_=lam[:, 2, :], func=AF.Exp), reads=[lam], writes=[dt_])
    dv(lambda e: e.tensor_tensor(out=t1[:], in0=lam[:, 0, :], in1=dt_[:], op=ALU.mult), [lam, dt_], [t1])
    P.op("act", lambda e: e.activation(out=mag[:], in_=t1[:], func=AF.Exp), reads=[t1], writes=[mag])
    dv(lambda e: e.tensor_tensor(out=th[:], in0=lam[:, 1, :], in1=dt_[:], op=ALU.mult), [lam, dt_], [th])

    def sin_of(dst, shift):
        dv(lambda e: e.tensor_scalar(out=t1[:], in0=th[:], scalar1=shift, scalar2=1.0 / TWO_PI, op0=ALU.add, op1=ALU.mult), [th], [t1])
        dv(lambda e: e.tensor_scalar(out=t2[:], in0=t1[:], scalar1=MAGIC, scalar2=None, op0=ALU.add), [t1], [t2])
        dv(lambda e: e.tensor_scalar(out=t2[:], in0=t2[:], scalar1=-MAGIC, scalar2=None, op0=ALU.add), [t2], [t2])
        dv(lambda e: e.tensor_tensor(out=t1[:], in0=t1[:], in1=t2[:], op=ALU.subtract), [t1, t2], [t1])
        dv(lambda e: e.tensor_scalar(out=t1[:], in0=t1[:], scalar1=TWO_PI, scalar2=3.1415925, op0=ALU.mult, op1=ALU.min), [t1], [t1])
        dv(lambda e: e.tensor_scalar(out=t1[:], in0=t1[:], scalar1=-3.1415925, scalar2=None, op0=ALU.max), [t1], [t1])
        P.op("act", lambda e: e.activation(out=dst[:], in_=t1[:], func=AF.Sin), reads=[t1], writes=[dst])

    sin_of(sn, 0.0)
    sin_of(cs, float(np.pi / 2))
    dv(lambda e: e.tensor_tensor(out=are[:], in0=mag[:], in1=cs[:], op=ALU.mult), [mag, cs], [are])
    dv(lambda e: e.tensor_tensor(out=aim[:], in0=mag[:], in1=sn[:], op=ALU.mult), [mag, sn], [aim])
    dv(lambda e: e.tensor_tensor(out=den[:], in0=lam[:, 0, :], in1=lam[:, 0, :], op=ALU.mult), [lam], [den])
    dv(lambda e: e.tensor_tensor(out=t1[:], in0=lam[:, 1, :], in1=lam[:, 1, :], op=ALU.mult), [lam], [t1])
    dv(lambda e: e.tensor_tensor(out=den[:], in0=den[:], in1=t1[:], op=ALU.add), [den, t1], [den])
    dv(lambda e: e.reciprocal(out=den[:], in_=den[:]), [den], [den])
    dv(lambda e: e.tensor_scalar(out=am1[:], in0=are[:], scalar1=-1.0, scalar2=None, op0=ALU.add), [are], [am1])
    dv(lambda e: e.tensor_tensor(out=t1[:], in0=am1[:], in1=lam[:, 0, :], op=ALU.mult), [am1, lam], [t1])
    dv(lambda e: e.tensor_tensor(out=t2[:], in0=aim[:], in1=lam[:, 1, :], op=ALU.mult), [aim, lam], [t2])
    dv(lambda e: e.tensor_tensor(out=t1[:], in0=t1[:], in1=t2[:], op=ALU.add), [t1, t2], [t1])
    dv(lambda e: e.tensor_tensor(out=fre[:], in0=t1[:], in1=den[:], op=ALU.mult), [t1, den], [fre])
    dv(lambda e: e.tensor_tensor(out=t1[:], in0=aim[:], in1=lam[:, 0, :], op=ALU.mult), [aim, lam], [t1])
    dv(lambda e: e.tensor_tensor(out=t2[:], in0=am1[:], in1=lam[:, 1, :], op=ALU.mult), [am1, lam], [t2])
    dv(lambda e: e.tensor_tensor(out=t1[:], in0=t1[:], in1=t2[:], op=ALU.subtract), [t1, t2], [t1])
    dv(lambda e: e.tensor_tensor(out=fim[:], in0=t1[:], in1=den[:], op=ALU.mult), [t1, den], [fim])
    dv(lambda e: e.tensor_scalar(out=nfim[:], in0=fim[:], scalar1=-1.0, scalar2=None, op0=ALU.mult), [fim], [nfim])
    dv(lambda e: e.tensor_scalar(out=nfre[:], in0=fre[:], scalar1=-1.0, scalar2=None, op0=ALU.mult), [fre], [nfre])
    for d_ in range(2):
        sl = slice(d_ * 32, d_ * 32 + 32)
        dv(lambda e, d_=d_, sl=sl: e.tensor_copy(out=A1[:, d_, 0, :], in_=are[:, sl]), [are], [A1])
        dv(lambda e, d_=d_, sl=sl: e.tensor_copy(out=A1[:, d_, 1, :], in_=are[:, sl]), [are], [A1])
        dv(lambda e, d_=d_, sl=sl: e.tensor_scalar(out=A2[:, d_, 0, :], in0=aim[:, sl], scalar1=-1.0, scalar2=None, op0=ALU.mult), [aim], [A2])
        dv(lambda e, d_=d_, sl=sl: e.tensor_copy(out=A2[:, d_, 1, :], in_=aim[:, sl]), [aim], [A2])
    Cf = P.sb("Cf", [128, 2, 32, 128], F32)
    ctmp = P.sb("ctmp", [128, 128], F32)
    for d_ in range(2):
        P.dma("sp", Cf[:], s5C_in[:, d_], reads=[s5C_in], writes=[Cf])
        for gp in range(32):
            col = d_ * 32 + gp
            dv(lambda e, gp=gp, col=col: e.tensor_scalar(out=ctmp[:], in0=Cf[:, 0, gp, :], scalar1=fre[:, col:col + 1], scalar2=None, op0=ALU.mult), [Cf, fre], [ctmp])
            dv(lambda e, gp=gp, col=col, d_=d_: e.scalar_tensor_tensor(out=Cb[:, d_, 0, gp, :], in0=Cf[:, 1, gp, :], scalar=nfim[:, col:col + 1], in1=ctmp[:], op0=ALU.mult, op1=ALU.add), [Cf, nfim, ctmp], [Cb])
            dv(lambda e, gp=gp, col=col: e.tensor_scalar(out=ctmp[:], in0=Cf[:, 0, gp, :], scalar1=nfim[:, col:col + 1], scalar2=None, op0=ALU.mult), [Cf, nfim], [ctmp])
            dv(lambda e, gp=gp, col=col, d_=d_: e.scalar_tensor_tensor(out=Cb[:, d_, 1, gp, :], in0=Cf[:, 1, gp, :], scalar=nfre[:, col:col + 1], in1=ctmp[:], op0=ALU.mult, op1=ALU.add), [Cf, nfre, ctmp], [Cb])
    P.release(mk4a)
    W = 32
    NW = T // W
    H = [[P.sb("H%d%d" % (d_, i), [128, 3, 32, W], F32) for i in range(2)] for d_ in range(2)]
    BU = [[P.sb("BU%d%d" % (d_, i), [128, 2, 32, W], F32) for i in range(2)] for d_ in range(2)]
    Hb = [[P.sb("Hb%d%d" % (d_, i), [128, 2, 32, W], BF16) for i in range(2)] for d_ in range(2)]
    uw = [[P.sb("uw%d%d" % (d_, i), [128, 8, W], BF16) for i in range(2)] for d_ in range(2)]
    ys = [[P.sb("ys%d%d" % (d_, i), [128, 8, W], F32) for i in range(2)] for d_ in range(2)]
    tm1 = [P.sb("tm1_%d" % d_, [128, 2, 32], F32) for d_ in range(2)]
    tm2 = [P.sb("tm2_%d" % d_, [128, 2, 32], F32) for d_ in range(2)]
    ENG = ["dve", "pool"]

    def win_tok0(d_, w):
        if d_ == 0:
            return w * W
        pos = T - (w + 1) * W
        return TC + pos if pos < TL else pos - TL

    for w in range(NW):
        par = w % 2
        for d_ in range(2):
            tok0 = win_tok0(d_, w)
            uwb = uw[d_][par]
            P.dma("sp", uwb[:], UT[:, :, tok0:tok0 + W], reads=[UT], writes=[uwb])
            bu = BU[d_][par]
            for reim in range(2):
                for half in range(2):
                    pt = k.psum()
                    for g16 in range(16):
                        gp = half * 16 + g16
                        P.op("pe", lambda e, pt=pt, g16=g16, gp=gp, d_=d_, reim=reim, uwb=uwb: e.matmul(
                            pt[:, g16 * W:(g16 + 1) * W], lhsT=Bm[:, d_, reim, gp, :], rhs=uwb[:, gp // 4, :],
                            start=True, stop=True), reads=[Bm, uwb], writes=[pt], inc=(g16 == 15))
                    P.op("act", lambda e, pt=pt, bu=bu, reim=reim, half=half: e.copy(
                        out=bu[:, reim, half * 16:(half + 1) * 16, :],
                        in_=pt[:, :16 * W].rearrange("p (g w) -> p g w", w=W)), reads=[pt], writes=[bu])
        for j in range(W):
            ops = [[], []]
            for d_ in range(2):
                Hc = H[d_][par]
                Hp = H[d_][1 - par]
                bu = BU[d_][par]
                a1, a2 = tm1[d_], tm2[d_]
                c = j if d_ == 0 else W - 1 - j
                if j == 0:
                    Hprev, cp = Hp, (W - 1 if d_ == 0 else 0)
                else:
                    Hprev, cp = Hc, (c - 1 if d_ == 0 else c + 1)
                L = ops[d_]
                if w == 0 and j == 0:
                    L.append((lambda e, Hc=Hc, bu=bu, c=c: e.tensor_copy(out=Hc[:, 0:2, :, c], in_=bu[:, :, :, c]), [bu], [Hc]))
                else:
                    L.append((lambda e, Hprev=Hprev, cp=cp, a1=a1, d_=d_: e.tensor_tensor(
                        out=a1[:], in0=A1[:, d_], in1=Hprev[:, 0:2, :, cp], op=ALU.mult), [A1, Hprev], [a1]))
                    L.append((lambda e, Hprev=Hprev, cp=cp, a2=a2, d_=d_: e.tensor_tensor(
                        out=a2[:], in0=A2[:, d_], in1=Hprev[:, 1:3, :, cp], op=ALU.mult), [A2, Hprev], [a2]))
                    L.append((lambda e, a1=a1, a2=a2: e.tensor_tensor(out=a1[:], in0=a1[:], in1=a2[:], op=ALU.add), [a1, a2], [a1]))
                    L.append((lambda e, Hc=Hc, bu=bu, c=c, a1=a1: e.tensor_tensor(
                        out=Hc[:, 0:2, :, c], in0=a1[:], in1=bu[:, :, :, c], op=ALU.add), [a1, bu], [Hc]))
                L.append((lambda e, Hc=Hc, c=c: e.tensor_copy(out=Hc[:, 2, :, c], in_=Hc[:, 0, :, c]), [Hc], [Hc]))
            for i in range(max(len(ops[0]), len(ops[1]))):
                for d_ in range(2):
                    if i < len(ops[d_]):
                        fn, rd, wr = ops[d_][i]
                        P.op("dve", fn, reads=rd, writes=wr, nosame=True)
        for d_ in range(2):
            tok0 = win_tok0(d_, w)
            Hc = H[d_][par]
            hb = Hb[d_][par]
            P.op("act", lambda e, hb=hb, Hc=Hc: e.copy(out=hb[:], in_=Hc[:, 0:2, :, :]), reads=[Hc], writes=[hb])
            pt = k.psum()
            for ch in range(8):
                for i4 in range(4):
                    gp = ch * 4 + i4
                    for reim in range(2):
                        first = (i4 == 0 and reim == 0)
                        lastm = (i4 == 3 and reim == 1)
                        P.op("pe", lambda e, pt=pt, ch=ch, gp=gp, reim=reim, d_=d_, hb=hb, first=first, lastm=lastm: e.matmul(
                            pt[:, ch * W:(ch + 1) * W], lhsT=Cb[:, d_, reim, gp, :], rhs=hb[:, reim, gp, :],
                            start=first, stop=lastm), reads=[Cb, hb], writes=[pt], inc=(lastm and ch == 7))
            ysb = ys[d_][par]
            P.op("act", lambda e, pt=pt, ysb=ysb: e.copy(out=ysb[:], in_=pt[:, :8 * W].rearrange("p (c w) -> p c w", w=W)),
                 reads=[pt], writes=[ysb])
            Yd = YF if d_ == 0 else YB
            P.dma("sp", Yd[:, :, tok0:tok0 + W], ysb[:], reads=[ysb], writes=[Yd])
    P.release(mk4)
    if stage == 4:
        o1 = k.out("o_yf", [128, 8 * T], F32)
        P.dma("sp", o1[:], YF[:].rearrange("p a b -> p (a b)"), reads=[YF], writes=[o1])
        o2 = k.out("o_yb", [128, 8 * T], F32)
        P.dma("sp", o2[:], YB[:].rearrange("p a b -> p (a b)"), reads=[YB], writes=[o2])
        P.finish([o1, o2])
        return k

    mk5 = P.mark()
    dg = P.sb("dg", [128, 2, 8], F32)
    P.dma("sp", dg[:], s5d_in[:], reads=[s5d_in], writes=[dg])
    glT = P.sb("glT", [128, 8, T], BF16)
    mk5a = P.mark()
    yft = [P.sb("yft%d" % i, [128, T], F32) for i in range(2)]
    ybt = [P.sb("ybt%d" % i, [128, T], F32) for i in range(2)]
    ut = [P.sb("ut%d" % i, [128, T], BF16) for i in range(2)]
    for ch in range(8):
        a_, b_, u_ = yft[ch % 2], ybt[ch % 2], ut[ch % 2]
        P.dma("sp", a_[:], YF[:, ch, :], reads=[YF], writes=[a_])
        P.dma("sp", b_[:], YB[:, ch, :], reads=[YB], writes=[b_])
        P.dma("sp", u_[:], UT[:, ch, :], reads=[UT], writes=[u_])
        P.op("dve", lambda e, a_=a_, b_=b_: e.tensor_tensor(out=a_[:], in0=a_[:], in1=b_[:], op=ALU.add),
             reads=[a_, b_], writes=[a_])
        P.op("dve", lambda e, a_=a_, u_=u_, ch=ch: e.scalar_tensor_tensor(
            out=a_[:], in0=u_[:], scalar=dg[:, 0, ch:ch + 1], in1=a_[:], op0=ALU.mult, op1=ALU.add),
            reads=[a_, u_, dg], writes=[a_])
        P.op("act", lambda e, a_=a_, ch=ch: e.activation(out=glT[:, ch, :], in_=a_[:], func=AF.Gelu),
             reads=[a_], writes=[glT])
    P.release(mk5a)
    stg5 = [P.sb("stg5_%d" % i, [128, T], BF16) for i in range(2)]
    sgt = [P.sb("sgt%d" % i, [128, 512], BF16) for i in range(2)]
    c5 = {"n": 0}

    def epi_glu(pt, col0, t0, n):
        ch = col0 // 128
        sg = sgt[c5["n"] % 2]
        st = stg5[(c5["n"] // len(TOKT)) % 2]
        c5["n"] += 1
        P.op("act", lambda e: e.activation(out=sg[:, :n], in_=pt[:, :n], func=AF.Sigmoid, bias=dg[:, 1, ch:ch + 1], scale=1.0),
             reads=[pt, dg], writes=[sg])
        P.op("dve", lambda e: e.tensor_tensor(out=st[:, t0:t0 + n], in0=glT[:, ch, t0:t0 + n], in1=sg[:, :n], op=ALU.mult),
             reads=[glT, sg], writes=[st])
        if t0 + n == T:
            P.dma("sp", CAT[:, 8 + ch, :], st[:], reads=[st], writes=[CAT])

    linear(glT, glu_w, 1024, epi_fm=epi_glu, nkc=8)
    P.release(mk5)
    if stage == 5:
        o = k.out("o_cat", [128, NCH * T], BF16)
        P.dma("sp", o[:], CAT[:].rearrange("p a b -> p (a b)"), reads=[CAT], writes=[o])
        P.finish([o])
        return k

    def out_proj_residual(Wout, layer, toks):
        mk = P.mark()
        catT = P.sb("catT", [128, NCH, T], BF16)
        P.dma("sp", catT[:], CAT[:], reads=[CAT], writes=[catT])
        xrow = [P.sb("xrow%d" % i, [128, T], F32) for i in range(2)]
        cnt_ = {"n": 0}
        tlo = toks[0][0]
        thi = toks[-1][0] + toks[-1][1]
        whmap = {t0: wh for (t0, n, wh) in toks}

        def epi(pt, col0, t0, n):
            ch = col0 // 128
            xr = xrow[(cnt_["n"] // len(toks)) % 2]
            if cnt_["n"] % len(toks) == 0:
                P.dma("sp", xr[:, tlo:thi], XT[:, ch, tlo:thi], reads=[XT], writes=[xr])
            cnt_["n"] += 1
            wh = whmap[t0]
            P.op("dve", lambda e: e.scalar_tensor_tensor(
                out=xr[:, t0:t0 + n], in0=pt[:, :n], scalar=gate_ap(layer, 0, wh, ch), in1=xr[:, t0:t0 + n],
                op0=ALU.mult, op1=ALU.add), reads=[pt, mod, xr], writes=[xr])
            if t0 + n == thi:
                P.dma("sp", XT[:, ch, tlo:thi], xr[:, tlo:thi], reads=[xr], writes=[XT])

        linear(catT, Wout, D, epi_fm=epi, toks=[(t0, n) for (t0, n, wh) in toks])
        P.release(mk)

    out_proj_residual(abw_out, 0, [(0, 256, 1), (256, 512, 0), (768, 512, 0), (1280, 512, 0), (1792, 512, 0)])
    if stage == 6:
        o = k.out("o_xt", [128, NCH * T], F32)
        P.dma("sp", o[:], XT[:].rearrange("p a b -> p (a b)"), reads=[XT], writes=[o])
        P.finish([o])
        return k

    def mlp(layer, blocks):
        def do_block(blk):
            b0 = blk[0][0]
            nt = sum(n for (_, n, _) in blk)
            mk = P.mark()
            h2 = P.sb("h2", [128, NCH, nt], BF16)
            hid = P.sb("hid", [128, 64, nt], BF16)
            mkn = P.mark()
            alloc_norm_scratch()
            ti = 0
            for (t0, n, wh) in blk:
                for s0 in range(0, n, 256):
                    xt = NS.xt_tiles[ti % 2]
                    ti += 1
                    P.dma("sp", xt[:, :, :256], XT[:, :, t0 + s0:t0 + s0 + 256], reads=[XT], writes=[xt])
                    norm_tile(xt, 256, layer, 1, wh, h2, t0 + s0 - b0)
            P.release(mkn)
            mkw1 = P.mark()
            w1s = [P.sb("w1s%d" % i, [128, NCH, 256], BF16) for i in range(2)]
            w1st = [P.sb("w1st%d" % i, [128, NCH, 256], F32) for i in range(2)]
            rl = [P.sb("rl%d" % i, [128, 512], BF16) for i in range(2)]
            ri = 0
            for sidx in range(4 * D // 256):
                wb = w1s[sidx % 2]
                wload(wb, mlp_w1[layer, :, sidx * 256:(sidx + 1) * 256], 256, NCH, w1st, via_hw=(sidx % 4 != 3))
                for j in range(2):
                    hc_ = sidx * 2 + j
                    for (t0, n, wh) in blk:
                        pt = k.psum()
                        for kc in range(NCH):
                            P.op("pe", lambda e, kc=kc, j=j, t0=t0, n=n, pt=pt, wb=wb: e.matmul(
                                pt[:, :n], lhsT=wb[:, kc, j * 128:(j + 1) * 128], rhs=h2[:, kc, t0 - b0:t0 - b0 + n],
                                start=(kc == 0), stop=(kc == NCH - 1)), reads=[wb, h2], writes=[pt], inc=(kc == NCH - 1))
                        r_ = rl[ri % 2]
                        P.op("act", lambda e, pt=pt, r_=r_, n=n: e.activation(out=r_[:, :n], in_=pt[:, :n], func=AF.Relu),
                             reads=[pt], writes=[r_])
                        P.op("dve", lambda e, r_=r_, n=n, hc_=hc_, t0=t0: e.tensor_tensor(
                            out=hid[:, hc_, t0 - b0:t0 - b0 + n], in0=r_[:, :n], in1=r_[:, :n], op=ALU.mult),
                            reads=[r_], writes=[hid])
                        ri += 1
            P.release(mkw1)
            w2s = [P.sb("w2s%d" % i, [128, 64, 128], BF16) for i in range(2)]
            xr2 = [P.sb("xr2_%d" % i, [128, nt], F32) for i in range(2)]
            for ch in range(NCH):
                wb = w2s[ch % 2]
                P.dma("pool", wb[:], mlp_w2[layer, :, ch * 128:(ch + 1) * 128].rearrange("(kc p) n -> p kc n", p=128),
                      reads=[], writes=[wb])
                xr = xr2[ch % 2]
                P.dma("sp", xr[:], XT[:, ch, b0:b0 + nt], reads=[XT], writes=[xr])
                for (t0, n, wh) in blk:
                    pt = k.psum()
                    for kc in range(64):
                        P.op("pe", lambda e, kc=kc, t0=t0, n=n, pt=pt, wb=wb: e.matmul(
                            pt[:, :n], lhsT=wb[:, kc, :], rhs=hid[:, kc, t0 - b0:t0 - b0 + n],
                            start=(kc == 0), stop=(kc == 63)), reads=[wb, hid], writes=[pt], inc=(kc == 63))
                    P.op("dve", lambda e, pt=pt, xr=xr, t0=t0, n=n, wh=wh, ch=ch: e.scalar_tensor_tensor(
                        out=xr[:, t0 - b0:t0 - b0 + n], in0=pt[:, :n], scalar=gate_ap(layer, 1, wh, ch),
                        in1=xr[:, t0 - b0:t0 - b0 + n], op0=ALU.mult, op1=ALU.add), reads=[pt, mod, xr], writes=[xr])
                P.dma("sp", XT[:, ch, b0:b0 + nt], xr[:], reads=[xr], writes=[XT])
            P.release(mk)

        for blk_ in blocks:
            do_block(blk_)

    mlp(0, [[(0, 256, 1), (256, 512, 0)], [(768, 512, 0), (1280, 256, 0)], [(1536, 512, 0), (2048, 256, 0)]])
    if stage == 7:
        o = k.out("o_xt", [128, NCH * T], F32)
        P.dma("sp", o[:], XT[:].rearrange("p a b -> p (a b)"), reads=[XT], writes=[o])
        P.finish([o])
        return k

    VT2 = P.dram("VT2", [T, 2048], BF16)
    GT2 = P.dram("GT2", [T, 2048], BF16)
    ATd = P.dram("ATd", [64, T], BF16)
    OF = P.dram("OF", [TL, 2048], F32)
    OB = P.dram("OB", [TL, 2048], F32)
    mk8 = P.mark()
    hT = P.sb("hT1", [128, NCH, T], BF16)
    mkn = P.mark()
    alloc_norm_scratch()
    for ti, (t0, n, wh) in enumerate(TT):
        xt = NS.xt_tiles[ti % 2]
        P.dma("sp", xt[:, :, :n], XT[:, :, t0:t0 + n], reads=[XT], writes=[xt])
        norm_tile(xt, n, 1, 0, wh, hT, t0)
    P.release(mkn)
    rope = P.sb("rope", [128, 2, 2, TL], F32)
    P.dma("sp", rope[:], rope_in[:], reads=[rope_in], writes=[rope])
    pre = P.sb("pre", [128, T], F32)
    rtmp = [P.sb("rtmp%d" % i, [128, 512], F32) for i in range(2)]
    stg8 = [P.sb("stg8_%d" % i, [128, T], BF16) for i in range(2)]
    stg8t = [P.sb("stg8t%d" % i, [128, 512], BF16) for i in range(3)]
    c8 = {"fm": 0, "tm": 0}

    def epi_fm8(pt, col0, t0, n):
        c = col0 // 128
        if c >= 32:
            if col0 == 8192:
                sg = stg8[0]
                evac(sg[:64, t0:t0 + n], pt[:64, :n], [pt], [sg])
                if t0 + n == T:
                    P.dma("sp", ATd[:, :], sg[:64, :], reads=[sg], writes=[ATd])
            return
        isk = c >= 16
        cc_ = (c % 16) // 2
        primed = c % 2 == 1
        qscale = 1.0 if isk else 1.0 / 16.0
        if not primed:
            P.op("act", lambda e: e.mul(out=pre[:, t0:t0 + n], in_=pt[:, :n], mul=qscale), reads=[pt], writes=[pre])
            return
        sg = stg8[cc_ % 2]
        if t0 < TC:
            P.op("act", lambda e: e.copy(out=sg[:, t0:t0 + n], in_=pre[:, t0:t0 + n]), reads=[pre], writes=[sg])
        else:
            rc = cc_ % 2
            l0 = t0 - TC
            rt = rtmp[c8["fm"] % 2]
            c8["fm"] += 1
            P.op("dve", lambda e: e.tensor_tensor(out=rt[:, :n], in0=pt[:, :n], in1=rope[:, 1, rc, l0:l0 + n], op=ALU.mult),
                 reads=[pt, rope], writes=[rt])
            P.op("pool", lambda e: e.tensor_tensor(out=pre[:, t0:t0 + n], in0=pre[:, t0:t0 + n], in1=rope[:, 0, rc, l0:l0 + n], op=ALU.mult),
                 reads=[pre, rope], writes=[pre])
            P.op("dve", lambda e: e.scalar_tensor_tensor(out=sg[:, t0:t0 + n], in0=rt[:, :n], scalar=qscale, in1=pre[:, t0:t0 + n],
                                                         op0=ALU.mult, op1=ALU.add), reads=[rt, pre], writes=[sg])
        if t0 + n == T:
            dst = KT if isk else QT
            P.dma("sp", dst[:, cc_, :], sg[:], reads=[sg], writes=[dst])

    def epi_tm8(pt, col0, tok0):
        sg = stg8t[c8["tm"] % 3]
        c8["tm"] += 1
        evac(sg[:], pt[:, :], [pt], [sg])
        if col0 < 4096 + 2048:
            P.dma("sp", VT2[tok0:tok0 + 128, col0 - 4096:col0 - 4096 + 512], sg[:], reads=[sg], writes=[VT2])
        else:
            P.dma("sp", GT2[tok0:tok0 + 128, col0 - 6144:col0 - 6144 + 512], sg[:], reads=[sg], writes=[GT2])

    linear(hT, gla_wcat, 8704, epi_fm=epi_fm8, epi_tm=epi_tm8,
           col_mode=lambda sidx: "tm" if 8 <= sidx < 16 else "fm")
    P.release(mk8)
    if stage == 8:
        outs_ = []
        for nm_, tb_, shp in (("o_q", QT, [128, 8 * T]), ("o_k", KT, [128, 8 * T])):
            o = k.out(nm_, shp, BF16)
            P.dma("sp", o[:], tb_[:].rearrange("p a b -> p (a b)"), reads=[tb_], writes=[o])
            outs_.append(o)
        for nm_, tb_, shp in (("o_v", VT2, [T, 2048]), ("o_g", GT2, [T, 2048]), ("o_a", ATd, [64, T])):
            o = k.out(nm_, shp, BF16)
            P.dma("sp", o[:], tb_[:], reads=[tb_], writes=[o])
            outs_.append(o)
        P.finish(outs_)
        return k

    P.skip = False
    if stage >= 100:
        P.op("dve", lambda e: e.memset(epsb[:], EPS), writes=[epsb])
        QT = k.inp("QT_in", [128, 8, T], BF16)
        KT = k.inp("KT_in", [128, 8, T], BF16)
        VT2 = k.inp("VT2_in", [T, 2048], BF16)
        ATd = k.inp("ATd_in", [64, T], BF16)
    mk9 = P.mark()
    gqT = P.sb("gqT", [128, 8, T], BF16)
    gkT = P.sb("gkT", [128, 8, T], BF16)
    gaT = P.sb("gaT", [64, T], BF16)
    P.dma("sp", gqT[:], QT[:], reads=[QT], writes=[gqT])
    P.dma("sp", gkT[:], KT[:], reads=[KT], writes=[gkT])
    P.op("dve", lambda e: e.memset(gaT[:], 1.0), writes=[gaT])
    P.dma("sp", gaT[0:16, :], ATd[0:16, :], reads=[ATd], writes=[gaT])
    P.dma("sp", gaT[32:48, :], ATd[32:48, :], reads=[ATd], writes=[gaT])
    wa2 = P.sb("wa2", [64, 1024], BF16)
    P.dma("pool", wa2[:], wa2_in[:], reads=[wa2_in], writes=[wa2])
    tri = P.sb("tri", [128, 2, 128], F32)
    P.dma("sp", tri[:], tri_in[:], reads=[tri_in], writes=[tri])
    S32 = [P.sb("S32_%d" % i, [128, 8, 512], F32) for i in range(2)]
    Sbf = [P.sb("Sbf_%d" % i, [128, 8, 512], BF16) for i in range(2)]
    for i in range(2):
        P.op("pool", lambda e, i=i: e.memset(S32[i][:], 0.0), writes=[S32[i]])
        P.op("pool", lambda e, i=i: e.memset(Sbf[i][:], 0.0), writes=[Sbf[i]])
    vtc = [P.sb("vtc%d" % i, [128, 2048], BF16) for i in range(3)]
    lap = [P.sb("lap%d" % i, [128, 1024], F32) for i in range(2)]
    e1 = [P.sb("e1_%d" % i, [128, 512], F32) for i in range(2)]
    eq4 = [P.sb("eq4_%d" % i, [128, 512], F32) for i in range(2)]
    ek4 = [P.sb("ek4_%d" % i, [128, 512], F32) for i in range(2)]
    qin = [P.sb("qin%d" % i, [128, 8, 128], BF16) for i in range(2)]
    kin = [P.sb("kin%d" % i, [128, 8, 128], BF16) for i in range(2)]
    kintok = [P.sb("kintok%d" % i, [128, 1024], BF16) for i in range(2)]
    ATb = [P.sb("ATb%d" % i, [128, 4, 128], BF16) for i in range(2)]
    ostg = [P.sb("ostg%d" % i, [128, 2048], F32) for i in range(2)]
    decb = [P.sb("dec%d" % i, [128, 8], F32) for i in range(2)]
    psbf = k.psbf
    seqs = [list(range(18)), [1, 0] + list(range(17, 1, -1))]
    un = 0

    def gla_unit(d_, ci, un):
        tok0 = 128 * ci
        is_lat = ci >= 2
        base = 32 * d_
        last = 127 if d_ == 0 else 0
        v_ = vtc[un % 3]
        P.dma("sp", v_[:], VT2[tok0:tok0 + 128, :], reads=[VT2], writes=[v_])
        la_ = lap[un % 2]
        for half in range(2):
            pz = k.psum()
            P.op("pe", lambda e, pz=pz, half=half: e.matmul(pz[:, :], lhsT=gaT[base:base + 17, tok0:tok0 + 128],
                                          rhs=wa2[base:base + 17, half * 512:(half + 1) * 512], start=True, stop=True),
                 reads=[gaT, wa2], writes=[pz])
            e_ = e1[half]
            P.op("act", lambda e, pz=pz, e_=e_: e.activation(out=e_[:], in_=pz[:, :], func=AF.Exp, scale=-1.0), reads=[pz], writes=[e_])
            P.op("act", lambda e, e_=e_, half=half: e.activation(out=la_[:, half * 512:(half + 1) * 512], in_=e_[:], func=AF.Ln, bias=1.0, scale=1.0),
                 reads=[e_], writes=[la_])
        qi_, ki_ = qin[un % 2], kin[un % 2]
        dc_ = decb[un % 2]
        for bnk in range(2):
            pc = k.psum()
            for c4 in range(4):
                c = bnk * 4 + c4
                P.op("pe", lambda e, c=c, c4=c4, pc=pc: e.matmul(pc[:, c4 * 128:(c4 + 1) * 128], lhsT=la_[:, c * 128:(c + 1) * 128],
                                                          rhs=tri[:, d_, :], start=True, stop=True),
                     reads=[la_, tri], writes=[pc], inc=(c4 == 3))
            eq_, ek_ = eq4[bnk], ek4[bnk]
            P.op("act", lambda e, pc=pc, eq_=eq_: e.activation(out=eq_[:], in_=pc[:, :], func=AF.Exp, scale=-1.0 / 16.0), reads=[pc], writes=[eq_])
            P.op("act", lambda e, pc=pc, ek_=ek_: e.activation(out=ek_[:], in_=pc[:, :], func=AF.Exp, scale=1.0 / 16.0), reads=[pc], writes=[ek_])
            if is_lat:
                P.op("dve", lambda e, bnk=bnk, eq_=eq_: e.tensor_tensor(out=qi_[:, bnk * 4:bnk * 4 + 4, :], in0=gqT[:, bnk * 4:bnk * 4 + 4, tok0:tok0 + 128],
                                                      in1=eq_[:].rearrange("p (c t) -> p c t", t=128), op=ALU.mult),
                     reads=[gqT, eq_], writes=[qi_])
            P.op("dve", lambda e, bnk=bnk, ek_=ek_: e.tensor_tensor(out=ki_[:, bnk * 4:bnk * 4 + 4, :], in0=gkT[:, bnk * 4:bnk * 4 + 4, tok0:tok0 + 128],
                                                  in1=ek_[:].rearrange("p (c t) -> p c t", t=128), op=ALU.mult),
                 reads=[gkT, ek_], writes=[ki_])
            P.op("dve", lambda e, bnk=bnk, eq_=eq_: e.tensor_copy(out=dc_[:, bnk * 4:bnk * 4 + 4],
                                                in_=eq_[:].rearrange("p (c t) -> p c t", t=128)[:, :, last]),
                 reads=[eq_], writes=[dc_])
        kt_ = kintok[un % 2]
        for c in range(8):
            P.op("pe", lambda e, c=c: e.transpose(out=psbf[:, c * 128:(c + 1) * 128], in_=ki_[:, c, :], identity=identb[:]),
                 reads=[ki_, identb], writes=[psbf], inc=(c == 7))
        P.op("act", lambda e: e.copy(out=kt_[:], in_=psbf[:, :]), reads=[psbf], writes=[kt_])
        S32_, Sbf_ = S32[d_], Sbf[d_]
        if is_lat:
            pa = k.psum()
            for h in range(4):
                for dc in range(2):
                    P.op("pe", lambda e, h=h, dc=dc: e.matmul(pa[:, h * 128:(h + 1) * 128], lhsT=ki_[:, 2 * h + dc, :],
                                                              rhs=qi_[:, 2 * h + dc, :], start=(dc == 0), stop=(dc == 1)),
                         reads=[ki_, qi_], writes=[pa], inc=(h == 3 and dc == 1))
            at_ = ATb[un % 2]
            P.op("dve", lambda e: e.tensor_tensor(out=at_[:], in0=pa[:, :].rearrange("p (h t) -> p h t", t=128),
                                                  in1=tri[:, d_:d_ + 1, :].to_broadcast([128, 4, 128]), op=ALU.mult),
                 reads=[pa, tri], writes=[at_])
            os_ = ostg[un % 2]
            for h in range(4):
                po = k.psum()
                P.op("pe", lambda e, h=h, po=po: e.matmul(po[:, :], lhsT=at_[:, h, :], rhs=v_[:, h * 512:(h + 1) * 512],
                                                          start=True, stop=False), reads=[at_, v_], writes=[po], inc=False)
                for dc in range(2):
                    P.op("pe", lambda e, h=h, dc=dc, po=po: e.matmul(po[:, :], lhsT=qi_[:, 2 * h + dc, :], rhs=Sbf_[:, 2 * h + dc, :],
                                                                     start=False, stop=(dc == 1)),
                         reads=[qi_, Sbf_], writes=[po], inc=(dc == 1))
                evac(os_[:, h * 512:(h + 1) * 512], po[:, :], [po], [os_])
            Od = OF if d_ == 0 else OB
            P.dma("sp", Od[tok0 - TC:tok0 - TC + 128, :], os_[:], reads=[os_], writes=[Od])
        for c in range(8):
            h = c // 2
            pS = k.psum()
            P.op("pe", lambda e, c=c, h=h, pS=pS: e.matmul(pS[:, :], lhsT=kt_[:, c * 128:(c + 1) * 128], rhs=v_[:, h * 512:(h + 1) * 512],
                                                          start=True, stop=True), reads=[kt_, v_], writes=[pS])
            P.op("act", lambda e, c=c: e.activation(out=S32_[:, c, :], in_=S32_[:, c, :], func=AF.Copy, scale=dc_[:, c:c + 1]),
                 reads=[S32_, dc_], writes=[S32_])
            P.op("dve", lambda e, c=c, pS=pS: e.scalar_tensor_tensor(out=S32_[:, c, :], in0=pS[:, :], scalar=dc_[:, c:c + 1],
                                                                    in1=S32_[:, c, :], op0=ALU.mult, op1=ALU.add),
                 reads=[pS, dc_, S32_], writes=[S32_])
            P.op("act", lambda e, c=c: e.copy(out=Sbf_[:, c, :], in_=S32_[:, c, :]), reads=[S32_], writes=[Sbf_])

    for step in range(18):
        for d_ in range(2):
            gla_unit(d_, seqs[d_][step], un)
            un += 1
    P.release(mk9)
    if stage % 100 == 9:
        o1 = k.out("o_of", [TL, 2048], F32)
        P.dma("sp", o1[:], OF[:], reads=[OF], writes=[o1])
        o2 = k.out("o_ob", [TL, 2048], F32)
        P.dma("sp", o2[:], OB[:], reads=[OB], writes=[o2])
        P.finish([o1, o2])
        return k

    if stage >= 100:
        OF = k.inp("OF_in", [TL, 2048], F32)
        OB = k.inp("OB_in", [TL, 2048], F32)
        GT2 = k.inp("GT2_in", [T, 2048], BF16)
    mk10 = P.mark()
    ngb = P.sb("ngb", [128, 512], F32)
    P.dma("sp", ngb[:], gng_in[:], reads=[gng_in], writes=[ngb])
    oft = [P.sb("oft%d" % i, [128, 2048], F32) for i in range(2)]
    obt = [P.sb("obt%d" % i, [128, 2048], F32) for i in range(2)]
    gtt = [P.sb("gtt%d" % i, [128, 2048], BF16) for i in range(2)]
    sgl = [P.sb("sgl%d" % i, [128, 2048], BF16) for i in range(2)]
    sqj = P.sb("sqj", [128, 512], BF16)
    ssq = [P.sb("ssq%d" % i, [128, 4], F32) for i in range(2)]
    ytk = [P.sb("ytk%d" % i, [128, 2048], BF16) for i in range(2)]
    cst = [P.sb("cst%d" % i, [128, NCH, 128], BF16) for i in range(2)]
    psbf = k.psbf

    def fin_chunk(ci):
        r0 = 128 * ci
        a_, b_, g_, s_, q_, y_, c_ = oft[ci % 2], obt[ci % 2], gtt[ci % 2], sgl[ci % 2], ssq[ci % 2], ytk[ci % 2], cst[ci % 2]
        P.dma("sp", a_[:], OF[r0:r0 + 128, :], reads=[OF], writes=[a_])
        P.dma("sp", b_[:], OB[r0:r0 + 128, :], reads=[OB], writes=[b_])
        P.dma("sp", g_[:], GT2[TC + r0:TC + r0 + 128, :], reads=[GT2], writes=[g_])
        P.op("pool", lambda e: e.tensor_tensor(out=a_[:], in0=a_[:], in1=b_[:], op=ALU.add), reads=[a_, b_], writes=[a_])
        P.op("act", lambda e: e.activation(out=s_[:], in_=g_[:], func=AF.Silu), reads=[g_], writes=[s_])
        for h in range(4):
            P.op("act", lambda e, h=h: e.activation(out=sqj[:], in_=a_[:, h * 512:(h + 1) * 512], func=AF.Square,
                                                    accum_out=q_[:, h:h + 1]), reads=[a_], writes=[sqj, q_])
        P.op("act", lambda e: e.activation(out=q_[:], in_=q_[:], func=AF.Sqrt, bias=epsb[:, 0:1], scale=1.0 / 512.0),
             reads=[q_, epsb], writes=[q_])
        P.op("dve", lambda e: e.reciprocal(out=q_[:], in_=q_[:]), reads=[q_], writes=[q_])
        for h in range(4):
            P.op("dve", lambda e, h=h: e.scalar_tensor_tensor(out=a_[:, h * 512:(h + 1) * 512], in0=a_[:, h * 512:(h + 1) * 512],
                                                              scalar=q_[:, h:h + 1], in1=ngb[:], op0=ALU.mult, op1=ALU.mult),
                 reads=[a_, q_, ngb], writes=[a_])
        P.op("dve", lambda e: e.tensor_tensor(out=y_[:], in0=a_[:], in1=s_[:], op=ALU.mult), reads=[a_, s_], writes=[y_])
        for half in range(2):
            for c8_ in range(8):
                c = half * 8 + c8_
                P.op("pe", lambda e, c=c, c8_=c8_: e.transpose(out=psbf[:, c8_ * 128:(c8_ + 1) * 128], in_=y_[:, c * 128:(c + 1) * 128],
                                                              identity=identb[:]), reads=[y_, identb], writes=[psbf], inc=(c8_ == 7))
            P.op("act", lambda e, half=half: e.copy(out=c_[:, half * 8:half * 8 + 8, :],
                                                    in_=psbf[:, :].rearrange("p (c t) -> p c t", t=128)), reads=[psbf], writes=[c_])
        P.dma("sp", CAT[:, :, TC + r0:TC + r0 + 128], c_[:], reads=[c_], writes=[CAT])

    for ci in range(16):
        fin_chunk(ci)
    P.release(mk10)
    if stage % 100 == 10:
        o = k.out("o_cat1", [128, NCH * TL], BF16)
        P.dma("sp", o[:].rearrange("p (a b) -> p a b", a=NCH), CAT[:, :, TC:], reads=[CAT], writes=[o])
        P.finish([o])
        return k

    LAT5 = [(256, 512, 0), (768, 512, 0), (1280, 512, 0), (1792, 512, 0)]
    out_proj_residual(glaw_out, 1, LAT5)
    mlp(1, [[(256, 512, 0), (768, 256, 0)], [(1024, 512, 0), (1536, 256, 0)], [(1792, 512, 0)]])

    mkf = P.mark()
    alloc_norm_scratch()
    xt_tiles = NS.xt_tiles
    out = k.out("out", [TL, D])
    ynf = P.sb("ynf", [128, NCH, 256], F32)
    ytok = [P.sb("ytok%d" % i, [128, D], F32) for i in range(2)]
    yi = 0
    for ti, (t0, n, wh) in enumerate(TT):
        if wh == 1:
            continue
        xt = xt_tiles[ti % 2]
        P.dma("sp", xt[:, :, :n], XT[:, :, t0:t0 + n], reads=[XT], writes=[xt])
        rs = rstd_tile(xt, n)
        for c in range(NCH):
            P.op("dve", lambda e, c=c, xt=xt, rs=rs: e.scalar_tensor_tensor(
                out=ynf[:, c, :n], in0=xt[:, c, :n], scalar=fg[:, c:c + 1], in1=rs[:, :n],
                op0=ALU.mult, op1=ALU.mult), reads=[xt, fg, rs], writes=[ynf])
        for sub in range(n // 128):
            yk = ytok[yi % 2]
            yi += 1
            for q in range(4):
                pt = k.psum()
                for j in range(4):
                    c = q * 4 + j
                    P.op("pe", lambda e, c=c, j=j, pt=pt, sub=sub: e.transpose(
                        out=pt[:, j * 128:(j + 1) * 128], in_=ynf[:, c, sub * 128:(sub + 1) * 128],
                        identity=ident[:]), reads=[ynf, ident], writes=[pt], inc=(j == 3))
                evac(yk[:, q * 512:(q + 1) * 512], pt[:, :], [pt], [yk])
            tok0 = t0 - TC + sub * 128
            P.dma("sp", out[tok0:tok0 + 128, :], yk[:], reads=[yk], writes=[out])
    P.finish([out])
    return k


_CACHE = {}


def na_table(rel_bias):
    keyl = np.arange(128)
    q = np.arange(64)
    kcol = keyl % 64
    win_start = np.clip(q - 8, 0, 48)
    col_ok = (kcol[:, None] >= win_start[None, :]) & (kcol[:, None] < win_start[None, :] + 16)
    ci = np.clip(kcol[:, None] - q[None, :] + 15, 0, 30)
    tab = np.full((128, 37, 8, 64), -1e4, np.float32)
    variants = [(o, 0, 4, 4 * o) for o in range(8)] + [(4, -1, 5, 32)]
    for (o, shift, nt, t0) in variants:
        for kt in range(nt):
            krow_off = 2 * kt + keyl // 64 + shift
            row_ok = (krow_off >= 0) & (krow_off < 8)
            ri = np.clip(krow_off - o + 7, 0, 14)
            ok = col_ok & row_ok[:, None]
            for h in range(8):
                g = rel_bias[h][ri[:, None], ci]
                tab[:, t0 + kt, h, :] = np.where(ok, g, np.float32(-1e4))
    return tab


def rope_table():
    t = np.arange(TL)
    pos = np.stack([(t // 64).astype(np.float32), (t % 64).astype(np.float32)])
    freqs = (10000.0 ** (-np.arange(0, 128, 2, dtype=np.float32) / 128)).astype(np.float32)
    fd = freqs[np.arange(128) % 64]
    ang = (pos[None, :, :] * fd[:, None, None]).astype(np.float32)
    sign = np.where(np.arange(128) < 64, -1.0, 1.0).astype(np.float32)
    return np.ascontiguousarray(np.stack([np.cos(ang), np.sin(ang) * sign[:, None, None]], axis=1).astype(np.float32))


def host_inputs(b, inputs):
    f = np.float32
    def pc(v, n):
        return np.ascontiguousarray(np.asarray(v, f).reshape(n, 128).T)
    m = {}
    m["x"] = np.ascontiguousarray(inputs["x"][b], f)
    m["ctx"] = np.ascontiguousarray(inputs["ctx"][b], f)
    m["cc"] = np.ascontiguousarray(np.stack([pc(inputs["c"][b], NCH), pc(inputs["c_ctx"], NCH)], axis=-1))
    m["ada_w"] = np.ascontiguousarray(inputs["ada_w"], f)
    m["ada_b"] = np.ascontiguousarray(np.stack([pc(inputs["ada_b"][l], 96) for l in range(2)], axis=1))
    m["n1g"] = np.ascontiguousarray(np.stack([pc(inputs["norm1_g"][l], NCH) for l in range(2)], axis=1))
    m["n2g"] = np.ascontiguousarray(np.stack([pc(inputs["norm2_g"][l], NCH) for l in range(2)], axis=1))
    m["fing"] = pc(inputs["final_g"], NCH)
    m["ident"] = np.eye(128, dtype=f)
    m["ab_w_in"] = np.ascontiguousarray(inputs["ab_w_in"][0], f)
    m["na_tab"] = na_table(np.asarray(inputs["na_rel_bias"][0], f))
    def st_layout(a):
        a = np.asarray(a, f).reshape(2, 32, 2, 64)
        return np.ascontiguousarray(a.transpose(2, 3, 0, 1).reshape(128, 64))
    ldt = np.broadcast_to(np.asarray(inputs["s5_log_dt"][0], f)[:, :, None], (2, 64, 64))
    m["s5_lam"] = np.ascontiguousarray(np.stack([st_layout(inputs["s5_lambda_re"][0]), st_layout(inputs["s5_lambda_im"][0]),
                                                 st_layout(ldt)], axis=1))
    Bm = np.zeros((128, 2, 2, 32, 128), f)
    Cm = np.zeros((128, 2, 2, 32, 128), f)
    for reim, (bn, cn) in enumerate((("s5_b_re", "s5_c_re"), ("s5_b_im", "s5_c_im"))):
        Bsrc = np.asarray(inputs[bn][0], f)
        Csrc = np.asarray(inputs[cn][0], f)
        for d_ in range(2):
            for g in range(64):
                gp, two = g // 2, g % 2
                r0 = (g % 8) * 16
                Bm[r0:r0 + 16, d_, reim, gp, two * 64:(two + 1) * 64] = Bsrc[d_, g].T
                Cm[two * 64:(two + 1) * 64, d_, reim, gp, r0:r0 + 16] = Csrc[d_, g].T
    m["s5_B"] = Bm
    m["s5_C"] = Cm
    m["s5_dg"] = np.ascontiguousarray(np.stack([pc(inputs["s5_d"][0], 8), pc(inputs["s5_glu_b"][0], 8)], axis=1))
    m["s5_glu_w"] = np.ascontiguousarray(inputs["s5_glu_w"][0], f)
    m["ab_w_out"] = np.ascontiguousarray(inputs["ab_w_out"][0], f)
    m["mlp_w1"] = np.ascontiguousarray(inputs["mlp_w1"], f)
    m["mlp_w2"] = np.ascontiguousarray(inputs["mlp_w2"], f)
    gw = np.asarray(inputs["gla_w_in"][0], f)
    perm = np.concatenate([np.arange(64, 128), np.arange(0, 64)])
    cols = []
    for base in (0, 1024):
        for c_ in range(8):
            blk = base + c_ * 128
            cols.append(np.arange(blk, blk + 128))
            cols.append(blk + perm)
    cols = np.concatenate(cols)
    apad = np.zeros((D, 512), f)
    apad[:, 0:16] = gw[:, 6144:6160]
    apad[:, 32:48] = gw[:, 6160:6176]
    m["gla_wcat"] = np.ascontiguousarray(np.concatenate([gw[:, cols], gw[:, 2048:6144], apad], axis=1))
    m["rope_tab"] = rope_table()
    wa2 = np.zeros((64, 1024), f)
    wa2[0:16] = inputs["gla_w_a2"][0][0]
    wa2[16] = inputs["gla_b_a"][0][0]
    wa2[32:48] = inputs["gla_w_a2"][0][1]
    wa2[48] = inputs["gla_b_a"][0][1]
    m["gla_wa2"] = wa2
    jj = np.arange(128)
    m["tri"] = np.ascontiguousarray(np.stack([(jj[:, None] <= jj[None, :]).astype(f), (jj[:, None] >= jj[None, :]).astype(f)], axis=1))
    m["gla_ng"] = np.ascontiguousarray(np.broadcast_to(np.asarray(inputs["gla_norm_g"][0], f)[None, :], (128, 512)))
    m["gla_w_out"] = np.ascontiguousarray(inputs["gla_w_out"][0], f)
    return m


def run(inputs, stage=99, cores=8):
    k = build(stage)
    maps = []
    for b in range(cores):
        m = host_inputs(b, inputs)
        maps.append({n: m[n] for n in k.inputs})
    res = run_bass_kernel_spmd(k.nc, maps, core_ids=list(range(cores)))
    return res.results


def kernel(**inputs):
    res = run(inputs)
    out = np.stack([r["out"] for r in res], axis=0)
    return out.astype(np.float32)
```

```python
import numpy as np
import concourse.bass as bass
import concourse.mybir as mybir
from concourse.bass_utils import run_bass_kernel_spmd

F32 = mybir.dt.float32
BF16 = mybir.dt.bfloat16
AF = mybir.ActivationFunctionType
ALU = mybir.AluOpType

D = 2048
TC = 256
TL = 2048
T = TC + TL
NCH = D // 128
EPS = 1e-6


class Buf:
    def __init__(self, prog, name, handle, dma_written=False):
        self.name = name
        self.h = handle
        self.w = {}
        self.r = {}
        self.dsem = None
        self.dcount = 0
        self.prog = prog

    def __getitem__(self, idx):
        return self.h[idx]


class Prog:
    ENGS = ("pe", "act", "dve", "pool", "sp")

    def __init__(self, nc):
        self.nc = nc
        self.lists = {e: [] for e in self.ENGS}
        self.sem = {e: nc.alloc_semaphore(name="sem_" + e) for e in self.ENGS}
        self.cnt = {e: 0 for e in self.ENGS}
        self.waited = {e: {} for e in self.ENGS}
        self.semobj = {}
        for e in self.ENGS:
            self.semobj[id(self.sem[e])] = self.sem[e]
        self.nbuf = 0
        self.ptr = 16512
        self.top = 229344
        self.live = []
        self.pend_w = {}
        self.pend_r = {}
        self.free_dsems = []
        self.dfinal = {}

    def sb(self, name, shape, dt):
        n = 1
        for d_ in shape[1:]:
            n *= d_
        size = n * (2 if dt == BF16 else 4)
        size = (size + 63) // 64 * 64
        off = self.ptr
        self.ptr += size
        assert self.ptr <= self.top, "SBUF overflow allocating %s (%d)" % (name, size)
        self.nbuf += 1
        h = self.nc.alloc_sbuf_tensor_at("sb%d_%s" % (self.nbuf, name), list(shape), dt, offset=off)
        b = Buf(self, name, h)
        b.w = dict(self.pend_w)
        b.r = dict(self.pend_r)
        self.live.append(b)
        return b

    def mark(self):
        return (self.ptr, len(self.live))

    def release(self, mark):
        ptr, nl = mark
        for b in self.live[nl:]:
            for k_, v in list(b.w.items()) + list(b.r.items()):
                if self.pend_w.get(k_, 0) < v:
                    self.pend_w[k_] = v
                    self.pend_r[k_] = v
            if b.dsem is not None:
                self.free_dsems.append((b.dsem, b.dcount, b.dq))
                b.dsem = None
        del self.live[nl:]
        self.ptr = ptr

    def ps(self, name, shape, dt=F32):
        return Buf(self, name, self.nc.alloc_psum_tensor(name, list(shape), dt))

    def dram(self, name, shape, dt, kind="Internal"):
        t = self.nc.dram_tensor(name, list(shape), dt, kind=kind)
        b = Buf(self, name, t.ap())
        return b

    def _deps(self, reads, writes):
        d = {}
        for b in reads:
            for k, v in b.w.items():
                if d.get(k, 0) < v:
                    d[k] = v
        for b in writes:
            for k, v in b.w.items():
                if d.get(k, 0) < v:
                    d[k] = v
            for k, v in b.r.items():
                if d.get(k, 0) < v:
                    d[k] = v
        return d

    def _waits(self, eng, deps):
        out = []
        wd = self.waited[eng]
        for k, v in deps.items():
            if wd.get(k, 0) < v:
                wd[k] = v
                out.append((self.semobj[k], v))
        return out

    skip = False

    def op(self, eng, fn, reads=(), writes=(), inc=True, nosame=False):
        if self.skip:
            return
        deps = self._deps(reads, writes)
        sem = self.sem[eng]
        if nosame and id(sem) in deps:
            del deps[id(sem)]
        if deps.get(id(sem), 0) > self.cnt[eng]:
            del deps[id(sem)]
        waits = self._waits(eng, deps)
        if inc:
            self.cnt[eng] += 1
            val = self.cnt[eng]
            k = id(sem)
            for b in writes:
                b.w[k] = val
            for b in reads:
                b.r[k] = val
        else:
            val = self.cnt[eng] + 1
            k = id(sem)
            for b in writes:
                b.w[k] = val
            for b in reads:
                b.r[k] = val

        def run(e, waits=waits, fn=fn, inc=inc, sem=sem):
            for s, v in waits:
                e.wait_ge(s, v)
            ins = fn(e)
            if inc:
                ins.then_inc(sem, 1)

        self.lists[eng].append(run)

    def dma(self, q, out_ap, in_ap, reads=(), writes=()):
        if self.skip:
            return
        dst = writes[0]
        if dst.dsem is None:
            fl = [i for i, t in enumerate(self.free_dsems) if t[2] == q]
            if fl:
                dst.dsem, dst.dcount, _ = self.free_dsems.pop(fl[0])
                dst.dq = q
            else:
                dst.dq = q
                dst.dsem = self.nc.alloc_semaphore(name="d%d_%s" % (len(self.semobj), dst.name))
                self.semobj[id(dst.dsem)] = dst.dsem
        deps = self._deps(reads, writes)
        k = id(dst.dsem)
        if dst.dcount > 0 and deps.get(k, 0) < dst.dcount:
            deps[k] = dst.dcount
        waits = self._waits(q, deps)
        dst.dcount += 16
        val = dst.dcount
        self.dfinal[k] = val
        for b in writes:
            b.w[k] = val
        for b in reads:
            b.r[k] = val
        dsem = dst.dsem

        def run(e, waits=waits, dsem=dsem, out_ap=out_ap, in_ap=in_ap):
            for s, v in waits:
                e.wait_ge(s, v)
            e.dma_start(out=out_ap, in_=in_ap).then_inc(dsem, 16)

        self.lists[q].append(run)

    def finish(self, final_bufs):
        deps = self._deps(final_bufs, ())
        for k_, v in self.dfinal.items():
            if deps.get(k_, 0) < v:
                deps[k_] = v
        waits = self._waits("sp", deps)

        def run(e, waits=waits):
            for s, v in waits:
                e.wait_ge(s, v)

        self.lists["sp"].append(run)
        L = self.lists
        with self.nc.Block() as block:

            @block.tensor
            def _(e):
                for f in L["pe"]:
                    f(e)

            @block.scalar
            def _(e):
                for f in L["act"]:
                    f(e)

            @block.vector
            def _(e):
                for f in L["dve"]:
                    f(e)

            @block.gpsimd
            def _(e):
                for f in L["pool"]:
                    f(e)

            @block.sync
            def _(e):
                for f in L["sp"]:
                    f(e)


class K:
    def __init__(self, stage=99, debug=False):
        self.stage = stage
        nc = bass.Bass("TRN2", target_bir_lowering=False)
        self.nc = nc
        P = Prog(nc)
        self.P = P
        self.inputs = {}
        self.outs = {}
        self.psb = [P.ps("ps%d" % i, [128, 512]) for i in range(7)]
        self.psbf = P.ps("psbf", [128, 1024], BF16)
        self.ps_rr = 0

    def inp(self, name, shape, dt=F32):
        b = self.P.dram(name, shape, dt, kind="ExternalInput")
        self.inputs[name] = b
        return b

    def out(self, name, shape, dt=F32):
        b = self.P.dram(name, shape, dt, kind="ExternalOutput")
        self.outs[name] = b
        return b

    def psum(self):
        b = self.psb[self.ps_rr % 7]
        self.ps_rr += 1
        return b


def build(stage=99):
    k = K(stage)
    P = k.P
    nc = k.nc
    x_in = k.inp("x", [TL, D])
    ctx_in = k.inp("ctx", [TC, D])
    cc_in = k.inp("cc", [128, NCH, 2])
    ada_w = k.inp("ada_w", [2, D, 6 * D])
    ada_b = k.inp("ada_b", [128, 2, 96])
    n1g = k.inp("n1g", [128, 2, NCH])
    n2g = k.inp("n2g", [128, 2, NCH])
    fing = k.inp("fing", [128, NCH])
    ident_in = k.inp("ident", [128, 128])
    abw_in = k.inp("ab_w_in", [D, 4096])
    natab_in = k.inp("na_tab", [128, 37, 8, 64])
    s5lam_in = k.inp("s5_lam", [128, 3, 64])
    s5B_in = k.inp("s5_B", [128, 2, 2, 32, 128])
    s5C_in = k.inp("s5_C", [128, 2, 2, 32, 128])
    s5d_in = k.inp("s5_dg", [128, 2, 8])
    glu_w = k.inp("s5_glu_w", [1024, 1024])
    abw_out = k.inp("ab_w_out", [D, D])
    mlp_w1 = k.inp("mlp_w1", [2, D, 4 * D])
    mlp_w2 = k.inp("mlp_w2", [2, 4 * D, D])
    gla_wcat = k.inp("gla_wcat", [D, 8704])
    rope_in = k.inp("rope_tab", [128, 2, 2, TL])
    wa2_in = k.inp("gla_wa2", [64, 1024])
    tri_in = k.inp("tri", [128, 2, 128])
    gng_in = k.inp("gla_ng", [128, 512])
    glaw_out = k.inp("gla_w_out", [D, D])

    ident = P.sb("ident", [128, 128], F32)
    P.dma("sp", ident[:], ident_in[:], reads=[ident_in], writes=[ident])
    ones_bf = P.sb("ones_bf", [128, 128], BF16)
    P.op("dve", lambda e: e.memset(ones_bf[:], 1.0), writes=[ones_bf])
    identb = P.sb("identb", [128, 128], BF16)
    P.op("dve", lambda e: e.tensor_copy(out=identb[:], in_=ident[:]), reads=[ident], writes=[identb])

    wl_rr = [0]

    def wload(wb, src, ncols, nkc, stage_bufs, via_hw):
        if not via_hw:
            P.dma("pool", wb[:, :nkc, :ncols], src.rearrange("(kc p) n -> p kc n", p=128), reads=[], writes=[wb])
            return
        pw = stage_bufs[0].h.shape[2]
        for c0 in range(0, ncols, pw):
            st = stage_bufs[wl_rr[0] % len(stage_bufs)]
            eng = "dve" if wl_rr[0] % 2 == 0 else "act"
            wl_rr[0] += 1
            P.dma("sp", st[:, :nkc, :], src[:, c0:c0 + pw].rearrange("(kc p) n -> p kc n", p=128), reads=[], writes=[st])
            if eng == "dve":
                P.op("dve", lambda e, st=st, c0=c0: e.tensor_copy(out=wb[:, :nkc, c0:c0 + pw], in_=st[:, :nkc, :]),
                     reads=[st], writes=[wb])
            else:
                P.op("act", lambda e, st=st, c0=c0: e.copy(out=wb[:, :nkc, c0:c0 + pw], in_=st[:, :nkc, :]),
                     reads=[st], writes=[wb])


    P.skip = stage >= 100
    mod = P.sb("mod", [128, 2, 96, 2], F32)
    cc = P.sb("cc", [128, NCH, 2], F32)
    P.dma("sp", cc[:], cc_in[:], reads=[cc_in], writes=[cc])
    scb = P.sb("scb", [128, NCH, 2], BF16)
    P.op("act", lambda e: e.activation(out=scb[:], in_=cc[:], func=AF.Silu), reads=[cc], writes=[scb])
    adab = P.sb("adab", [128, 2, 96], F32)
    P.dma("sp", adab[:], ada_b[:], reads=[ada_b], writes=[adab])
    NSL = 1024
    mk0 = P.mark()
    wsl = [P.sb("adaw%d" % i, [128, NCH, NSL], BF16) for i in range(2)]
    ada_st = [P.sb("adast%d" % i, [128, NCH, 512], F32) for i in range(2)]
    si = 0
    for layer in range(2):
        for s in range(6 * D // NSL):
            wb = wsl[si % 2]
            si += 1
            wload(wb, ada_w[layer, :, s * NSL:(s + 1) * NSL], NSL, NCH, ada_st, via_hw=(si % 2 == 0))
            pt = k.psum()
            nsub = NSL // 128
            for j in range(nsub):
                for kc in range(NCH):
                    last = (kc == NCH - 1) and (j == nsub - 1)
                    P.op("pe", lambda e, j=j, kc=kc, wb=wb, pt=pt: e.matmul(
                        pt[:, 2 * j:2 * j + 2], lhsT=wb[:, kc, j * 128:(j + 1) * 128], rhs=scb[:, kc, :],
                        start=(kc == 0), stop=(kc == NCH - 1)),
                        reads=[wb, scb], writes=[pt], inc=last)
            c0 = s * nsub
            P.op("dve", lambda e, pt=pt, layer=layer, c0=c0, nsub=nsub: e.tensor_tensor(
                out=mod[:, layer, c0:c0 + nsub, :],
                in0=pt[:, 0:2 * nsub].rearrange("p (c w) -> p c w", w=2),
                in1=adab[:, layer, c0:c0 + nsub].unsqueeze(2).to_broadcast([128, nsub, 2]),
                op=ALU.add), reads=[pt, adab], writes=[mod])

    P.release(mk0)
    if stage == 0:
        o = k.out("o_mod", [128, 2 * 96 * 2])
        P.dma("sp", o[:], mod[:].rearrange("p a b c -> p (a b c)"), reads=[mod], writes=[o])
        P.finish([o])
        return k

    TT = [(0, 256, 1)] + [(256 + 256 * i, 256, 0) for i in range(8)]

    n1 = P.sb("n1", [128, 2, NCH], F32)
    n2 = P.sb("n2", [128, 2, NCH], F32)
    fg = P.sb("fg", [128, NCH], F32)
    P.dma("sp", n1[:], n1g[:], reads=[n1g], writes=[n1])
    P.dma("sp", n2[:], n2g[:], reads=[n2g], writes=[n2])
    P.dma("sp", fg[:], fing[:], reads=[fing], writes=[fg])
    acoef = P.sb("acoef", [128, 2, 2, 2, NCH], F32)
    for layer in range(2):
        for nm in range(2):
            g = n1 if nm == 0 else n2
            for wh in range(2):
                scl = mod[:, layer, (3 * nm + 1) * NCH:(3 * nm + 2) * NCH, wh]
                P.op("dve", lambda e, layer=layer, nm=nm, wh=wh, g=g, scl=scl: e.scalar_tensor_tensor(
                    out=acoef[:, layer, nm, wh, :], in0=scl, scalar=1.0, in1=g[:, layer, :],
                    op0=ALU.add, op1=ALU.mult), reads=[mod, g], writes=[acoef])

    def shift_ap(layer, nm, wh, c):
        return mod[:, layer, 3 * nm * NCH + c, wh:wh + 1]

    def gate_ap(layer, nm, wh, c):
        return mod[:, layer, (3 * nm + 2) * NCH + c, wh:wh + 1]

    XT = P.dram("XT", [128, NCH, T], F32)
    QT = P.dram("QT", [128, 8, T], BF16)
    KT = P.dram("KT", [128, 8, T], BF16)
    UT = P.dram("UT", [128, 8, T], BF16)
    VTOK = P.dram("VTOK", [T, 1024], BF16)
    CAT = P.dram("CAT", [128, NCH, T], BF16)
    epsb = P.sb("epsb", [128, 1], F32)
    P.op("dve", lambda e: e.memset(epsb[:], EPS), writes=[epsb])

    class NS:
        pass

    def alloc_norm_scratch():
        NS.xt_tiles = [P.sb("xt%d" % i, [128, NCH, 256], F32) for i in range(2)]
        NS.sq = P.sb("sq", [128, NCH, 256], BF16)
        NS.rstd = P.sb("rstd", [128, 256], F32)
        NS.tmpn = [P.sb("tmpn%d" % i, [128, 256], F32) for i in range(2)]

    def rstd_tile(xt, n):
        sqb, rs = NS.sq, NS.rstd
        P.op("act", lambda e: e.activation(out=sqb[:, :, :n], in_=xt[:, :, :n], func=AF.Square),
             reads=[xt], writes=[sqb])
        pt = k.psum()
        for c in range(NCH):
            P.op("pe", lambda e, c=c: e.matmul(pt[:, :n], lhsT=ones_bf[:], rhs=sqb[:, c, :n],
                                               start=(c == 0), stop=(c == NCH - 1)),
                 reads=[ones_bf, sqb], writes=[pt], inc=(c == NCH - 1))
        P.op("act", lambda e: e.activation(out=rs[:, :n], in_=pt[:, :n], func=AF.Sqrt, bias=epsb[:, 0:1],
                                           scale=1.0 / D), reads=[pt, epsb], writes=[rs])
        P.op("dve", lambda e: e.reciprocal(out=rs[:, :n], in_=rs[:, :n]), reads=[rs], writes=[rs])
        return rs

    def norm_tile(xt, n, layer, nm, wh, dst, dst_t0):
        rs = rstd_tile(xt, n)
        tmpn = NS.tmpn
        for c in range(NCH):
            tb = tmpn[c % 2]
            P.op("dve", lambda e, c=c, tb=tb: e.scalar_tensor_tensor(
                out=tb[:, :n], in0=xt[:, c, :n], scalar=acoef[:, layer, nm, wh, c:c + 1], in1=rs[:, :n],
                op0=ALU.mult, op1=ALU.mult), reads=[xt, acoef, rs], writes=[tb])
            P.op("act", lambda e, c=c, tb=tb: e.activation(
                out=dst[:, c, dst_t0:dst_t0 + n], in_=tb[:, :n], func=AF.Identity,
                bias=shift_ap(layer, nm, wh, c), scale=1.0), reads=[tb, mod], writes=[dst])

    evac_rr = [0]

    def evac(out_ap, in_ap, reads, writes):
        evac_rr[0] += 1
        if evac_rr[0] % 2 == 0:
            P.op("dve", lambda e: e.tensor_copy(out=out_ap, in_=in_ap), reads=reads, writes=writes)
        else:
            P.op("act", lambda e: e.copy(out=out_ap, in_=in_ap), reads=reads, writes=writes)

    TOKT = [(0, 256), (256, 512), (768, 512), (1280, 512), (1792, 512)]
    def linear(src, W, ncols, epi_fm=None, epi_tm=None, col_mode=None, toks=TOKT, nkc=NCH, wrows=None):
        mk = P.mark()
        SL = 512
        wsl = [P.sb("wsl%d" % i, [128, nkc, SL], BF16) for i in range(2)]
        wst = [P.sb("wst%d" % i, [128, nkc, 256], F32) for i in range(2)]
        for sidx in range(ncols // SL):
            wb = wsl[sidx % 2]
            wload(wb, W[:, sidx * SL:(sidx + 1) * SL], SL, nkc, wst, via_hw=(sidx % 2 == 1))
            mode = col_mode(sidx) if col_mode else "fm"
            if mode == "fm":
                for j in range(SL // 128):
                    for (t0, n) in toks:
                        pt = k.psum()
                        for kc in range(nkc):
                            P.op("pe", lambda e, kc=kc, j=j, t0=t0, n=n, pt=pt, wb=wb: e.matmul(
                                pt[:, :n], lhsT=wb[:, kc, j * 128:(j + 1) * 128], rhs=src[:, kc, t0:t0 + n],
                                start=(kc == 0), stop=(kc == nkc - 1)),
                                reads=[wb, src], writes=[pt], inc=(kc == nkc - 1))
                        epi_fm(pt, sidx * SL + j * 128, t0, n)
            else:
                for (t0, n) in toks:
                    for sub in range(n // 128):
                        tok0 = t0 + sub * 128
                        pt = k.psum()
                        for kc in range(nkc):
                            P.op("pe", lambda e, kc=kc, tok0=tok0, pt=pt, wb=wb: e.matmul(
                                pt[:, :], lhsT=src[:, kc, tok0:tok0 + 128], rhs=wb[:, kc, :],
                                start=(kc == 0), stop=(kc == nkc - 1)),
                                reads=[wb, src], writes=[pt], inc=(kc == nkc - 1))
                        epi_tm(pt, sidx * SL, tok0)
        P.release(mk)

    mk_h = P.mark()
    hT = P.sb("hT", [128, NCH, T], BF16)
    mk1 = P.mark()
    alloc_norm_scratch()
    xt_tiles = NS.xt_tiles
    xtok = [P.sb("xtok%d" % i, [128, D], F32) for i in range(2)]
    li = 0
    for ti, (t0, n, wh) in enumerate(TT):
        xt = xt_tiles[ti % 2]
        for sub in range(n // 128):
            xk = xtok[li % 2]
            li += 1
            tok0 = t0 + sub * 128
            src = ctx_in[tok0:tok0 + 128, :] if wh == 1 else x_in[tok0 - TC:tok0 - TC + 128, :]
            P.dma("sp", xk[:], src, reads=[ctx_in if wh == 1 else x_in], writes=[xk])
            for q in range(4):
                pt = k.psum()
                for j in range(4):
                    c = q * 4 + j
                    P.op("pe", lambda e, c=c, j=j, pt=pt, xk=xk: e.transpose(
                        out=pt[:, j * 128:(j + 1) * 128], in_=xk[:, c * 128:(c + 1) * 128], identity=ident[:]),
                        reads=[xk, ident], writes=[pt], inc=(j == 3))
                evac(xt[:, q * 4:q * 4 + 4, sub * 128:(sub + 1) * 128],
                     pt[:, :].rearrange("p (j t) -> p j t", j=4), [pt], [xt])
        P.dma("sp", XT[:, :, t0:t0 + n], xt[:, :, :n], reads=[xt], writes=[XT])
        norm_tile(xt, n, 0, 0, wh, hT, t0)
    P.release(mk1)
    if stage == 1:
        o = k.out("o_h", [128, NCH * T], BF16)
        P.dma("sp", o[:], hT[:].rearrange("p a b -> p (a b)"), reads=[hT], writes=[o])
        o2 = k.out("o_xt", [128, NCH * T], F32)
        P.dma("sp", o2[:], XT[:].rearrange("p a b -> p (a b)"), reads=[XT], writes=[o2])
        P.finish([o, o2])
        return k

    mk2 = P.mark()
    stg_fm = [P.sb("stgfm%d" % i, [128, T], BF16) for i in range(2)]
    stg_tm = [P.sb("stgtm%d" % i, [128, 512], BF16) for i in range(3)]
    cnt = {"fm": 0, "tm": 0}

    def epi_fm0(pt, col0, t0, n):
        sg = stg_fm[(cnt["fm"] // len(TOKT)) % 2]
        cnt["fm"] += 1
        evac(sg[:, t0:t0 + n], pt[:, :n], [pt], [sg])
        if t0 + n == T:
            dstT, ch = (QT, col0 // 128) if col0 < 1024 else ((KT, (col0 - 1024) // 128) if col0 < 2048 else (UT, (col0 - 3072) // 128))
            P.dma("sp", dstT[:, ch, :], sg[:], reads=[sg], writes=[dstT])

    def epi_tm0(pt, col0, tok0):
        sg = stg_tm[cnt["tm"] % 3]
        cnt["tm"] += 1
        evac(sg[:], pt[:, :], [pt], [sg])
        P.dma("sp", VTOK[tok0:tok0 + 128, col0 - 2048:col0 - 2048 + 512], sg[:], reads=[sg], writes=[VTOK])

    linear(hT, abw_in, 4096, epi_fm=epi_fm0, epi_tm=epi_tm0,
           col_mode=lambda sidx: "tm" if 4 <= sidx < 6 else "fm")
    P.release(mk2)
    P.release(mk_h)
    if stage == 2:
        outs_ = []
        for nm_, tb_, shp in (("o_q", QT, [128, 8 * T]), ("o_k", KT, [128, 8 * T]), ("o_u", UT, [128, 8 * T])):
            o = k.out(nm_, shp, BF16)
            P.dma("sp", o[:], tb_[:].rearrange("p a b -> p (a b)"), reads=[tb_], writes=[o])
            outs_.append(o)
        o = k.out("o_v", [T, 1024], BF16)
        P.dma("sp", o[:], VTOK[:], reads=[VTOK], writes=[o])
        outs_.append(o)
        P.finish(outs_)
        return k

    mk3 = P.mark()
    qT = P.sb("qT", [128, 8, T], BF16)
    kT = P.sb("kT", [128, 8, T], BF16)
    vtok = P.sb("vtok", [128, 18, 1024], BF16)
    ET = P.sb("ET", [128, 37, 8, 64], BF16)
    P.dma("sp", qT[:], QT[:], reads=[QT], writes=[qT])
    P.dma("sp", kT[:], KT[:], reads=[KT], writes=[kT])
    P.dma("sp", vtok[:], VTOK[:].rearrange("(tt p) n -> p tt n", p=128), reads=[VTOK], writes=[vtok])
    mk3b = P.mark()
    tabf = [P.sb("tabf%d" % i, [128, 8, 8, 64], F32) for i in range(2)]
    for i, t0_ in enumerate(range(0, 37, 8)):
        nt_ = min(8, 37 - t0_)
        tb = tabf[i % 2]
        P.dma("sp", tb[:, :nt_], natab_in[:, t0_:t0_ + nt_], reads=[natab_in], writes=[tb])
        P.op("act", lambda e, tb=tb, t0_=t0_, nt_=nt_: e.activation(out=ET[:, t0_:t0_ + nt_], in_=tb[:, :nt_], func=AF.Exp),
             reads=[tb], writes=[ET])
    P.release(mk3b)
    PTb = [P.sb("PT%d" % i, [128, 7, 64], BF16) for i in range(16)]
    recb = [P.sb("rec%d" % i, [128, 512], F32) for i in range(2)]
    narow = [P.sb("narow%d" % i, [128, 8, 64], BF16) for i in range(2)]
    SCALE = 128.0 ** -0.5
    rows = [("c", i) for i in range(4)] + [("l", r) for r in range(32)]
    pti = 0
    for ri_, (kind, r) in enumerate(rows):
        if kind == "c":
            q0 = 64 * r
            wtiles = []
            tab0 = None
        else:
            q0 = TC + 64 * r
            rs = min(max(r - 4, 0), 24)
            o = r - rs
            if rs % 2 == 0:
                wtiles = [(TC + 64 * rs + 128 * i) for i in range(4)]
                tab0 = 4 * o
            else:
                wtiles = [(TC + 64 * (rs - 1) + 128 * i) for i in range(5)]
                tab0 = 32
        ktiles = wtiles + [0, 128]
        nw = len(wtiles)
        nk = len(ktiles)
        bankA = k.psum()
        bankB = k.psum()
        pts = []
        for h in range(8):
            pt = k.psum()
            for i, kt0 in enumerate(ktiles):
                P.op("pe", lambda e, pt=pt, i=i, kt0=kt0, h=h, q0=q0: e.matmul(
                    pt[:, i * 64:(i + 1) * 64], lhsT=kT[:, h, kt0:kt0 + 128], rhs=qT[:, h, q0:q0 + 64],
                    start=True, stop=True), reads=[kT, qT], writes=[pt], inc=(i == nk - 1))
            PT = PTb[pti % 16]
            pti += 1
            P.op("act", lambda e, pt=pt, PT=PT, nk=nk: e.activation(
                out=PT[:, :nk, :], in_=pt[:, :nk * 64].rearrange("p (a b) -> p a b", b=64), func=AF.Exp, scale=SCALE),
                reads=[pt], writes=[PT])
            if nw:
                P.op("dve", lambda e, PT=PT, nw=nw, tab0=tab0, h=h: e.tensor_tensor(
                    out=PT[:, :nw, :], in0=PT[:, :nw, :], in1=ET[:, tab0:tab0 + nw, h, :], op=ALU.mult),
                    reads=[PT, ET], writes=[PT])
            pts.append(PT)
        for h in range(8):
            PT = pts[h]
            for i, kt0 in enumerate(ktiles):
                P.op("pe", lambda e, PT=PT, i=i, kt0=kt0, h=h, bankA=bankA, nk=nk: e.matmul(
                    bankA[:, h * 64:(h + 1) * 64], lhsT=vtok[:, kt0 // 128, h * 128:(h + 1) * 128], rhs=PT[:, i, :],
                    start=(i == 0), stop=(i == nk - 1)), reads=[vtok, PT], writes=[bankA], inc=(i == nk - 1 and h == 7))
        for h in range(8):
            PT = pts[h]
            for i, kt0 in enumerate(ktiles):
                P.op("pe", lambda e, PT=PT, i=i, h=h, bankB=bankB, nk=nk: e.matmul(
                    bankB[:, h * 64:(h + 1) * 64], lhsT=ones_bf[:], rhs=PT[:, i, :],
                    start=(i == 0), stop=(i == nk - 1)), reads=[ones_bf, PT], writes=[bankB], inc=(i == nk - 1 and h == 7))
        rec = recb[ri_ % 2]
        P.op("dve", lambda e, rec=rec, bankB=bankB: e.reciprocal(out=rec[:], in_=bankB[:, :]), reads=[bankB], writes=[rec])
        nr = narow[ri_ % 2]
        P.op("dve", lambda e, rec=rec, bankA=bankA, nr=nr: e.tensor_tensor(
            out=nr[:], in0=bankA[:, :].rearrange("p (h q) -> p h q", q=64),
            in1=rec[:].rearrange("p (h q) -> p h q", q=64), op=ALU.mult), reads=[bankA, rec], writes=[nr])
        P.dma("sp", CAT[:, 0:8, q0:q0 + 64], nr[:], reads=[nr], writes=[CAT])
    P.release(mk3)
    if stage == 3:
        o = k.out("o_na", [128, 8 * T], BF16)
        P.dma("sp", o[:], CAT[:, 0:8, :], reads=[CAT], writes=[o])
        P.finish([o])
        return k

    YF = P.dram("YF", [128, 8, T], F32)
    YB = P.dram("YB", [128, 8, T], F32)
    mk4 = P.mark()
    Bm = P.sb("Bm", [128, 2, 2, 32, 128], BF16)
    P.dma("pool", Bm[:], s5B_in[:], reads=[s5B_in], writes=[Bm])
    Cb = P.sb("Cb", [128, 2, 2, 32, 128], BF16)
    A1 = P.sb("A1", [128, 2, 2, 32], F32)
    A2 = P.sb("A2", [128, 2, 2, 32], F32)
    mk4a = P.mark()
    lam = P.sb("lam", [128, 3, 64], F32)
    P.dma("sp", lam[:], s5lam_in[:], reads=[s5lam_in], writes=[lam])
    sc_ = {}

    def T64(nm):
        sc_[nm] = P.sb("s5_" + nm, [128, 64], F32)
        return sc_[nm]

    dt_ = T64("dt"); mag = T64("mag"); th = T64("th"); t1 = T64("t1"); t2 = T64("t2")
    sn = T64("sn"); cs = T64("cs"); are = T64("are"); aim = T64("aim"); den = T64("den")
    fre = T64("fre"); fim = T64("fim"); nfim = T64("nfim"); am1 = T64("am1"); nfre = T64("nfre")
    TWO_PI = float(2 * np.pi)
    MAGIC = 12582912.0

    def dv(fn, reads, writes):
        P.op("dve", fn, reads=reads, writes=writes)

    P.op("act", lambda e: e.activation(out=dt_[:], in_=lam[:, 2, :], func=AF.Exp), reads=[lam], writes=[dt_])
    dv(lambda e: e.tensor_tensor(out=t1[:], in0=lam[:, 0, :], in1=dt_[:], op=ALU.mult), [lam, dt_], [t1])
    P.op("act", lambda e: e.activation(out=mag[:], in_=t1[:], func=AF.Exp), reads=[t1], writes=[mag])
    dv(lambda e: e.tensor_tensor(out=th[:], in0=lam[:, 1, :], in1=dt_[:], op=ALU.mult), [lam, dt_], [th])

    def sin_of(dst, shift):
        dv(lambda e: e.tensor_scalar(out=t1[:], in0=th[:], scalar1=shift, scalar2=1.0 / TWO_PI, op0=ALU.add, op1=ALU.mult), [th], [t1])
        dv(lambda e: e.tensor_scalar(out=t2[:], in0=t1[:], scalar1=MAGIC, scalar2=None, op0=ALU.add), [t1], [t2])
        dv(lambda e: e.tensor_scalar(out=t2[:], in0=t2[:], scalar1=-MAGIC, scalar2=None, op0=ALU.add), [t2], [t2])
        dv(lambda e: e.tensor_tensor(out=t1[:], in0=t1[:], in1=t2[:], op=ALU.subtract), [t1, t2], [t1])
        dv(lambda e: e.tensor_scalar(out=t1[:], in0=t1[:], scalar1=TWO_PI, scalar2=3.1415925, op0=ALU.mult, op1=ALU.min), [t1], [t1])
        dv(lambda e: e.tensor_scalar(out=t1[:], in0=t1[:], scalar1=-3.1415925, scalar2=None, op0=ALU.max), [t1], [t1])
        P.op("act", lambda e: e.activation(out=dst[:], in_=t1[:], func=AF.Sin), reads=[t1], writes=[dst])

    sin_of(sn, 0.0)
    sin_of(cs, float(np.pi / 2))
    dv(lambda e: e.tensor_tensor(out=are[:], in0=mag[:], in1=cs[:], op=ALU.mult), [mag, cs], [are])
    dv(lambda e: e.tensor_tensor(out=aim[:], in0=mag[:], in1=sn[:], op=ALU.mult), [mag, sn], [aim])
    dv(lambda e: e.tensor_tensor(out=den[:], in0=lam[:, 0, :], in1=lam[:, 0, :], op=ALU.mult), [lam], [den])
    dv(lambda e: e.tensor_tensor(out=t1[:], in0=lam[:, 1, :], in1=lam[:, 1, :], op=ALU.mult), [lam], [t1])
    dv(lambda e: e.tensor_tensor(out=den[:], in0=den[:], in1=t1[:], op=ALU.add), [den, t1], [den])
    dv(lambda e: e.reciprocal(out=den[:], in_=den[:]), [den], [den])
    dv(lambda e: e.tensor_scalar(out=am1[:], in0=are[:], scalar1=-1.0, scalar2=None, op0=ALU.add), [are], [am1])
    dv(lambda e: e.tensor_tensor(out=t1[:], in0=am1[:], in1=lam[:, 0, :], op=ALU.mult), [am1, lam], [t1])
    dv(lambda e: e.tensor_tensor(out=t2[:], in0=aim[:], in1=lam[:, 1, :], op=ALU.mult), [aim, lam], [t2])
    dv(lambda e: e.tensor_tensor(out=t1[:], in0=t1[:], in1=t2[:], op=ALU.add), [t1, t2], [t1])
    dv(lambda e: e.tensor_tensor(out=fre[:], in0=t1[:], in1=den[:], op=ALU.mult), [t1, den], [fre])
    dv(lambda e: e.tensor_tensor(out=t1[:], in0=aim[:], in1=lam[:, 0, :], op=ALU.mult), [aim, lam], [t1])
    dv(lambda e: e.tensor_tensor(out=t2[:], in0=am1[:], in1=lam[:, 1, :], op=ALU.mult), [am1, lam], [t2])
    dv(lambda e: e.tensor_tensor(out=t1[:], in0=t1[:], in1=t2[:], op=ALU.subtract), [t1, t2], [t1])
    dv(lambda e: e.tensor_tensor(out=fim[:], in0=t1[:], in1=den[:], op=ALU.mult), [t1, den], [fim])
    dv(lambda e: e.tensor_scalar(out=nfim[:], in0=fim[:], scalar1=-1.0, scalar2=None, op0=ALU.mult), [fim], [nfim])
    dv(lambda e: e.tensor_scalar(out=nfre[:], in0=fre[:], scalar1=-1.0, scalar2=None, op0=ALU.mult), [fre], [nfre])
    for d_ in range(2):
        sl = slice(d_ * 32, d_ * 32 + 32)
        dv(lambda e, d_=d_, sl=sl: e.tensor_copy(out=A1[:, d_, 0, :], in_=are[:, sl]), [are], [A1])
        dv(lambda e, d_=d_, sl=sl: e.tensor_copy(out=A1[:, d_, 1, :], in_=aim[:, sl]), [aim], [A1])
        dv(lambda e, d_=d_, sl=sl: e.tensor_scalar(out=A2[:, d_, 0, :], in0=aim[:, sl], scalar1=-1.0, scalar2=None, op0=ALU.mult), [aim], [A2])
        dv(lambda e, d_=d_, sl=sl: e.tensor_copy(out=A2[:, d_, 1, :], in_=are[:, sl]), [are], [A2])
    Cf = P.sb("Cf", [128, 2, 32, 128], F32)
    ctmp = P.sb("ctmp", [128, 128], F32)
    for d_ in range(2):
        P.dma("sp", Cf[:], s5C_in[:, d_], reads=[s5C_in], writes=[Cf])
        for gp in range(32):
            col = d_ * 32 + gp
            dv(lambda e, gp=gp, col=col: e.tensor_scalar(out=ctmp[:], in0=Cf[:, 0, gp, :], scalar1=fre[:, col:col + 1], scalar2=None, op0=ALU.mult), [Cf, fre], [ctmp])
            dv(lambda e, gp=gp, col=col, d_=d_: e.scalar_tensor_tensor(out=Cb[:, d_, 0, gp, :], in0=Cf[:, 1, gp, :], scalar=nfim[:, col:col + 1], in1=ctmp[:], op0=ALU.mult, op1=ALU.add), [Cf, nfim, ctmp], [Cb])
            dv(lambda e, gp=gp, col=col: e.tensor_scalar(out=ctmp[:], in0=Cf[:, 0, gp, :], scalar1=nfim[:, col:col + 1], scalar2=None, op0=ALU.mult), [Cf, nfim], [ctmp])
            dv(lambda e, gp=gp, col=col, d_=d_: e.scalar_tensor_tensor(out=Cb[:, d_, 1, gp, :], in0=Cf[:, 1, gp, :], scalar=nfre[:, col:col + 1], in1=ctmp[:], op0=ALU.mult, op1=ALU.add), [Cf, nfre, ctmp], [Cb])
    P.release(mk4a)
    W = 32
    NW = T // W
    H = [[P.sb("H%d%d" % (d_, i), [128, 2, 32, W], F32) for i in range(2)] for d_ in range(2)]
    BU = [[P.sb("BU%d%d" % (d_, i), [128, 2, 32, W], F32) for i in range(2)] for d_ in range(2)]
    Hb = [[P.sb("Hb%d%d" % (d_, i), [128, 2, 32, W], BF16) for i in range(2)] for d_ in range(2)]
    uw = [[P.sb("uw%d%d" % (d_, i), [128, 8, W], BF16) for i in range(2)] for d_ in range(2)]
    ys = [[P.sb("ys%d%d" % (d_, i), [128, 8, W], F32) for i in range(2)] for d_ in range(2)]
    tm1 = [P.sb("tm1_%d" % d_, [128, 2, 32], F32) for d_ in range(2)]
    tm2 = [P.sb("tm2_%d" % d_, [128, 2, 32], F32) for d_ in range(2)]
    ENG = ["dve", "pool"]

    def win_tok0(d_, w):
        if d_ == 0:
            return w * W
        pos = T - (w + 1) * W
        return TC + pos if pos < TL else pos - TL

    for w in range(NW):
        par = w % 2
        for d_ in range(2):
            tok0 = win_tok0(d_, w)
            uwb = uw[d_][par]
            P.dma("sp", uwb[:], UT[:, :, tok0:tok0 + W], reads=[UT], writes=[uwb])
            bu = BU[d_][par]
            for reim in range(2):
                for half in range(2):
                    pt = k.psum()
                    for g16 in range(16):
                        gp = half * 16 + g16
                        P.op("pe", lambda e, pt=pt, g16=g16, gp=gp, d_=d_, reim=reim, uwb=uwb: e.matmul(
                            pt[:, g16 * W:(g16 + 1) * W], lhsT=Bm[:, d_, reim, gp, :], rhs=uwb[:, gp // 4, :],
                            start=True, stop=True), reads=[Bm, uwb], writes=[pt], inc=(g16 == 15))
                    P.op("act", lambda e, pt=pt, bu=bu, reim=reim, half=half: e.copy(
                        out=bu[:, reim, half * 16:(half + 1) * 16, :],
                        in_=pt[:, :16 * W].rearrange("p (g w) -> p g w", w=W)), reads=[pt], writes=[bu])
        for j in range(W):
            ops = [[], []]
            for d_ in range(2):
                Hc = H[d_][par]
                Hp = H[d_][1 - par]
                bu = BU[d_][par]
                a1, a2 = tm1[d_], tm2[d_]
                c = j if d_ == 0 else W - 1 - j
                if j == 0:
                    Hprev, cp = Hp, (W - 1 if d_ == 0 else 0)
                else:
                    Hprev, cp = Hc, (c - 1 if d_ == 0 else c + 1)
                L = ops[d_]
                if w == 0 and j == 0:
                    L.append((lambda e, Hc=Hc, bu=bu, c=c: e.tensor_copy(out=Hc[:, :, :, c], in_=bu[:, :, :, c]), [bu], [Hc]))
                else:
                    L.append((lambda e, Hprev=Hprev, cp=cp, a1=a1, d_=d_: e.tensor_tensor(
                        out=a1[:], in0=A1[:, d_], in1=Hprev[:, 0:1, :, cp].to_broadcast([128, 2, 32]), op=ALU.mult), [A1, Hprev], [a1]))
                    L.append((lambda e, Hprev=Hprev, cp=cp, a2=a2, d_=d_: e.tensor_tensor(
                        out=a2[:], in0=A2[:, d_], in1=Hprev[:, 1:2, :, cp].to_broadcast([128, 2, 32]), op=ALU.mult), [A2, Hprev], [a2]))
                    L.append((lambda e, a1=a1, a2=a2: e.tensor_tensor(out=a1[:], in0=a1[:], in1=a2[:], op=ALU.add), [a1, a2], [a1]))
                    L.append((lambda e, Hc=Hc, bu=bu, c=c, a1=a1: e.tensor_tensor(
                        out=Hc[:, :, :, c], in0=a1[:], in1=bu[:, :, :, c], op=ALU.add), [a1, bu], [Hc]))
            for i in range(max(len(ops[0]), len(ops[1]))):
                for d_ in range(2):
                    if i < len(ops[d_]):
                        fn, rd, wr = ops[d_][i]
                        P.op("dve", fn, reads=rd, writes=wr, nosame=True)
        for d_ in range(2):
            tok0 = win_tok0(d_, w)
            Hc = H[d_][par]
            hb = Hb[d_][par]
            P.op("act", lambda e, hb=hb, Hc=Hc: e.copy(out=hb[:], in_=Hc[:]), reads=[Hc], writes=[hb])
            pt = k.psum()
            for ch in range(8):
                for i4 in range(4):
                    gp = ch * 4 + i4
                    for reim in range(2):
                        first = (i4 == 0 and reim == 0)
                        lastm = (i4 == 3 and reim == 1)
                        P.op("pe", lambda e, pt=pt, ch=ch, gp=gp, reim=reim, d_=d_, hb=hb, first=first, lastm=lastm: e.matmul(
                            pt[:, ch * W:(ch + 1) * W], lhsT=Cb[:, d_, reim, gp, :], rhs=hb[:, reim, gp, :],
                            start=first, stop=lastm), reads=[Cb, hb], writes=[pt], inc=(lastm and ch == 7))
            ysb = ys[d_][par]
            P.op("act", lambda e, pt=pt, ysb=ysb: e.copy(out=ysb[:], in_=pt[:, :8 * W].rearrange("p (c w) -> p c w", w=W)),
                 reads=[pt], writes=[ysb])
            Yd = YF if d_ == 0 else YB
            P.dma("sp", Yd[:, :, tok0:tok0 + W], ysb[:], reads=[ysb], writes=[Yd])
    P.release(mk4)
    if stage == 4:
        o1 = k.out("o_yf", [128, 8 * T], F32)
        P.dma("sp", o1[:], YF[:].rearrange("p a b -> p (a b)"), reads=[YF], writes=[o1])
        o2 = k.out("o_yb", [128, 8 * T], F32)
        P.dma("sp", o2[:], YB[:].rearrange("p a b -> p (a b)"), reads=[YB], writes=[o2])
        P.finish([o1, o2])
        return k

    mk5 = P.mark()
    dg = P.sb("dg", [128, 2, 8], F32)
    P.dma("sp", dg[:], s5d_in[:], reads=[s5d_in], writes=[dg])
    glT = P.sb("glT", [128, 8, T], BF16)
    mk5a = P.mark()
    yft = [P.sb("yft%d" % i, [128, T], F32) for i in range(2)]
    ybt = [P.sb("ybt%d" % i, [128, T], F32) for i in range(2)]
    ut = [P.sb("ut%d" % i, [128, T], BF16) for i in range(2)]
    for ch in range(8):
        a_, b_, u_ = yft[ch % 2], ybt[ch % 2], ut[ch % 2]
        P.dma("sp", a_[:], YF[:, ch, :], reads=[YF], writes=[a_])
        P.dma("sp", b_[:], YB[:, ch, :], reads=[YB], writes=[b_])
        P.dma("sp", u_[:], UT[:, ch, :], reads=[UT], writes=[u_])
        P.op("dve", lambda e, a_=a_, b_=b_: e.tensor_tensor(out=a_[:], in0=a_[:], in1=b_[:], op=ALU.add),
             reads=[a_, b_], writes=[a_])
        P.op("dve", lambda e, a_=a_, u_=u_, ch=ch: e.scalar_tensor_tensor(
            out=a_[:], in0=u_[:], scalar=dg[:, 0, ch:ch + 1], in1=a_[:], op0=ALU.mult, op1=ALU.add),
            reads=[a_, u_, dg], writes=[a_])
        P.op("act", lambda e, a_=a_, ch=ch: e.activation(out=glT[:, ch, :], in_=a_[:], func=AF.Gelu),
             reads=[a_], writes=[glT])
    P.release(mk5a)
    stg5 = [P.sb("stg5_%d" % i, [128, T], BF16) for i in range(2)]
    sgt = [P.sb("sgt%d" % i, [128, 512], BF16) for i in range(2)]
    c5 = {"n": 0}

    def epi_glu(pt, col0, t0, n):
        ch = col0 // 128
        sg = sgt[c5["n"] % 2]
        st = stg5[(c5["n"] // len(TOKT)) % 2]
        c5["n"] += 1
        P.op("act", lambda e: e.activation(out=sg[:, :n], in_=pt[:, :n], func=AF.Sigmoid, bias=dg[:, 1, ch:ch + 1], scale=1.0),
             reads=[pt, dg], writes=[sg])
        P.op("dve", lambda e: e.tensor_tensor(out=st[:, t0:t0 + n], in0=glT[:, ch, t0:t0 + n], in1=sg[:, :n], op=ALU.mult),
             reads=[glT, sg], writes=[st])
        if t0 + n == T:
            P.dma("sp", CAT[:, 8 + ch, :], st[:], reads=[st], writes=[CAT])

    linear(glT, glu_w, 1024, epi_fm=epi_glu, nkc=8)
    P.release(mk5)
    if stage == 5:
        o = k.out("o_cat", [128, NCH * T], BF16)
        P.dma("sp", o[:], CAT[:].rearrange("p a b -> p (a b)"), reads=[CAT], writes=[o])
        P.finish([o])
        return k

    def out_proj_residual(Wout, layer, toks):
        mk = P.mark()
        catT = P.sb("catT", [128, NCH, T], BF16)
        P.dma("sp", catT[:], CAT[:], reads=[CAT], writes=[catT])
        xrow = [P.sb("xrow%d" % i, [128, T], F32) for i in range(2)]
        cnt_ = {"n": 0}
        tlo = toks[0][0]
        thi = toks[-1][0] + toks[-1][1]
        whmap = {t0: wh for (t0, n, wh) in toks}

        def epi(pt, col0, t0, n):
            ch = col0 // 128
            xr = xrow[(cnt_["n"] // len(toks)) % 2]
            if cnt_["n"] % len(toks) == 0:
                P.dma("sp", xr[:, tlo:thi], XT[:, ch, tlo:thi], reads=[XT], writes=[xr])
            cnt_["n"] += 1
            wh = whmap[t0]
            P.op("dve", lambda e: e.scalar_tensor_tensor(
                out=xr[:, t0:t0 + n], in0=pt[:, :n], scalar=gate_ap(layer, 0, wh, ch), in1=xr[:, t0:t0 + n],
                op0=ALU.mult, op1=ALU.add), reads=[pt, mod, xr], writes=[xr])
            if t0 + n == thi:
                P.dma("sp", XT[:, ch, tlo:thi], xr[:, tlo:thi], reads=[xr], writes=[XT])

        linear(catT, Wout, D, epi_fm=epi, toks=[(t0, n) for (t0, n, wh) in toks])
        P.release(mk)

    out_proj_residual(abw_out, 0, [(0, 256, 1), (256, 512, 0), (768, 512, 0), (1280, 512, 0), (1792, 512, 0)])
    if stage == 6:
        o = k.out("o_xt", [128, NCH * T], F32)
        P.dma("sp", o[:], XT[:].rearrange("p a b -> p (a b)"), reads=[XT], writes=[o])
        P.finish([o])
        return k

    def mlp(layer, blocks):
        def do_block(blk):
            b0 = blk[0][0]
            nt = sum(n for (_, n, _) in blk)
            mk = P.mark()
            h2 = P.sb("h2", [128, NCH, nt], BF16)
            hid = P.sb("hid", [128, 64, nt], BF16)
            mkn = P.mark()
            alloc_norm_scratch()
            ti = 0
            for (t0, n, wh) in blk:
                for s0 in range(0, n, 256):
                    xt = NS.xt_tiles[ti % 2]
                    ti += 1
                    P.dma("sp", xt[:, :, :256], XT[:, :, t0 + s0:t0 + s0 + 256], reads=[XT], writes=[xt])
                    norm_tile(xt, 256, layer, 1, wh, h2, t0 + s0 - b0)
            P.release(mkn)
            mkw1 = P.mark()
            w1s = [P.sb("w1s%d" % i, [128, NCH, 256], BF16) for i in range(2)]
            w1st = [P.sb("w1st%d" % i, [128, NCH, 256], F32) for i in range(2)]
            rl = [P.sb("rl%d" % i, [128, 512], BF16) for i in range(2)]
            ri = 0
            for sidx in range(4 * D // 256):
                wb = w1s[sidx % 2]
                wload(wb, mlp_w1[layer, :, sidx * 256:(sidx + 1) * 256], 256, NCH, w1st, via_hw=(sidx % 4 != 3))
                for j in range(2):
                    hc_ = sidx * 2 + j
                    for (t0, n, wh) in blk:
                        pt = k.psum()
                        for kc in range(NCH):
                            P.op("pe", lambda e, kc=kc, j=j, t0=t0, n=n, pt=pt, wb=wb: e.matmul(
                                pt[:, :n], lhsT=wb[:, kc, j * 128:(j + 1) * 128], rhs=h2[:, kc, t0 - b0:t0 - b0 + n],
                                start=(kc == 0), stop=(kc == NCH - 1)), reads=[wb, h2], writes=[pt], inc=(kc == NCH - 1))
                        r_ = rl[ri % 2]
                        P.op("act", lambda e, pt=pt, r_=r_, n=n: e.activation(out=r_[:, :n], in_=pt[:, :n], func=AF.Relu),
                             reads=[pt], writes=[r_])
                        P.op("dve", lambda e, r_=r_, n=n, hc_=hc_, t0=t0: e.tensor_tensor(
                            out=hid[:, hc_, t0 - b0:t0 - b0 + n], in0=r_[:, :n], in1=r_[:, :n], op=ALU.mult),
                            reads=[r_], writes=[hid])
                        ri += 1
            P.release(mkw1)
            w2s = [P.sb("w2s%d" % i, [128, 64, 128], BF16) for i in range(2)]
            xr2 = [P.sb("xr2_%d" % i, [128, nt], F32) for i in range(2)]
            for ch in range(NCH):
                wb = w2s[ch % 2]
                P.dma("pool", wb[:], mlp_w2[layer, :, ch * 128:(ch + 1) * 128].rearrange("(kc p) n -> p kc n", p=128),
                      reads=[], writes=[wb])
                xr = xr2[ch % 2]
                P.dma("sp", xr[:], XT[:, ch, b0:b0 + nt], reads=[XT], writes=[xr])
                for (t0, n, wh) in blk:
                    pt = k.psum()
                    for kc in range(64):
                        P.op("pe", lambda e, kc=kc, t0=t0, n=n, pt=pt, wb=wb: e.matmul(
                            pt[:, :n], lhsT=wb[:, kc, :], rhs=hid[:, kc, t0 - b0:t0 - b0 + n],
                            start=(kc == 0), stop=(kc == 63)), reads=[wb, hid], writes=[pt], inc=(kc == 63))
                    P.op("dve", lambda e, pt=pt, xr=xr, t0=t0, n=n, wh=wh, ch=ch: e.scalar_tensor_tensor(
                        out=xr[:, t0 - b0:t0 - b0 + n], in0=pt[:, :n], scalar=gate_ap(layer, 1, wh, ch),
                        in1=xr[:, t0 - b0:t0 - b0 + n], op0=ALU.mult, op1=ALU.add), reads=[pt, mod, xr], writes=[xr])
                P.dma("sp", XT[:, ch, b0:b0 + nt], xr[:], reads=[xr], writes=[XT])
            P.release(mk)

        for blk_ in blocks:
            do_block(blk_)

    mlp(0, [[(0, 256, 1), (256, 512, 0)], [(768, 512, 0), (1280, 256, 0)], [(1536, 512, 0), (2048, 256, 0)]])
    if stage == 7:
        o = k.out("o_xt", [128, NCH * T], F32)
        P.dma("sp", o[:], XT[:].rearrange("p a b -> p (a b)"), reads=[XT], writes=[o])
        P.finish([o])
        return k

    VT2 = P.dram("VT2", [T, 2048], BF16)
    GT2 = P.dram("GT2", [T, 2048], BF16)
    ATd = P.dram("ATd", [64, T], BF16)
    OF = P.dram("OF", [TL, 2048], F32)
    OB = P.dram("OB", [TL, 2048], F32)
    mk8 = P.mark()
    hT = P.sb("hT1", [128, NCH, T], BF16)
    mkn = P.mark()
    alloc_norm_scratch()
    for ti, (t0, n, wh) in enumerate(TT):
        xt = NS.xt_tiles[ti % 2]
        P.dma("sp", xt[:, :, :n], XT[:, :, t0:t0 + n], reads=[XT], writes=[xt])
        norm_tile(xt, n, 1, 0, wh, hT, t0)
    P.release(mkn)
    rope = P.sb("rope", [128, 2, 2, TL], F32)
    P.dma("sp", rope[:], rope_in[:], reads=[rope_in], writes=[rope])
    pre = P.sb("pre", [128, T], F32)
    rtmp = [P.sb("rtmp%d" % i, [128, 512], F32) for i in range(2)]
    stg8 = [P.sb("stg8_%d" % i, [128, T], BF16) for i in range(2)]
    stg8t = [P.sb("stg8t%d" % i, [128, 512], BF16) for i in range(3)]
    c8 = {"fm": 0, "tm": 0}

    def epi_fm8(pt, col0, t0, n):
        c = col0 // 128
        if c >= 32:
            if col0 == 8192:
                sg = stg8[0]
                evac(sg[:64, t0:t0 + n], pt[:64, :n], [pt], [sg])
                if t0 + n == T:
                    P.dma("sp", ATd[:, :], sg[:64, :], reads=[sg], writes=[ATd])
            return
        isk = c >= 16
        cc_ = (c % 16) // 2
        primed = c % 2 == 1
        qscale = 1.0 if isk else 1.0 / 16.0
        if not primed:
            P.op("act", lambda e: e.mul(out=pre[:, t0:t0 + n], in_=pt[:, :n], mul=qscale), reads=[pt], writes=[pre])
            return
        sg = stg8[cc_ % 2]
        if t0 < TC:
            P.op("act", lambda e: e.copy(out=sg[:, t0:t0 + n], in_=pre[:, t0:t0 + n]), reads=[pre], writes=[sg])
        else:
            rc = cc_ % 2
            l0 = t0 - TC
            rt = rtmp[c8["fm"] % 2]
            c8["fm"] += 1
            P.op("dve", lambda e: e.tensor_tensor(out=rt[:, :n], in0=pt[:, :n], in1=rope[:, 1, rc, l0:l0 + n], op=ALU.mult),
                 reads=[pt, rope], writes=[rt])
            P.op("pool", lambda e: e.tensor_tensor(out=pre[:, t0:t0 + n], in0=pre[:, t0:t0 + n], in1=rope[:, 0, rc, l0:l0 + n], op=ALU.mult),
                 reads=[pre, rope], writes=[pre])
            P.op("dve", lambda e: e.scalar_tensor_tensor(out=sg[:, t0:t0 + n], in0=rt[:, :n], scalar=qscale, in1=pre[:, t0:t0 + n],
                                                         op0=ALU.mult, op1=ALU.add), reads=[rt, pre], writes=[sg])
        if t0 + n == T:
            dst = KT if isk else QT
            P.dma("sp", dst[:, cc_, :], sg[:], reads=[sg], writes=[dst])

    def epi_tm8(pt, col0, tok0):
        sg = stg8t[c8["tm"] % 3]
        c8["tm"] += 1
        evac(sg[:], pt[:, :], [pt], [sg])
        if col0 < 4096 + 2048:
            P.dma("sp", VT2[tok0:tok0 + 128, col0 - 4096:col0 - 4096 + 512], sg[:], reads=[sg], writes=[VT2])
        else:
            P.dma("sp", GT2[tok0:tok0 + 128, col0 - 6144:col0 - 6144 + 512], sg[:], reads=[sg], writes=[GT2])

    linear(hT, gla_wcat, 8704, epi_fm=epi_fm8, epi_tm=epi_tm8,
           col_mode=lambda sidx: "tm" if 8 <= sidx < 16 else "fm")
    P.release(mk8)
    if stage == 8:
        outs_ = []
        for nm_, tb_, shp in (("o_q", QT, [128, 8 * T]), ("o_k", KT, [128, 8 * T])):
            o = k.out(nm_, shp, BF16)
            P.dma("sp", o[:], tb_[:].rearrange("p a b -> p (a b)"), reads=[tb_], writes=[o])
            outs_.append(o)
        for nm_, tb_, shp in (("o_v", VT2, [T, 2048]), ("o_g", GT2, [T, 2048]), ("o_a", ATd, [64, T])):
            o = k.out(nm_, shp, BF16)
            P.dma("sp", o[:], tb_[:], reads=[tb_], writes=[o])
            outs_.append(o)
        P.finish(outs_)
        return k

    P.skip = False
    if stage >= 100:
        P.op("dve", lambda e: e.memset(epsb[:], EPS), writes=[epsb])
        QT = k.inp("QT_in", [128, 8, T], BF16)
        KT = k.inp("KT_in", [128, 8, T], BF16)
        VT2 = k.inp("VT2_in", [T, 2048], BF16)
        ATd = k.inp("ATd_in", [64, T], BF16)
    mk9 = P.mark()
    gqT = P.sb("gqT", [128, 8, T], BF16)
    gkT = P.sb("gkT", [128, 8, T], BF16)
    gaT = P.sb("gaT", [64, T], BF16)
    P.dma("sp", gqT[:], QT[:], reads=[QT], writes=[gqT])
    P.dma("sp", gkT[:], KT[:], reads=[KT], writes=[gkT])
    P.op("dve", lambda e: e.memset(gaT[:], 1.0), writes=[gaT])
    P.dma("sp", gaT[0:16, :], ATd[0:16, :], reads=[ATd], writes=[gaT])
    P.dma("sp", gaT[32:48, :], ATd[32:48, :], reads=[ATd], writes=[gaT])
    wa2 = P.sb("wa2", [64, 1024], BF16)
    P.dma("pool", wa2[:], wa2_in[:], reads=[wa2_in], writes=[wa2])
    tri = P.sb("tri", [128, 2, 128], F32)
    P.dma("sp", tri[:], tri_in[:], reads=[tri_in], writes=[tri])
    S32 = [P.sb("S32_%d" % i, [128, 8, 512], F32) for i in range(2)]
    Sbf = [P.sb("Sbf_%d" % i, [128, 8, 512], BF16) for i in range(2)]
    for i in range(2):
        P.op("pool", lambda e, i=i: e.memset(S32[i][:], 0.0), writes=[S32[i]])
        P.op("pool", lambda e, i=i: e.memset(Sbf[i][:], 0.0), writes=[Sbf[i]])
    vtc = [P.sb("vtc%d" % i, [128, 2048], BF16) for i in range(3)]
    lap = [P.sb("lap%d" % i, [128, 1024], F32) for i in range(2)]
    e1 = [P.sb("e1_%d" % i, [128, 512], F32) for i in range(2)]
    eq4 = [P.sb("eq4_%d" % i, [128, 512], F32) for i in range(2)]
    ek4 = [P.sb("ek4_%d" % i, [128, 512], F32) for i in range(2)]
    qin = [P.sb("qin%d" % i, [128, 8, 128], BF16) for i in range(2)]
    kin = [P.sb("kin%d" % i, [128, 8, 128], BF16) for i in range(2)]
    kintok = [P.sb("kintok%d" % i, [128, 1024], BF16) for i in range(2)]
    ATb = [P.sb("ATb%d" % i, [128, 4, 128], BF16) for i in range(2)]
    ostg = [P.sb("ostg%d" % i, [128, 2048], F32) for i in range(2)]
    decb = [P.sb("dec%d" % i, [128, 8], F32) for i in range(2)]
    psbf = k.psbf
    seqs = [list(range(18)), [1, 0] + list(range(17, 1, -1))]
    un = 0

    def gla_unit(d_, ci, un):
        tok0 = 128 * ci
        is_lat = ci >= 2
        base = 32 * d_
        last = 127 if d_ == 0 else 0
        v_ = vtc[un % 3]
        P.dma("sp", v_[:], VT2[tok0:tok0 + 128, :], reads=[VT2], writes=[v_])
        la_ = lap[un % 2]
        for half in range(2):
            pz = k.psum()
            P.op("pe", lambda e, pz=pz, half=half: e.matmul(pz[:, :], lhsT=gaT[base:base + 17, tok0:tok0 + 128],
                                          rhs=wa2[base:base + 17, half * 512:(half + 1) * 512], start=True, stop=True),
                 reads=[gaT, wa2], writes=[pz])
            e_ = e1[half]
            P.op("act", lambda e, pz=pz, e_=e_: e.activation(out=e_[:], in_=pz[:, :], func=AF.Exp, scale=-1.0), reads=[pz], writes=[e_])
            P.op("act", lambda e, e_=e_, half=half: e.activation(out=la_[:, half * 512:(half + 1) * 512], in_=e_[:], func=AF.Ln, bias=1.0, scale=1.0),
                 reads=[e_], writes=[la_])
        qi_, ki_ = qin[un % 2], kin[un % 2]
        dc_ = decb[un % 2]
        for bnk in range(2):
            pc = k.psum()
            for c4 in range(4):
                c = bnk * 4 + c4
                P.op("pe", lambda e, c=c, c4=c4, pc=pc: e.matmul(pc[:, c4 * 128:(c4 + 1) * 128], lhsT=la_[:, c * 128:(c + 1) * 128],
                                                          rhs=tri[:, d_, :], start=True, stop=True),
                     reads=[la_, tri], writes=[pc], inc=(c4 == 3))
            eq_, ek_ = eq4[bnk], ek4[bnk]
            P.op("act", lambda e, pc=pc, eq_=eq_: e.activation(out=eq_[:], in_=pc[:, :], func=AF.Exp, scale=-1.0 / 16.0), reads=[pc], writes=[eq_])
            P.op("act", lambda e, pc=pc, ek_=ek_: e.activation(out=ek_[:], in_=pc[:, :], func=AF.Exp, scale=1.0 / 16.0), reads=[pc], writes=[ek_])
            if is_lat:
                P.op("dve", lambda e, bnk=bnk, eq_=eq_: e.tensor_tensor(out=qi_[:, bnk * 4:bnk * 4 + 4, :], in0=gqT[:, bnk * 4:bnk * 4 + 4, tok0:tok0 + 128],
                                                      in1=eq_[:].rearrange("p (c t) -> p c t", t=128), op=ALU.mult),
                     reads=[gqT, eq_], writes=[qi_])
            P.op("dve", lambda e, bnk=bnk, ek_=ek_: e.tensor_tensor(out=ki_[:, bnk * 4:bnk * 4 + 4, :], in0=gkT[:, bnk * 4:bnk * 4 + 4, tok0:tok0 + 128],
                                                  in1=ek_[:].rearrange("p (c t) -> p c t", t=128), op=ALU.mult),
                 reads=[gkT, ek_], writes=[ki_])
            P.op("dve", lambda e, bnk=bnk, eq_=eq_: e.tensor_copy(out=dc_[:, bnk * 4:bnk * 4 + 4],
                                                in_=eq_[:].rearrange("p (c t) -> p c t", t=128)[:, :, last]),
                 reads=[eq_], writes=[dc_])
        kt_ = kintok[un % 2]
        for c in range(8):
            P.op("pe", lambda e, c=c: e.transpose(out=psbf[:, c * 128:(c + 1) * 128], in_=ki_[:, c, :], identity=identb[:]),
                 reads=[ki_, identb], writes=[psbf], inc=(c == 7))
        P.op("act", lambda e: e.copy(out=kt_[:], in_=psbf[:, :]), reads=[psbf], writes=[kt_])
        S32_, Sbf_ = S32[d_], Sbf[d_]
        if is_lat:
            pa = k.psum()
            for h in range(4):
                for dc in range(2):
                    P.op("pe", lambda e, h=h, dc=dc: e.matmul(pa[:, h * 128:(h + 1) * 128], lhsT=ki_[:, 2 * h + dc, :],
                                                              rhs=qi_[:, 2 * h + dc, :], start=(dc == 0), stop=(dc == 1)),
                         reads=[ki_, qi_], writes=[pa], inc=(h == 3 and dc == 1))
            at_ = ATb[un % 2]
            P.op("dve", lambda e: e.tensor_tensor(out=at_[:], in0=pa[:, :].rearrange("p (h t) -> p h t", t=128),
                                                  in1=tri[:, d_:d_ + 1, :].to_broadcast([128, 4, 128]), op=ALU.mult),
                 reads=[pa, tri], writes=[at_])
            os_ = ostg[un % 2]
            for h in range(4):
                po = k.psum()
                P.op("pe", lambda e, h=h, po=po: e.matmul(po[:, :], lhsT=at_[:, h, :], rhs=v_[:, h * 512:(h + 1) * 512],
                                                          start=True, stop=False), reads=[at_, v_], writes=[po], inc=False)
                for dc in range(2):
                    P.op("pe", lambda e, h=h, dc=dc, po=po: e.matmul(po[:, :], lhsT=qi_[:, 2 * h + dc, :], rhs=Sbf_[:, 2 * h + dc, :],
                                                                     start=False, stop=(dc == 1)),
                         reads=[qi_, Sbf_], writes=[po], inc=(dc == 1))
                evac(os_[:, h * 512:(h + 1) * 512], po[:, :], [po], [os_])
            Od = OF if d_ == 0 else OB
            P.dma("sp", Od[tok0 - TC:tok0 - TC + 128, :], os_[:], reads=[os_], writes=[Od])
        for c in range(8):
            h = c // 2
            pS = k.psum()
            P.op("pe", lambda e, c=c, h=h, pS=pS: e.matmul(pS[:, :], lhsT=kt_[:, c * 128:(c + 1) * 128], rhs=v_[:, h * 512:(h + 1) * 512],
                                                          start=True, stop=True), reads=[kt_, v_], writes=[pS])
            P.op("act", lambda e, c=c: e.activation(out=S32_[:, c, :], in_=S32_[:, c, :], func=AF.Copy, scale=dc_[:, c:c + 1]),
                 reads=[S32_, dc_], writes=[S32_])
            P.op("dve", lambda e, c=c, pS=pS: e.scalar_tensor_tensor(out=S32_[:, c, :], in0=pS[:, :], scalar=dc_[:, c:c + 1],
                                                                    in1=S32_[:, c, :], op0=ALU.mult, op1=ALU.add),
                 reads=[pS, dc_, S32_], writes=[S32_])
            P.op("act", lambda e, c=c: e.copy(out=Sbf_[:, c, :], in_=S32_[:, c, :]), reads=[S32_], writes=[Sbf_])

    for step in range(18):
        for d_ in range(2):
            gla_unit(d_, seqs[d_][step], un)
            un += 1
    P.release(mk9)
    if stage % 100 == 9:
        o1 = k.out("o_of", [TL, 2048], F32)
        P.dma("sp", o1[:], OF[:], reads=[OF], writes=[o1])
        o2 = k.out("o_ob", [TL, 2048], F32)
        P.dma("sp", o2[:], OB[:], reads=[OB], writes=[o2])
        P.finish([o1, o2])
        return k

    if stage >= 100:
        OF = k.inp("OF_in", [TL, 2048], F32)
        OB = k.inp("OB_in", [TL, 2048], F32)
        GT2 = k.inp("GT2_in", [T, 2048], BF16)
    mk10 = P.mark()
    ngb = P.sb("ngb", [128, 512], F32)
    P.dma("sp", ngb[:], gng_in[:], reads=[gng_in], writes=[ngb])
    oft = [P.sb("oft%d" % i, [128, 2048], F32) for i in range(2)]
    obt = [P.sb("obt%d" % i, [128, 2048], F32) for i in range(2)]
    gtt = [P.sb("gtt%d" % i, [128, 2048], BF16) for i in range(2)]
    sgl = [P.sb("sgl%d" % i, [128, 2048], BF16) for i in range(2)]
    sqj = P.sb("sqj", [128, 512], BF16)
    ssq = [P.sb("ssq%d" % i, [128, 4], F32) for i in range(2)]
    ytk = [P.sb("ytk%d" % i, [128, 2048], BF16) for i in range(2)]
    cst = [P.sb("cst%d" % i, [128, NCH, 128], BF16) for i in range(2)]
    psbf = k.psbf

    def fin_chunk(ci):
        r0 = 128 * ci
        a_, b_, g_, s_, q_, y_, c_ = oft[ci % 2], obt[ci % 2], gtt[ci % 2], sgl[ci % 2], ssq[ci % 2], ytk[ci % 2], cst[ci % 2]
        P.dma("sp", a_[:], OF[r0:r0 + 128, :], reads=[OF], writes=[a_])
        P.dma("sp", b_[:], OB[r0:r0 + 128, :], reads=[OB], writes=[b_])
        P.dma("sp", g_[:], GT2[TC + r0:TC + r0 + 128, :], reads=[GT2], writes=[g_])
        P.op("pool", lambda e: e.tensor_tensor(out=a_[:], in0=a_[:], in1=b_[:], op=ALU.add), reads=[a_, b_], writes=[a_])
        P.op("act", lambda e: e.activation(out=s_[:], in_=g_[:], func=AF.Silu), reads=[g_], writes=[s_])
        for h in range(4):
            P.op("act", lambda e, h=h: e.activation(out=sqj[:], in_=a_[:, h * 512:(h + 1) * 512], func=AF.Square,
                                                    accum_out=q_[:, h:h + 1]), reads=[a_], writes=[sqj, q_])
        P.op("act", lambda e: e.activation(out=q_[:], in_=q_[:], func=AF.Sqrt, bias=epsb[:, 0:1], scale=1.0 / 512.0),
             reads=[q_, epsb], writes=[q_])
        P.op("dve", lambda e: e.reciprocal(out=q_[:], in_=q_[:]), reads=[q_], writes=[q_])
        for h in range(4):
            P.op("dve", lambda e, h=h: e.scalar_tensor_tensor(out=a_[:, h * 512:(h + 1) * 512], in0=a_[:, h * 512:(h + 1) * 512],
                                                              scalar=q_[:, h:h + 1], in1=ngb[:], op0=ALU.mult, op1=ALU.mult),
                 reads=[a_, q_, ngb], writes=[a_])
        P.op("dve", lambda e: e.tensor_tensor(out=y_[:], in0=a_[:], in1=s_[:], op=ALU.mult), reads=[a_, s_], writes=[y_])
        for half in range(2):
            for c8_ in range(8):
                c = half * 8 + c8_
                P.op("pe", lambda e, c=c, c8_=c8_: e.transpose(out=psbf[:, c8_ * 128:(c8_ + 1) * 128], in_=y_[:, c * 128:(c + 1) * 128],
                                                              identity=identb[:]), reads=[y_, identb], writes=[psbf], inc=(c8_ == 7))
            P.op("act", lambda e, half=half: e.copy(out=c_[:, half * 8:half * 8 + 8, :],
                                                    in_=psbf[:, :].rearrange("p (c t) -> p c t", t=128)), reads=[psbf], writes=[c_])
        P.dma("sp", CAT[:, :, TC + r0:TC + r0 + 128], c_[:], reads=[c_], writes=[CAT])

    for ci in range(16):
        fin_chunk(ci)
    P.release(mk10)
    if stage % 100 == 10:
        o = k.out("o_cat1", [128, NCH * TL], BF16)
        P.dma("sp", o[:].rearrange("p (a b) -> p a b", a=NCH), CAT[:, :, TC:], reads=[CAT], writes=[o])
        P.finish([o])
        return k

    LAT5 = [(256, 512, 0), (768, 512, 0), (1280, 512, 0), (1792, 512, 0)]
    out_proj_residual(glaw_out, 1, LAT5)
    mlp(1, [[(256, 512, 0), (768, 256, 0)], [(1024, 512, 0), (1536, 256, 0)], [(1792, 512, 0)]])

    mkf = P.mark()
    alloc_norm_scratch()
    xt_tiles = NS.xt_tiles
    out = k.out("out", [TL, D])
    ynf = P.sb("ynf", [128, NCH, 256], F32)
    ytok = [P.sb("ytok%d" % i, [128, D], F32) for i in range(2)]
    yi = 0
    for ti, (t0, n, wh) in enumerate(TT):
        if wh == 1:
            continue
        xt = xt_tiles[ti % 2]
        P.dma("sp", xt[:, :, :n], XT[:, :, t0:t0 + n], reads=[XT], writes=[xt])
        rs = rstd_tile(xt, n)
        for c in range(NCH):
            P.op("dve", lambda e, c=c, xt=xt, rs=rs: e.scalar_tensor_tensor(
                out=ynf[:, c, :n], in0=xt[:, c, :n], scalar=fg[:, c:c + 1], in1=rs[:, :n],
                op0=ALU.mult, op1=ALU.mult), reads=[xt, fg, rs], writes=[ynf])
        for sub in range(n // 128):
            yk = ytok[yi % 2]
            yi += 1
            for q in range(4):
                pt = k.psum()
                for j in range(4):
                    c = q * 4 + j
                    P.op("pe", lambda e, c=c, j=j, pt=pt, sub=sub: e.transpose(
                        out=pt[:, j * 128:(j + 1) * 128], in_=ynf[:, c, sub * 128:(sub + 1) * 128],
                        identity=ident[:]), reads=[ynf, ident], writes=[pt], inc=(j == 3))
                evac(yk[:, q * 512:(q + 1) * 512], pt[:, :], [pt], [yk])
            tok0 = t0 - TC + sub * 128
            P.dma("sp", out[tok0:tok0 + 128, :], yk[:], reads=[yk], writes=[out])
    P.finish([out])
    return k


_CACHE = {}


def na_table(rel_bias):
    keyl = np.arange(128)
    q = np.arange(64)
    kcol = keyl % 64
    win_start = np.clip(q - 8, 0, 48)
    col_ok = (kcol[:, None] >= win_start[None, :]) & (kcol[:, None] < win_start[None, :] + 16)
    ci = np.clip(kcol[:, None] - q[None, :] + 15, 0, 30)
    tab = np.full((128, 37, 8, 64), -1e4, np.float32)
    variants = [(o, 0, 4, 4 * o) for o in range(8)] + [(4, -1, 5, 32)]
    for (o, shift, nt, t0) in variants:
        for kt in range(nt):
            krow_off = 2 * kt + keyl // 64 + shift
            row_ok = (krow_off >= 0) & (krow_off < 8)
            ri = np.clip(krow_off - o + 7, 0, 14)
            ok = col_ok & row_ok[:, None]
            for h in range(8):
                g = rel_bias[h][ri[:, None], ci]
                tab[:, t0 + kt, h, :] = np.where(ok, g, np.float32(-1e4))
    return tab


def rope_table():
    t = np.arange(TL)
    pos = np.stack([(t // 64).astype(np.float32), (t % 64).astype(np.float32)])
    freqs = (10000.0 ** (-np.arange(0, 128, 2, dtype=np.float32) / 128)).astype(np.float32)
    fd = freqs[np.arange(128) % 64]
    ang = (pos[None, :, :] * fd[:, None, None]).astype(np.float32)
    sign = np.where(np.arange(128) < 64, -1.0, 1.0).astype(np.float32)
    return np.ascontiguousarray(np.stack([np.cos(ang), np.sin(ang) * sign[:, None, None]], axis=1).astype(np.float32))


def host_inputs(b, inputs):
    f = np.float32
    def pc(v, n):
        return np.ascontiguousarray(np.asarray(v, f).reshape(n, 128).T)
    m = {}
    m["x"] = np.ascontiguousarray(inputs["x"][b], f)
    m["ctx"] = np.ascontiguousarray(inputs["ctx"][b], f)
    m["cc"] = np.ascontiguousarray(np.stack([pc(inputs["c"][b], NCH), pc(inputs["c_ctx"], NCH)], axis=-1))
    m["ada_w"] = np.ascontiguousarray(inputs["ada_w"], f)
    m["ada_b"] = np.ascontiguousarray(np.stack([pc(inputs["ada_b"][l], 96) for l in range(2)], axis=1))
    m["n1g"] = np.ascontiguousarray(np.stack([pc(inputs["norm1_g"][l], NCH) for l in range(2)], axis=1))
    m["n2g"] = np.ascontiguousarray(np.stack([pc(inputs["norm2_g"][l], NCH) for l in range(2)], axis=1))
    m["fing"] = pc(inputs["final_g"], NCH)
    m["ident"] = np.eye(128, dtype=f)
    m["ab_w_in"] = np.ascontiguousarray(inputs["ab_w_in"][0], f)
    m["na_tab"] = na_table(np.asarray(inputs["na_rel_bias"][0], f))
    def st_layout(a):
        a = np.asarray(a, f).reshape(2, 32, 2, 64)
        return np.ascontiguousarray(a.transpose(2, 3, 0, 1).reshape(128, 64))
    ldt = np.broadcast_to(np.asarray(inputs["s5_log_dt"][0], f)[:, :, None], (2, 64, 64))
    m["s5_lam"] = np.ascontiguousarray(np.stack([st_layout(inputs["s5_lambda_re"][0]), st_layout(inputs["s5_lambda_im"][0]),
                                                 st_layout(ldt)], axis=1))
    Bm = np.zeros((128, 2, 2, 32, 128), f)
    Cm = np.zeros((128, 2, 2, 32, 128), f)
    for reim, (bn, cn) in enumerate((("s5_b_re", "s5_c_re"), ("s5_b_im", "s5_c_im"))):
        Bsrc = np.asarray(inputs[bn][0], f)
        Csrc = np.asarray(inputs[cn][0], f)
        for d_ in range(2):
            for g in range(64):
                gp, two = g // 2, g % 2
                r0 = (g % 8) * 16
                Bm[r0:r0 + 16, d_, reim, gp, two * 64:(two + 1) * 64] = Bsrc[d_, g].T
                Cm[two * 64:(two + 1) * 64, d_, reim, gp, r0:r0 + 16] = Csrc[d_, g].T
    m["s5_B"] = Bm
    m["s5_C"] = Cm
    m["s5_dg"] = np.ascontiguousarray(np.stack([pc(inputs["s5_d"][0], 8), pc(inputs["s5_glu_b"][0], 8)], axis=1))
    m["s5_glu_w"] = np.ascontiguousarray(inputs["s5_glu_w"][0], f)
    m["ab_w_out"] = np.ascontiguousarray(inputs["ab_w_out"][0], f)
    m["mlp_w1"] = np.ascontiguousarray(inputs["mlp_w1"], f)
    m["mlp_w2"] = np.ascontiguousarray(inputs["mlp_w2"], f)
    gw = np.asarray(inputs["gla_w_in"][0], f)
    perm = np.concatenate([np.arange(64, 128), np.arange(0, 64)])
    cols = []
    for base in (0, 1024):
        for c_ in range(8):
            blk = base + c_ * 128
            cols.append(np.arange(blk, blk + 128))
            cols.append(blk + perm)
    cols = np.concatenate(cols)
    apad = np.zeros((D, 512), f)
    apad[:, 0:16] = gw[:, 6144:6160]
    apad[:, 32:48] = gw[:, 6160:6176]
    m["gla_wcat"] = np.ascontiguousarray(np.concatenate([gw[:, cols], gw[:, 2048:6144], apad], axis=1))
    m["rope_tab"] = rope_table()
    wa2 = np.zeros((64, 1024), f)
    wa2[0:16] = inputs["gla_w_a2"][0][0]
    wa2[16] = inputs["gla_b_a"][0][0]
    wa2[32:48] = inputs["gla_w_a2"][0][1]
    wa2[48] = inputs["gla_b_a"][0][1]
    m["gla_wa2"] = wa2
    jj = np.arange(128)
    m["tri"] = np.ascontiguousarray(np.stack([(jj[:, None] <= jj[None, :]).astype(f), (jj[:, None] >= jj[None, :]).astype(f)], axis=1))
    m["gla_ng"] = np.ascontiguousarray(np.broadcast_to(np.asarray(inputs["gla_norm_g"][0], f)[None, :], (128, 512)))
    m["gla_w_out"] = np.ascontiguousarray(inputs["gla_w_out"][0], f)
    return m


def run(inputs, stage=99, cores=8):
    k = build(stage)
    maps = []
    for b in range(cores):
        m = host_inputs(b, inputs)
        maps.append({n: m[n] for n in k.inputs})
    res = run_bass_kernel_spmd(k.nc, maps, core_ids=list(range(cores)))
    return res.results


def kernel(**inputs):
    res = run(inputs)
    out = np.stack([r["out"] for r in res], axis=0)
    return out.astype(np.float32)
```
